# Optimizing a Trainium2 kernel written in Bass

```python
import jax, jax.numpy as jnp
from jax import lax
import numpy as np

D_MODEL = 2048
BATCH = 8
SEQ = 2048
DEPTH = 2

N_EVEN = (DEPTH + 1) // 2
N_ODD = DEPTH // 2
BRANCH = D_MODEL
MIX_WIDTH = 2 * BRANCH

A_HEAD_DIM = 128
A_HEADS = BRANCH // A_HEAD_DIM
A_CHUNK = 32
B_GROUPS = 8
B_GROUP_DIM = BRANCH // B_GROUPS
C_CONV = 3
D_HEAD_DIM = 64
D_HEADS = BRANCH // D_HEAD_DIM
D_GROUPS = 4
D_HPG = D_HEADS // D_GROUPS
D_STATE = 128
D_CONV = 3
D_CHUNK = 64
D_XBC = BRANCH + 2 * D_GROUPS * D_STATE
EPS = 1e-6

EV_SIZES = (BRANCH, BRANCH, BRANCH, BRANCH, BRANCH, BRANCH, BRANCH)
OD_SIZES = (BRANCH, BRANCH, BRANCH, BRANCH, BRANCH, D_XBC, 2 * D_HEADS)
EV_COLS = sum(EV_SIZES)
OD_COLS = sum(OD_SIZES)

kernel_name = "hybrid_hgrn2_fnet_shortconv_ssd_encoder"


def rms_norm(x, w):
    xf = x.astype(jnp.float32)
    y = xf * lax.rsqrt(jnp.mean(xf * xf, axis=-1, keepdims=True) + EPS) * w.astype(jnp.float32)
    return y.astype(x.dtype)


def split_cols(t, sizes):
    idx = np.cumsum(np.array(sizes))[:-1].tolist()
    return jnp.split(t, idx, axis=-1)


def flip(t):
    return t[:, ::-1]


def centred_dwconv(u, w):
    k_w = w.shape[0]
    pad = k_w // 2
    length = u.shape[1]
    up = jnp.pad(u, ((0, 0), (pad, pad), (0, 0)))
    y = up[:, 0:length] * w[0]
    for k in range(1, k_w):
        y = y + up[:, k:k + length] * w[k]
    return y


def gla_chunked(q, k, v, logf):
    bn, length, h, dk = q.shape
    dv = v.shape[-1]
    nc = length // A_CHUNK
    chunks = lambda t: jnp.moveaxis(t.reshape(bn, nc, A_CHUNK, h, t.shape[-1]), 1, 0)
    mask = jnp.tril(jnp.ones((A_CHUNK, A_CHUNK), dtype=bool))

    def step(state, inp):
        qc, kc, vc, gc = inp
        b = jnp.cumsum(gc, axis=1)
        b_last = b[:, -1:]
        q_in = qc * jnp.exp(b)
        k_in = kc * jnp.exp(-b)
        scores = jnp.where(mask, jnp.einsum('bihk,bjhk->bhij', q_in, k_in), 0.0)
        o = (jnp.einsum('bhij,bjhv->bihv', scores, vc)
             + jnp.einsum('bihk,bhkv->bihv', q_in, state))
        k_end = kc * jnp.exp(b_last - b)
        state = (jnp.exp(b_last[:, 0])[..., None] * state
                 + jnp.einsum('bjhk,bjhv->bhkv', k_end, vc))
        return state, o

    s0 = jnp.zeros((bn, h, dk, dv), q.dtype)
    _, o = lax.scan(step, s0, (chunks(q), chunks(k), chunks(v), chunks(logf)))
    return jnp.moveaxis(o, 0, 1).reshape(bn, length, h, dv)


def hgrn2_mixer(a_q, a_i, a_ff, a_fb, lb, norm_w):
    bn, length, _ = a_q.shape
    heads = lambda t: t.astype(jnp.float32).reshape(bn, length, A_HEADS, A_HEAD_DIM)
    q, v = heads(a_q), heads(a_i)
    lb = lb.astype(jnp.float32).reshape(A_HEADS, A_HEAD_DIM)

    def gates(zf):
        z = heads(zf)
        logf = jnp.log(lb + (1.0 - lb) * jax.nn.sigmoid(z))
        key = (1.0 - lb) * jax.nn.sigmoid(-z)
        return key, logf

    k_f, g_f = gates(a_ff)
    k_b, g_b = gates(a_fb)
    o = (gla_chunked(q, k_f, v, g_f)
         + flip(gla_chunked(flip(q), flip(k_b), flip(v), flip(g_b))))
    o = o * lax.rsqrt(jnp.mean(o * o, axis=-1, keepdims=True) + EPS)
    return (o.reshape(bn, length, BRANCH) * norm_w.astype(jnp.float32)).astype(a_q.dtype)


def fourier_mixer(u, fw, fb):
    bn, length, _ = u.shape
    ug = u.astype(jnp.float32).reshape(bn, length, B_GROUPS, B_GROUP_DIM)
    mixed = jnp.fft.fft2(ug, axes=(1, 3), norm="ortho").real
    y = jnp.einsum('blgc,gcd->blgd', mixed, fw.astype(jnp.float32)).reshape(bn, length, BRANCH)
    return (y + fb.astype(jnp.float32)).astype(u.dtype)


def ssd_chunked(x, dt, a, bm, cm):
    bn, length, g, hg, p = x.shape
    n = bm.shape[-1]
    nc = length // D_CHUNK
    chunks = lambda t: jnp.moveaxis(t.reshape(bn, nc, D_CHUNK, *t.shape[2:]), 1, 0)
    mask = jnp.tril(jnp.ones((D_CHUNK, D_CHUNK), dtype=bool))[None, :, :, None, None]

    def step(state, inp):
        xc, dtc, bc, cc = inp
        acum = jnp.cumsum(dtc * a, axis=1)
        decay = jnp.exp(jnp.where(mask, acum[:, :, None] - acum[:, None, :], -jnp.inf))
        xdt = xc * dtc[..., None]
        cb = jnp.einsum('bign,bjgn->bijg', cc, bc)
        y = jnp.einsum('bijg,bijgh,bjghp->bighp', cb, decay, xdt)
        y = y + jnp.einsum('bign,bghpn->bighp', cc, state) * jnp.exp(acum)[..., None]
        to_end = jnp.exp(acum[:, -1:] - acum)
        state = (state * jnp.exp(acum[:, -1])[..., None, None]
                 + jnp.einsum('bjgn,bjgh,bjghp->bghpn', bc, to_end, xdt))
        return state, y

    s0 = jnp.zeros((bn, g, hg, p, n), x.dtype)
    _, y = lax.scan(step, s0, (chunks(x), chunks(dt), chunks(bm), chunks(cm)))
    return jnp.moveaxis(y, 0, 1).reshape(bn, length, g, hg, p)


def ssd_mixer(d_xbc, d_dt, d_z, conv_w, conv_b, dt_bias, a_log, d_skip, norm_w):
    bn, length, _ = d_xbc.shape
    xbc = jax.nn.silu(centred_dwconv(d_xbc, conv_w) + conv_b).astype(jnp.float32)
    xs, bm, cm = split_cols(xbc, (BRANCH, D_GROUPS * D_STATE, D_GROUPS * D_STATE))
    xs = xs.reshape(bn, length, D_GROUPS, D_HPG, D_HEAD_DIM)
    bm = bm.reshape(bn, length, D_GROUPS, D_STATE)
    cm = cm.reshape(bn, length, D_GROUPS, D_STATE)
    dt = jax.nn.softplus(d_dt.astype(jnp.float32).reshape(bn, length, 2, D_HEADS)
                         + dt_bias.astype(jnp.float32))
    dt = dt.reshape(bn, length, 2, D_GROUPS, D_HPG)
    a = (-jnp.exp(a_log.astype(jnp.float32))).reshape(2, D_GROUPS, D_HPG)
    y = (ssd_chunked(xs, dt[:, :, 0], a[0], bm, cm)
         + flip(ssd_chunked(flip(xs), flip(dt[:, :, 1]), a[1], flip(bm), flip(cm))))
    y = y + xs * d_skip.astype(jnp.float32).reshape(D_GROUPS, D_HPG)[..., None]
    y = y.reshape(bn, length, BRANCH) * jax.nn.silu(d_z.astype(jnp.float32))
    return rms_norm(y, norm_w).astype(d_z.dtype)


def even_layer(h, w_in, w_out, lb, hgrn_nw, fw, fb):
    a_q, a_i, a_ff, a_fb, a_g, b_u, b_g = split_cols(h @ w_in, EV_SIZES)
    a_out = hgrn2_mixer(a_q, a_i, a_ff, a_fb, lb, hgrn_nw) * jax.nn.silu(a_g)
    b_out = fourier_mixer(b_u, fw, fb) * jax.nn.silu(b_g)
    return jnp.concatenate([a_out, b_out], axis=-1) @ w_out


def odd_layer(h, w_in, w_out, sconv_w, conv_w, conv_b, dt_bias, a_log, d_skip, ssd_nw):
    c_in, c_b, c_c, c_g, d_z, d_xbc, d_dt = split_cols(h @ w_in, OD_SIZES)
    c_out = c_b * centred_dwconv(c_c * c_in, sconv_w) * jax.nn.silu(c_g)
    d_out = ssd_mixer(d_xbc, d_dt, d_z, conv_w, conv_b, dt_bias, a_log, d_skip, ssd_nw)
    return jnp.concatenate([c_out, d_out], axis=-1) @ w_out


def setup_inputs(seed: int = 0) -> dict:
    key = jax.random.key(seed)
    ks = jax.random.split(key, 18)
    f32 = jnp.float32
    nrm = lambda k, shape, s: jax.random.normal(k, shape, f32) * s
    dt_init = jnp.exp(jax.random.uniform(ks[14], (N_ODD, 2, D_HEADS), f32)
                      * (np.log(0.1) - np.log(0.001)) + np.log(0.001))
    return {
        "x": nrm(ks[0], (BATCH, SEQ, D_MODEL), 1.0),
        "norm_w": 1.0 + nrm(ks[1], (DEPTH, D_MODEL), 0.02),
        "final_norm_w": 1.0 + nrm(ks[2], (D_MODEL,), 0.02),
        "ev_w_in": nrm(ks[3], (N_EVEN, D_MODEL, EV_COLS), D_MODEL ** -0.5),
        "ev_w_out": nrm(ks[4], (N_EVEN, MIX_WIDTH, D_MODEL), MIX_WIDTH ** -0.5),
        "hgrn_lb_logits": nrm(ks[5], (DEPTH + 1, BRANCH), 0.1),
        "hgrn_norm_w": 1.0 + nrm(ks[6], (N_EVEN, BRANCH), 0.02),
        "fnet_w": nrm(ks[7], (N_EVEN, B_GROUPS, B_GROUP_DIM, B_GROUP_DIM), B_GROUP_DIM ** -0.5),
        "fnet_b": nrm(ks[8], (N_EVEN, BRANCH), 0.01),
        "od_w_in": nrm(ks[9], (N_ODD, D_MODEL, OD_COLS), D_MODEL ** -0.5),
        "od_w_out": nrm(ks[10], (N_ODD, MIX_WIDTH, D_MODEL), MIX_WIDTH ** -0.5),
        "sconv_w": nrm(ks[11], (N_ODD, C_CONV, BRANCH), C_CONV ** -0.5),
        "ssd_conv_w": nrm(ks[12], (N_ODD, D_CONV, D_XBC), D_CONV ** -0.5),
        "ssd_conv_b": nrm(ks[13], (N_ODD, D_XBC), 0.01),
        "ssd_dt_bias": dt_init + jnp.log(-jnp.expm1(-dt_init)),
        "ssd_a_log": jnp.log(jax.random.uniform(ks[15], (N_ODD, 2, D_HEADS), f32, 1.0, 16.0)),
        "ssd_d": 1.0 + nrm(ks[16], (N_ODD, D_HEADS), 0.1),
        "ssd_norm_w": 1.0 + nrm(ks[17], (N_ODD, BRANCH), 0.02),
    }


def reference(x, norm_w, final_norm_w, ev_w_in, ev_w_out, hgrn_lb_logits, hgrn_norm_w,
              fnet_w, fnet_b, od_w_in, od_w_out, sconv_w, ssd_conv_w, ssd_conv_b,
              ssd_dt_bias, ssd_a_log, ssd_d, ssd_norm_w):
    lower_bounds = jnp.cumsum(jax.nn.softmax(hgrn_lb_logits.astype(jnp.float32), axis=0), axis=0)
    for layer in range(DEPTH):
        h = rms_norm(x, norm_w[layer])
        if layer % 2 == 0:
            e = layer // 2
            x = x + even_layer(h, ev_w_in[e], ev_w_out[e], lower_bounds[layer],
                               hgrn_norm_w[e], fnet_w[e], fnet_b[e])
        else:
            o = layer // 2
            x = x + odd_layer(h, od_w_in[o], od_w_out[o], sconv_w[o], ssd_conv_w[o],
                              ssd_conv_b[o], ssd_dt_bias[o], ssd_a_log[o], ssd_d[o],
                              ssd_norm_w[o])
    return rms_norm(x, final_norm_w)
```

```python
from contextlib import ExitStack
import numpy as np
import concourse.bass as bass
import concourse.mybir as mybir
from concourse.bass_utils import run_bass_kernel_spmd

F32 = mybir.dt.float32
BF16 = mybir.dt.bfloat16
AF = mybir.ActivationFunctionType
ALU = mybir.AluOpType
AX = mybir.AxisListType

L = 2048
D = 2048
NT = 16
KC = 16
EPS = 1e-6

ENGS = ("pe", "act", "dve", "pool", "sp")


class Tok:
    __slots__ = ("name", "w", "r", "war")

    def __init__(self, name=""):
        self.name = name
        self.w = {}
        self.r = {}
        self.war = {}


def _merge(dst, src):
    for k, v in src.items():
        if dst.get(k, -1) < v:
            dst[k] = v


class T:
    __slots__ = ("ap", "tok")

    def __init__(self, ap, tok=None, name=""):
        self.ap = ap
        self.tok = tok if tok is not None else Tok(name)

    def __getitem__(self, idx):
        return T(self.ap[idx], self.tok)

    def v(self, fn):
        return T(fn(self.ap), self.tok)

    def bc(self, shape):
        return T(self.ap.broadcast_to(list(shape)), self.tok)

    def re(self, s, **kw):
        return T(self.ap.rearrange(s, **kw), self.tok)


def _ap(x):
    return x.ap if isinstance(x, T) else x


def _toks(*xs):
    return [x.tok for x in xs if isinstance(x, T)]


class Op:
    __slots__ = ("eng", "fn", "deps", "seq", "signal", "sigval", "dma", "dsem", "dprev", "dval")


class Sched:
    def __init__(self, nc, n_dma_sems=48):
        self.nc = nc
        self.eng_ops = {e: [] for e in ENGS}
        self.n_dma_sems = n_dma_sems
        self.dma_val = [0] * n_dma_sems
        self.dma_rr = 0
        self.out_deps = {}

    def _deps(self, reads, writes, partial):
        deps = {}
        for t in reads:
            _merge(deps, t.w)
        for t in writes:
            _merge(deps, t.r)
            _merge(deps, t.war)
            if not partial:
                _merge(deps, t.w)
        return deps

    def _update(self, key, val, reads, writes, partial):
        me = {key: val}
        for t in reads:
            _merge(t.r, me)
        for t in writes:
            if t.r:
                t.war = t.r
                t.r = {}
                t.w = dict(me)
            elif partial:
                _merge(t.w, me)
            else:
                t.w = dict(me)

    def op(self, eng, fn, reads=(), writes=(), partial=False):
        o = Op()
        o.eng = eng
        o.fn = fn
        o.dma = None
        o.signal = False
        o.deps = self._deps(reads, writes, partial)
        lst = self.eng_ops[eng]
        lst.append(o)
        o.seq = len(lst)
        self._update(eng, o.seq, reads, writes, partial)
        return o

    def dma(self, queue, parts, reads=(), writes=(), partial=False, is_output=False):
        o = Op()
        o.eng = queue
        o.fn = None
        o.dma = parts
        o.signal = False
        o.deps = self._deps(reads, writes, partial)
        s = self.dma_rr
        self.dma_rr = (self.dma_rr + 1) % self.n_dma_sems
        o.dsem = s
        o.dprev = self.dma_val[s]
        self.dma_val[s] += 16 * len(parts)
        o.dval = self.dma_val[s]
        lst = self.eng_ops[queue]
        lst.append(o)
        o.seq = len(lst)
        self._update(("d", s), o.dval, reads, writes, partial)
        if is_output:
            _merge(self.out_deps, {("d", s): o.dval})
        return o

    def barrier(self):
        deps = {e: len(self.eng_ops[e]) for e in ENGS if len(self.eng_ops[e]) > 0 and e != "sp"}
        for s in range(self.n_dma_sems):
            if self.dma_val[s] > 0:
                deps[("d", s)] = self.dma_val[s]
        for e in ENGS:
            o = Op()
            o.eng = e
            o.fn = None
            o.dma = None
            o.signal = False
            o.deps = {k: v for k, v in deps.items() if k != e}
            lst = self.eng_ops[e]
            lst.append(o)
            o.seq = len(lst)

    def emit(self, es):
        nc = self.nc
        for e in ENGS:
            for o in self.eng_ops[e]:
                nd = {}
                for k, v in o.deps.items():
                    if isinstance(k, str):
                        if k == "pe" and e == "pe":
                            continue
                        while v > 0 and self.eng_ops[k][v - 1].fn is None and self.eng_ops[k][v - 1].dma is None:
                            v -= 1
                        if v == 0:
                            continue
                        p = self.eng_ops[k][v - 1]
                        if p.dma is not None:
                            _merge(nd, {("d", p.dsem): p.dval})
                        else:
                            p.signal = True
                            _merge(nd, {k: v})
                    else:
                        _merge(nd, {k: v})
                o.deps = nd
        for e in ENGS:
            c = 0
            for o in self.eng_ops[e]:
                if o.signal:
                    c += 1
                    o.sigval = c
        esem = {e: es.enter_context(nc.semaphore("s_" + e)) for e in ENGS}
        dsems = [es.enter_context(nc.semaphore("d%d" % i)) for i in range(self.n_dma_sems)]
        block = es.enter_context(nc.Block())
        sched = self

        def run(en, e):
            waited = {}
            for o in sched.eng_ops[en]:
                for k, v in o.deps.items():
                    if isinstance(k, str):
                        kk = k
                        tgt = sched.eng_ops[k][v - 1].sigval
                        sem = esem[k]
                    else:
                        kk = k
                        tgt = v
                        sem = dsems[k[1]]
                    if waited.get(kk, 0) >= tgt:
                        continue
                    waited[kk] = tgt
                    e.wait_ge(sem, tgt)
                if o.dma is not None:
                    kk = ("d", o.dsem)
                    if o.dprev > 0 and waited.get(kk, 0) < o.dprev:
                        e.wait_ge(dsems[o.dsem], o.dprev)
                        waited[kk] = o.dprev
                    for (oa, ia) in o.dma:
                        e.dma_start(out=oa, in_=ia).then_inc(dsems[o.dsem], 16)
                elif o.fn is not None:
                    inst = o.fn(e)
                    if o.signal:
                        inst.then_inc(esem[en], 1)
            if en == "sp":
                for k, v in sched.out_deps.items():
                    if waited.get(k, 0) < v:
                        e.wait_ge(dsems[k[1]], v)

        @block.sync
        def _(e):
            run("sp", e)

        @block.scalar
        def _(e):
            run("act", e)

        @block.vector
        def _(e):
            run("dve", e)

        @block.gpsimd
        def _(e):
            run("pool", e)

        @block.tensor
        def _(e):
            run("pe", e)


class B:
    def __init__(self, nc):
        self.nc = nc
        self.s = Sched(nc)

    def mm(self, out, lhsT, rhs, start=True, stop=True, tile_position=None):
        o, a, b = _ap(out), _ap(lhsT), _ap(rhs)
        kw = {}
        if tile_position is not None:
            kw["tile_position"] = tile_position
        self.s.op("pe", lambda e: e.matmul(o, lhsT=a, rhs=b, start=start, stop=stop, **kw),
                  reads=_toks(lhsT, rhs), writes=_toks(out), partial=True)

    def tr(self, out, in_, ident):
        o, a, i = _ap(out), _ap(in_), _ap(ident)
        self.s.op("pe", lambda e: e.transpose(o, a, i), reads=_toks(in_, ident), writes=_toks(out), partial=True)

    def act(self, out, in_, func, bias=None, scale=None, accum=None, extra_w=()):
        o, a = _ap(out), _ap(in_)
        kw = {}
        if bias is not None:
            kw["bias"] = _ap(bias)
        if scale is not None:
            kw["scale"] = _ap(scale)
        if accum is not None:
            kw["accum_out"] = _ap(accum)
        self.s.op("act", lambda e: e.activation(o, a, func, **kw),
                  reads=_toks(in_, bias, scale), writes=_toks(out, accum) + list(extra_w))

    def tt(self, eng, out, in0, in1, op):
        o, a, b = _ap(out), _ap(in0), _ap(in1)
        self.s.op(eng, lambda e: e.tensor_tensor(o, a, b, op), reads=_toks(in0, in1), writes=_toks(out))

    def ts(self, eng, out, in0, s1, s2=None, op0=ALU.mult, op1=None):
        o, a = _ap(out), _ap(in0)
        x1, x2 = _ap(s1), _ap(s2)
        if op1 is None:
            self.s.op(eng, lambda e: e.tensor_scalar(o, a, x1, None, op0),
                      reads=_toks(in0, s1), writes=_toks(out))
        else:
            self.s.op(eng, lambda e: e.tensor_scalar(o, a, x1, x2, op0, op1),
                      reads=_toks(in0, s1, s2), writes=_toks(out))

    def stt(self, out, in0, scalar, in1, op0, op1):
        o, a, c, b = _ap(out), _ap(in0), _ap(scalar), _ap(in1)
        self.s.op("dve", lambda e: e.scalar_tensor_tensor(o, a, c, b, op0, op1),
                  reads=_toks(in0, scalar, in1), writes=_toks(out))

    def scan(self, out, d0, d1, initial, op0=ALU.mult, op1=ALU.add):
        o, a, b, i = _ap(out), _ap(d0), _ap(d1), _ap(initial)
        self.s.op("dve", lambda e: e.tensor_tensor_scan(o, a, b, i, op0, op1),
                  reads=_toks(d0, d1, initial), writes=_toks(out))

    def copy(self, eng, out, in_):
        o, a = _ap(out), _ap(in_)
        if eng == "act":
            self.s.op("act", lambda e: e.copy(o, a), reads=_toks(in_), writes=_toks(out))
        else:
            self.s.op(eng, lambda e: e.tensor_copy(o, a), reads=_toks(in_), writes=_toks(out))

    def recip(self, out, in_):
        o, a = _ap(out), _ap(in_)
        self.s.op("dve", lambda e: e.reciprocal(o, a), reads=_toks(in_), writes=_toks(out))

    def memset(self, eng, out, val):
        o = _ap(out)
        self.s.op(eng, lambda e: e.memset(o, val), writes=_toks(out))

    def dma(self, queue, out, in_, is_output=False):
        self.s.dma(queue, [(_ap(out), _ap(in_))], reads=_toks(in_), writes=_toks(out), is_output=is_output)

    def dmas(self, queue, parts, is_output=False):
        self.s.dma(queue, [(_ap(o), _ap(i)) for o, i in parts],
                   reads=_toks(*[i for _, i in parts]), writes=_toks(*[o for o, _ in parts]),
                   is_output=is_output)


ARENA_BYTES = 206 * 1024
R_CONST = (0, 13312)
R_WEIGHT = (13312, 50176)
R_HT = (50176, 115712)
R_PHASE = (115712, ARENA_BYTES)
R_BIG = (13312, ARENA_BYTES)

C_NW0, C_NW1, C_LB, C_HNW, C_FB, C_SW, C_CW, C_CB, C_DD, C_DNW = 0, 16, 32, 80, 96, 112, 160, 232, 256, 272
NCOLS = 288
K_ID, K_MF, K_MB, K_TF, K_TB, K_GF, K_GB, K_NF, K_NB, K_RST, K_CC = 0, 128, 256, 384, 512, 640, 768, 896, 1024, 1152, 1664
NCST = 1664 + 1024


def _esize(dt):
    return 4 if dt == F32 else 2


class Arena:
    def __init__(self, arena_ap, region):
        self.a = arena_ap
        self.lo, self.hi = region
        self.p = self.lo

    def __call__(self, shape, dt=F32, name=""):
        n = 1
        for s in shape[1:]:
            n *= s
        nb = n * _esize(dt)
        nb_al = (nb + 31) // 32 * 32
        assert self.p + nb_al <= self.hi, "arena overflow %s: need %d have %d" % (name, nb_al, self.hi - self.p)
        v = self.a[0:shape[0], self.p:self.p + nb].bitcast(dt)
        self.p += nb_al
        if len(shape) == 3:
            v = v.rearrange("p (a b) -> p a b", a=shape[1])
        elif len(shape) == 4:
            v = v.rearrange("p (a b c) -> p a b c", a=shape[1], b=shape[2])
        return T(v, name=name)


class Builder:
    def __init__(self, nc, debug=False, stop_after=None):
        self.nc = nc
        self.b = B(nc)
        self.debug = debug
        self.stop_after = stop_after
        self.wi = 0
        self.pi = 0

    def din(self, name, shape, dt=F32):
        return T(self.nc.dram_tensor(name, list(shape), dt, kind="ExternalInput").ap(), name=name)

    def dscr(self, name, shape, dt=F32, out=False):
        kind = "ExternalOutput" if (out or self.debug) else "Internal"
        return T(self.nc.dram_tensor(name, list(shape), dt, kind=kind).ap(), name=name)

    def build(self):
        nc, b = self.nc, self.b
        self.x_d = self.din("x", [L, D])
        self.w_in0 = self.din("w_in0", [112, 128, 16, 128])
        self.w_out0 = self.din("w_out0", [4, 128, 32, 512])
        self.w_in1 = self.din("w_in1", [104, 128, 16, 128])
        self.w_dt = self.din("w_dt", [128, 16, 64])
        self.w_out1 = self.din("w_out1", [4, 128, 32, 512])
        self.cols_d = self.din("cols", [128, NCOLS])
        self.cst_d = self.din("cst", [128, NCST])
        self.fnw_d = self.din("fnw", [1, D])
        self.dtb_d = self.din("dtb", [1, 64])
        self.alog_d = self.din("alog", [1, 64])
        self.fw_d = self.din("fw", [8, 128, 2, 256])
        self.tabf_d = self.din("tabf", [16, 128, 8, 512])
        self.tabb_d = self.dscr("tabb", [16, 128, 8, 512], BF16)
        self.yT_d = self.dscr("yT", [4096, L], BF16)
        self.x1_d = self.dscr("x1", [L, D])
        self.x2_d = self.dscr("x2", [L, D])
        self.out_d = T(nc.dram_tensor("out", [L, D], F32, kind="ExternalOutput").ap(), name="out")
        self.arena = nc.alloc_sbuf_tensor("arena", [128, ARENA_BYTES], mybir.dt.uint8).ap()
        self.PS = [T(nc.alloc_psum_tensor("ps%d" % i, [128, 512], F32).ap(), name="ps%d" % i) for i in range(6)]
        self.PB = [T(nc.alloc_psum_tensor("pb%d" % i, [128, 1024], BF16).ap()[:, 0:512], name="pb%d" % i)
                   for i in range(2)]
        C = Arena(self.arena, R_CONST)
        self.cols = C([128, NCOLS], F32, "cols")
        self.ident = C([128, 128], BF16, "ident")
        self.identf = C([128, 128], F32, "identf")
        self.ones_bf = C([128, 128], BF16, "ones_bf")
        self.ones_f = C([128, 128], F32, "ones_f")
        self.rst = C([128, 512], F32, "rst")
        self.maskF = C([128, 128], F32, "maskF")
        self.maskB = C([128, 128], F32, "maskB")
        self.tri = [C([128, 128], F32, "triF"), C([128, 128], F32, "triB")]
        self.sgt = [C([128, 128], F32, "sgtF"), C([128, 128], F32, "sgtB")]
        self.neg = [C([128, 512], BF16, "negF"), C([128, 512], BF16, "negB")]
        self.cc = [C([128, 2, 256], BF16, "ccC"), C([128, 2, 256], BF16, "ccS")]
        self.lb3 = C([128, 3, 16], F32, "lb3")
        self.dtb_b = C([128, 64], F32, "dtb_b")
        self.a_b = C([128, 64], F32, "a_b")
        self.ssq = C([128, 16], F32, "ssq")
        self.rstd = C([128, 16], F32, "rstd")
        W = Arena(self.arena, R_WEIGHT)
        self.wstage = [W([128, 16, 128], F32, "wst%d" % i) for i in range(2)]
        self.wring = [W([128, 16, 128], BF16, "wr%d" % i) for i in range(5)]
        H = Arena(self.arena, R_HT)
        self.hT = H([128, 16, 2048], BF16, "hT")

        self.setup_phase()
        b.s.barrier()
        if self.stop_after == "setup":
            return self.finish()
        self.norm_phase(self.x_d, C_NW0)
        b.s.barrier()
        if self.stop_after == "norm0":
            return self.finish()
        self.hgrn_phase()
        b.s.barrier()
        if self.stop_after == "hgrn":
            return self.finish()
        self.fnet_phase()
        b.s.barrier()
        if self.stop_after == "fnet":
            return self.finish()
        self.outproj_phase(self.w_out0, self.x_d, self.x1_d, split=False)
        b.s.barrier()
        if self.stop_after == "out0":
            return self.finish()
        self.norm_phase(self.x1_d, C_NW1)
        b.s.barrier()
        self.sconv_phase()
        b.s.barrier()
        if self.stop_after == "sconv":
            return self.finish()
        self.ssd_phase()
        b.s.barrier()
        if self.stop_after == "ssd":
            return self.finish()
        self.outproj_phase(self.w_out1, self.x1_d, self.x2_d, split=True)
        b.s.barrier()
        self.final_norm(self.x2_d, self.out_d)
        return self.finish()

    def finish(self):
        if self.debug and self.stop_after in ("norm0", "hgrn", "fnet", "sconv", "ssd"):
            self.b.s.barrier()
            hd = T(self.nc.dram_tensor("dbg_hT", [128, 16, 2048], BF16, kind="ExternalOutput").ap())
            self.b.dma("sp", hd, self.hT, is_output=True)
        if self.stop_after is not None:
            A = Arena(self.arena, R_PHASE)
            t = A([128, 512], F32, "fin")
            self.b.s.barrier()
            self.b.memset("dve", t, 0.0)
            self.b.dma("sp", self.out_d[0:128, 0:512], t, is_output=True)
        self._es = ExitStack()
        self.b.s.emit(self._es)
        self._es.close()
        return self.nc

    def load_w(self, src):
        b = self.b
        st = self.wstage[self.wi % 2]
        wb = self.wring[self.wi % len(self.wring)]
        self.wi += 1
        b.dma("sp", st, src)
        b.copy("pool", wb, st)
        return wb

    def proj(self, wb, consumer):
        b = self.b
        for q in range(4):
            p = self.PS[self.pi % 2]
            self.pi += 1
            for kc in range(KC):
                b.mm(p, wb[:, kc, :], self.hT[:, kc, q * 512:(q + 1) * 512], start=(kc == 0), stop=(kc == KC - 1))
            consumer(q, p)

    def setup_phase(self):
        b = self.b
        A = Arena(self.arena, R_PHASE)
        cst = A([128, NCST], F32, "cst")
        b.dma("sp", self.cols, self.cols_d)
        b.dma("sp", cst, self.cst_d)
        b.dma("sp", self.dtb_b, self.dtb_d.bc([128, 64]))
        al = A([128, 64], F32, "al")
        b.dma("sp", al, self.alog_d.bc([128, 64]))
        b.copy("dve", self.ident, cst[:, K_ID:K_ID + 128])
        b.copy("dve", self.identf, cst[:, K_ID:K_ID + 128])
        b.memset("dve", self.ones_bf, 1.0)
        b.memset("dve", self.ones_f, 1.0)
        b.copy("dve", self.rst, cst[:, K_RST:K_RST + 512])
        b.copy("dve", self.maskF, cst[:, K_MF:K_MF + 128])
        b.copy("dve", self.maskB, cst[:, K_MB:K_MB + 128])
        b.copy("dve", self.tri[0], cst[:, K_TF:K_TF + 128])
        b.copy("dve", self.tri[1], cst[:, K_TB:K_TB + 128])
        b.copy("dve", self.sgt[0], cst[:, K_GF:K_GF + 128])
        b.copy("dve", self.sgt[1], cst[:, K_GB:K_GB + 128])
        for d, k in ((0, K_NF), (1, K_NB)):
            b.copy("dve", self.neg[d].re("p (h i) -> p h i", h=4),
                   cst[:, k:k + 128].v(lambda a: a.unsqueeze(1)).bc([128, 4, 128]))
        b.copy("dve", self.cc[0], cst[:, K_CC:K_CC + 512].re("p (k d) -> p k d", k=2))
        b.copy("dve", self.cc[1], cst[:, K_CC + 512:K_CC + 1024].re("p (k d) -> p k d", k=2))
        b.act(al, al, AF.Exp)
        b.ts("dve", self.a_b, al, -1.0, None, ALU.mult)
        l0 = self.cols[:, C_LB:C_LB + 16]
        l1 = self.cols[:, C_LB + 16:C_LB + 32]
        l2 = self.cols[:, C_LB + 32:C_LB + 48]
        t1 = A([128, 16], F32, "t1")
        t2 = A([128, 16], F32, "t2")
        b.tt("dve", t1, l1, l0, ALU.subtract)
        b.tt("dve", t2, l2, l0, ALU.subtract)
        b.act(t1, t1, AF.Exp)
        b.act(t2, t2, AF.Exp)
        b.tt("dve", t1, t1, t2, ALU.add)
        b.ts("dve", t1, t1, 1.0, None, ALU.add)
        b.recip(self.lb3[:, 0, :], t1)
        b.ts("dve", self.lb3[:, 1, :], self.lb3[:, 0, :], -1.0, 1.0, ALU.mult, ALU.add)
        b.ts("dve", self.lb3[:, 2, :], self.lb3[:, 0, :], -1.0, None, ALU.add)
        b.memset("dve", self.ssq, 0.0)
        st = [A([128, 8, 512], F32, "tst%d" % i) for i in range(2)]
        sb_ = [A([128, 8, 512], BF16, "tsb%d" % i) for i in range(2)]
        for i in range(16):
            b.dma("sp", st[i % 2], self.tabf_d[i])
            b.copy("pool" if i % 2 == 0 else "act", sb_[i % 2], st[i % 2])
            self.b.s.dma("sp", [(self.tabb_d.ap[i], sb_[i % 2].ap)], reads=[sb_[i % 2].tok],
                         writes=[self.tabb_d.tok], partial=True)

    def norm_phase(self, xsrc, ccol):
        b = self.b
        A = Arena(self.arena, R_PHASE)
        xt = [A([128, 2048], F32, "nxt%d" % i) for i in range(2)]
        junk = A([128, 2048], BF16, "njunk")
        xn = A([128, 4, 2048], BF16, "nxn")
        sm = [[A([128, 1], F32) for _ in range(4)] for _ in range(2)]
        for g in range(4):
            for j in range(4):
                tt = g * 4 + j
                x_t = xt[tt % 2]
                s1, s2, s3, s4 = sm[tt % 2]
                b.dma("sp", x_t, xsrc[tt * 128:(tt + 1) * 128, :])
                b.act(junk, x_t, AF.Square, accum=s1)
                b.ts("dve", s2, s1, 1.0 / D, EPS, ALU.mult, ALU.add)
                b.act(s3, s2, AF.Sqrt)
                b.recip(s4, s3)
                b.act(xn[:, j, :], x_t, AF.Copy, scale=s4)
            for c in range(16):
                pb = self.PB[c % 2]
                for j in range(4):
                    b.tr(pb[:, j * 128:(j + 1) * 128], xn[:, j, c * 128:(c + 1) * 128], self.ident)
                dst = self.hT[:, c, g * 512:(g + 1) * 512]
                sc = self.cols[:, ccol + c:ccol + c + 1]
                b.act(dst, pb, AF.Copy, scale=sc)

    def hgrn_phase(self):
        b = self.b
        A = Arena(self.arena, R_PHASE)
        qf = A([128, 2048], F32, "qf")
        vT = A([128, 2048], BF16, "vT")
        V = A([128, 16, 128], BF16, "V")
        sg = A([128, 2048], BF16, "sg")
        qin = A([128, 2048], BF16, "qin")
        kin = A([128, 2048], BF16, "kin")
        kend = A([128, 16, 128], BF16, "kend")
        kendT = [A([128, 512], BF16, "kendT%d" % i) for i in range(2)]
        Sbf = A([128, 64, 128], BF16, "Sbf")
        Of = A([128, 2048], F32, "Of")
        sn, lf, bb, b2, E, En, kinf = [A([128, 512], F32, n) for n in ("sn", "lf", "bb", "b2", "E", "En", "kinf")]
        Elast = A([128, 64], F32, "Elast")
        S32 = [A([128, 128], F32, "S32_%d" % i) for i in range(2)]
        Am = [A([128, 128], BF16, "Am%d" % i) for i in range(2)]
        osq = A([128, 512], BF16, "osq")
        ot = A([128, 512], F32, "ot")
        rs1 = A([128, 512], F32, "rs1")
        rs2 = A([128, 512], F32, "rs2")
        yT = A([128, 2048], BF16, "yTt")
        PS = self.PS
        r3 = lambda t: t.re("p (c k) -> p c k", k=32)
        for hd in range(16):
            oml = self.lb3[:, 1, hd:hd + 1]
            noml = self.lb3[:, 2, hd:hd + 1]
            wq = self.load_w(self.w_in0[0 * 16 + hd])
            self.proj(wq, lambda q, p: b.copy("act", qf[:, q * 512:(q + 1) * 512], p))
            wv = self.load_w(self.w_in0[1 * 16 + hd])
            self.proj(wv, lambda q, p: b.copy("dve", vT[:, q * 512:(q + 1) * 512], p))
            for q in range(4):
                pb = self.PB[q % 2]
                for j in range(4):
                    b.tr(pb[:, j * 128:(j + 1) * 128], vT[:, (4 * q + j) * 128:(4 * q + j + 1) * 128], self.ident)
                b.copy("act" if q % 2 else "dve", V[:, 4 * q:4 * q + 4, :], pb.re("p (j k) -> p j k", k=128))
            wg = self.load_w(self.w_in0[4 * 16 + hd])
            self.proj(wg, lambda q, p: b.act(sg[:, q * 512:(q + 1) * 512], p, AF.Silu))
            for dr in (0, 1):
                wz = self.load_w(self.w_in0[(2 + dr) * 16 + hd])

                def cons(q, p, dr=dr):
                    qs = slice(q * 512, (q + 1) * 512)
                    b.act(sn, p, AF.Sigmoid, scale=-1.0)
                    b.act(lf, sn, AF.Ln, bias=1.0, scale=noml)
                    b.scan(bb, self.rst, lf, 0.0)
                    if dr == 0:
                        bsel = bb
                    else:
                        b.tt("dve", r3(b2), r3(bb)[:, :, 31:32].bc([128, 16, 32]), r3(bb), ALU.subtract)
                        b.tt("dve", b2, b2, lf, ALU.add)
                        bsel = b2
                    b.act(E, bsel, AF.Exp)
                    b.act(En, bsel, AF.Exp, scale=-1.0)
                    b.tt("dve", qin[:, qs], qf[:, qs], E, ALU.mult)
                    b.stt(kinf, sn, oml, En, ALU.mult, ALU.mult)
                    b.copy("act", kin[:, qs], kinf)
                    edge = r3(E)[:, :, 31:32] if dr == 0 else r3(E)[:, :, 0:1]
                    kT = kendT[q % 2]
                    b.tt("dve", r3(kT), r3(kinf), edge.bc([128, 16, 32]), ALU.mult)
                    b.copy("dve", Elast[:, 16 * q:16 * q + 16].v(lambda a: a.unsqueeze(2)), edge)
                    pb = self.PB[q % 2]
                    for j in range(4):
                        b.tr(pb[:, j * 128:(j + 1) * 128], kT[:, j * 128:(j + 1) * 128], self.ident)
                    b.copy("act", kend[:, 4 * q:4 * q + 4, :], pb.re("p (j k) -> p j k", k=128))

                self.proj(wz, cons)
                order = list(range(64)) if dr == 0 else list(range(63, -1, -1))
                b.memset("dve", Sbf[:, order[0], :], 0.0)
                for idx, c in enumerate(order[:-1]):
                    tile, k = c // 4, c % 4
                    pS = PS[2 + k][:, 0:128]
                    b.mm(pS, kend[32 * k:32 * k + 32, tile, :], V[32 * k:32 * k + 32, tile, :],
                         tile_position=(32 * k, 0))
                    new, old = S32[idx % 2], S32[(idx + 1) % 2]
                    if idx == 0:
                        b.copy("dve", new, pS)
                    else:
                        b.stt(new, old, Elast[:, c:c + 1], pS, ALU.mult, ALU.add)
                    b.copy("act", Sbf[:, order[idx + 1], :], new)
                mask = self.maskF if dr == 0 else self.maskB
                zero_c = order[0]
                for q in range(4):
                    qs = slice(q * 512, (q + 1) * 512)
                    pO = PS[4 + q % 2]
                    for bl in range(4):
                        bi = 4 * q + bl
                        bs = slice(bi * 128, (bi + 1) * 128)
                        pA = PS[2 + bi % 2][:, 128:256]
                        am = Am[bi % 2]
                        b.mm(pA, kin[:, bs], qin[:, bs])
                        b.tt("dve", am, pA, mask, ALU.mult)
                        o_sl = pO[:, bl * 128:(bl + 1) * 128]
                        cl = [4 * bi + k for k in range(4) if 4 * bi + k != zero_c]
                        b.mm(o_sl, V[:, bi, :], am, start=True, stop=False)
                        for c in cl:
                            k = c % 4
                            b.mm(o_sl[:, 32 * k:32 * k + 32], Sbf[:, c, :], qin[:, c * 32:(c + 1) * 32],
                                 start=False, stop=(c == cl[-1]))
                    if dr == 0:
                        b.copy("act", Of[:, qs], pO)
                    else:
                        b.tt("dve", ot, pO, Of[:, qs], ALU.add)
                        b.act(osq, ot, AF.Square)
                        pss = PS[2 + q % 2]
                        b.mm(pss, self.ones_bf, osq)
                        b.ts("dve", rs1, pss, 1.0 / 128.0, EPS, ALU.mult, ALU.add)
                        b.act(rs2, rs1, AF.Sqrt)
                        b.recip(rs1, rs2)
                        b.tt("dve", ot, ot, rs1, ALU.mult)
                        b.stt(yT[:, qs], ot, self.cols[:, C_HNW + hd:C_HNW + hd + 1], sg[:, qs], ALU.mult, ALU.mult)
            self.b.s.dma("act", [(self.yT_d.ap[hd * 128:(hd + 1) * 128, :], yT.ap)], reads=[yT.tok],
                         writes=[self.yT_d.tok], partial=True)

    def fnet_phase(self):
        b = self.b
        PS = self.PS
        A = Arena(self.arena, R_PHASE)
        uT = A([128, 2, 2048], BF16, "uT")
        sgB = A([128, 2, 2048], BF16, "sgB")
        UW = A([128, 16, 2, 256], BF16, "UW")
        tabs = [[A([128, 8, 512], BF16, "tab%d_%d" % (s, w)) for w in range(2)] for s in range(2)]
        fwst = A([128, 2, 256], F32, "fwst")
        fwb = A([128, 2, 256], BF16, "fwb")
        W12 = A([128, 2, 2, 256], BF16, "W12")
        yo = [A([128, 512], BF16, "fyo%d" % i) for i in range(2)]
        slot_i = 0
        for g in range(8):
            for co in range(2):
                wb = self.load_w(self.w_in0[80 + 2 * g + co])
                self.proj(wb, lambda q, p, co=co: b.copy("act", uT[:, co, q * 512:(q + 1) * 512], p))
            for co in range(2):
                wb = self.load_w(self.w_in0[96 + 2 * g + co])
                self.proj(wb, lambda q, p, co=co: b.act(sgB[:, co, q * 512:(q + 1) * 512], p, AF.Silu))
            b.dma("sp", fwst, self.fw_d[g])
            b.copy("dve", fwb, fwst)
            for co in range(2):
                p = PS[2]
                for w in range(2):
                    for k in range(2):
                        b.mm(p[:, w * 256:(w + 1) * 256], self.cc[w][:, k, co * 128:(co + 1) * 128], fwb[:, k, :],
                             start=(k == 0), stop=(k == 1))
                b.copy("act", W12[:, co, 0, :], p[:, 0:256])
                b.ts("dve", W12[:, co, 1, :], p[:, 256:512], -1.0, None, ALU.mult)
            for st in range(16):
                p = PS[2 + st % 2]
                for co in range(2):
                    b.mm(p, uT[:, co, st * 128:(st + 1) * 128], W12[:, co, :, :].re("p w d -> p (w d)"),
                         start=(co == 0), stop=(co == 1))
                b.copy("act" if st % 2 else "dve", UW[:, st, :, :], p.re("p (w d) -> p w d", w=2))
            for tt in range(4):
                banks = (PS[2], PS[3]) if tt % 2 == 0 else (PS[4], PS[5])
                for half in range(2):
                    slot = tabs[slot_i % 2]
                    slot_i += 1
                    for w in range(2):
                        b.dma("sp", slot[w], self.tabb_d[w * 8 + tt * 2 + half])
                    for dblk in range(2):
                        p = banks[dblk]
                        for j in range(8):
                            st = half * 8 + j
                            for w in range(2):
                                b.mm(p, UW[:, st, w, dblk * 128:(dblk + 1) * 128], slot[w][:, j, :],
                                     start=(half == 0 and j == 0 and w == 0), stop=(half == 1 and j == 7 and w == 1))
                for dblk in range(2):
                    y = yo[dblk]
                    ch = 2 * g + dblk
                    b.stt(y, banks[dblk], self.cols[:, C_FB + ch:C_FB + ch + 1], sgB[:, dblk, tt * 512:(tt + 1) * 512],
                          ALU.add, ALU.mult)
                    r0 = 2048 + ch * 128
                    self.b.s.dma("act", [(self.yT_d.ap[r0:r0 + 128, tt * 512:(tt + 1) * 512], y.ap)], reads=[y.tok],
                                 writes=[self.yT_d.tok], partial=True)

    def outproj_phase(self, wout, xsrc, xdst, split):
        b = self.b
        PS = self.PS
        A = Arena(self.arena, R_BIG)
        wsl = [A([128, 32, 512], BF16, "wsl%d" % i) for i in range(2)]
        ysl = [A([128, 32, 512], BF16, "ysl%d" % i) for i in range(2)]
        wst = [A([128, 4, 512], F32, "owst%d" % i) for i in range(2)]
        xr = [A([128, 512], F32, "xr%d" % i) for i in range(2)]
        xo = [A([128, 512], F32, "xo%d" % i) for i in range(2)]
        if split:
            b.ts("dve", self.rstd, self.ssq, 1.0 / 2048.0, EPS, ALU.mult, ALU.add)
            b.act(self.rstd, self.rstd, AF.Sqrt)
            b.recip(self.rstd, self.rstd)
        yv = self.yT_d.re("(kc p) t -> p kc t", p=128)
        cnt = 0
        for dt_ in range(4):
            w = wsl[dt_ % 2]
            for pc in range(8):
                st = wst[pc % 2]
                b.dma("sp", st, wout[dt_][:, pc * 4:(pc + 1) * 4, :])
                b.copy("pool", w[:, pc * 4:(pc + 1) * 4, :], st)
            for tq in range(4):
                ys = ysl[(dt_ * 4 + tq) % 2]
                self.b.s.dma("sp", [(ys.ap[:, i * 8:(i + 1) * 8, :], yv.ap[:, i * 8:(i + 1) * 8, tq * 512:(tq + 1) * 512])
                                    for i in range(4)], reads=[self.yT_d.tok], writes=[ys.tok])
                for j in range(4):
                    tt = tq * 4 + j
                    ts_ = slice(j * 128, (j + 1) * 128)
                    pa = PS[cnt % 3]
                    pbk = PS[3 + cnt % 3]
                    x_r = xr[cnt % 2]
                    x_o = xo[cnt % 2]
                    cnt += 1
                    b.dma("sp", x_r, xsrc[tt * 128:(tt + 1) * 128, dt_ * 512:(dt_ + 1) * 512])
                    if not split:
                        for kc in range(32):
                            b.mm(pa, ys[:, kc, ts_], w[:, kc, :], start=(kc == 0), stop=(kc == 31))
                        b.tt("dve", x_o, pa, x_r, ALU.add)
                    else:
                        for kc in range(16):
                            b.mm(pa, ys[:, kc, ts_], w[:, kc, :], start=(kc == 0), stop=(kc == 15))
                        for kc in range(16, 32):
                            b.mm(pbk, ys[:, kc, ts_], w[:, kc, :], start=(kc == 16), stop=(kc == 31))
                        b.tt("dve", x_o, pa, x_r, ALU.add)
                        b.stt(x_o, pbk, self.rstd[:, tt:tt + 1], x_o, ALU.mult, ALU.add)
                    self.b.s.dma("act", [(xdst.ap[tt * 128:(tt + 1) * 128, dt_ * 512:(dt_ + 1) * 512], x_o.ap)],
                                 reads=[x_o.tok], writes=[xdst.tok], partial=True)

    def final_norm(self, xsrc, out):
        b = self.b
        A = Arena(self.arena, R_BIG)
        fnwb = A([128, 2048], F32, "fnwb")
        xt = [A([128, 2048], F32, "fxt%d" % i) for i in range(2)]
        xo = [A([128, 2048], F32, "fxo%d" % i) for i in range(2)]
        junk = A([128, 2048], BF16, "fjunk")
        sm = [[A([128, 1], F32) for _ in range(4)] for _ in range(2)]
        b.dma("sp", fnwb, self.fnw_d.bc([128, 2048]))
        for tt in range(16):
            x_t, x_o = xt[tt % 2], xo[tt % 2]
            s1, s2, s3, s4 = sm[tt % 2]
            b.dma("sp", x_t, xsrc[tt * 128:(tt + 1) * 128, :])
            b.act(junk, x_t, AF.Square, accum=s1)
            b.ts("dve", s2, s1, 1.0 / D, EPS, ALU.mult, ALU.add)
            b.act(s3, s2, AF.Sqrt)
            b.recip(s4, s3)
            b.stt(x_o, x_t, s4, fnwb, ALU.mult, ALU.mult)
            self.b.s.dma("act", [(out.ap[tt * 128:(tt + 1) * 128, :], x_o.ap)], reads=[x_o.tok],
                         writes=[out.tok], partial=True, is_output=True)

    def sconv_phase(self):
        b = self.b
        A = Arena(self.arena, R_PHASE)
        cin = A([128, 2048], F32, "cin")
        upad = A([128, 2050], F32, "upad")
        ya = A([128, 2048], F32, "ya")
        yb = A([128, 2048], F32, "yb")
        sgc = [A([128, 512], F32, "sgc%d" % i) for i in range(2)]
        yo = A([128, 2048], BF16, "syo")
        b.memset("dve", upad[:, 0:1], 0.0)
        b.memset("dve", upad[:, 2049:2050], 0.0)
        for cb in range(16):
            w0 = self.cols[:, C_SW + cb:C_SW + cb + 1]
            w1 = self.cols[:, C_SW + 16 + cb:C_SW + 16 + cb + 1]
            w2 = self.cols[:, C_SW + 32 + cb:C_SW + 32 + cb + 1]
            wb = self.load_w(self.w_in1[cb])
            self.proj(wb, lambda q, p: b.copy("act", cin[:, q * 512:(q + 1) * 512], p))
            wb = self.load_w(self.w_in1[32 + cb])
            self.proj(wb, lambda q, p: b.tt("dve", upad[:, 1 + q * 512:1 + (q + 1) * 512], p,
                                            cin[:, q * 512:(q + 1) * 512], ALU.mult))
            b.ts("dve", ya, upad[:, 1:2049], w1, None, ALU.mult)
            b.stt(yb, upad[:, 0:2048], w0, ya, ALU.mult, ALU.add)
            b.stt(ya, upad[:, 2:2050], w2, yb, ALU.mult, ALU.add)
            wb = self.load_w(self.w_in1[16 + cb])
            self.proj(wb, lambda q, p: b.tt("dve", yb[:, q * 512:(q + 1) * 512], p, ya[:, q * 512:(q + 1) * 512],
                                            ALU.mult))
            wb = self.load_w(self.w_in1[48 + cb])

            def cons(q, p):
                s = sgc[q % 2]
                b.act(s, p, AF.Silu)
                b.tt("dve", yo[:, q * 512:(q + 1) * 512], yb[:, q * 512:(q + 1) * 512], s, ALU.mult)

            self.proj(wb, cons)
            self.b.s.dma("act", [(self.yT_d.ap[cb * 128:(cb + 1) * 128, :], yo.ap)], reads=[yo.tok],
                         writes=[self.yT_d.tok], partial=True)

    def conv_block(self, wsrc, wcol, out, upad, yq):
        b = self.b
        w0 = self.cols[:, C_CW + wcol:C_CW + wcol + 1]
        w1 = self.cols[:, C_CW + 24 + wcol:C_CW + 24 + wcol + 1]
        w2 = self.cols[:, C_CW + 48 + wcol:C_CW + 48 + wcol + 1]
        cbias = self.cols[:, C_CB + wcol:C_CB + wcol + 1]
        wb = self.load_w(wsrc)
        self.proj(wb, lambda q, p: b.copy("act", upad[:, 1 + q * 512:1 + (q + 1) * 512], p))
        for q in range(4):
            ya, yb = yq[q % 2]
            b.ts("dve", ya, upad[:, 1 + q * 512:1 + (q + 1) * 512], w1, None, ALU.mult)
            b.stt(yb, upad[:, q * 512:(q + 1) * 512], w0, ya, ALU.mult, ALU.add)
            b.stt(ya, upad[:, 2 + q * 512:2 + (q + 1) * 512], w2, yb, ALU.mult, ALU.add)
            b.act(out[:, q * 512:(q + 1) * 512], ya, AF.Silu, bias=cbias)

    def ssd_phase(self):
        b = self.b
        PS = self.PS
        A = Arena(self.arena, R_PHASE)
        dt = A([128, 16, 64], F32, "dt")
        dta = A([128, 16, 64], F32, "dta")
        A0 = Arena(self.arena, (A.p, ARENA_BYTES))
        wdst = A0([128, 16, 64], F32, "wdst")
        wdtb = A0([128, 16, 64], BF16, "wdtb")
        xsT = A([128, 2, 2048], BF16, "xsT")
        xstok = A([128, 16, 256], BF16, "xstok")
        szT = A([128, 2, 2048], BF16, "szT")
        yacc = A([128, 16, 256], F32, "yacc")
        BT = A([128, 2048], BF16, "BT")
        CT = A([128, 2048], BF16, "CT")
        Btok = A([128, 16, 128], BF16, "Btok")
        upad = A([128, 2050], F32, "upad")
        yq = [(A([128, 512], F32, "cya%d" % i), A([128, 512], F32, "cyb%d" % i)) for i in range(2)]
        X = A([128, 512], F32, "X")
        Dm = A([128, 512], BF16, "Dm")
        Wm = A([128, 512], BF16, "Wm")
        cb = A([128, 128], F32, "cb")
        xdt = A([128, 256], BF16, "xdt")
        xdtw = A([128, 256], BF16, "xdtw")
        S32 = A([128, 256], F32, "S32")
        Sbf = A([128, 256], BF16, "Sbf")
        tz = A([128, 256], F32, "tz")
        ysum = A([128, 256], F32, "ysum")
        ac = A([128, 4], F32, "ac")
        ei = A([128, 4], F32, "ei")
        wj = A([128, 4], F32, "wj")
        et = A([128, 4], F32, "et")
        dif = A([128, 4], F32, "dif")
        y1 = A([128, 512], F32, "y1")
        y2 = A([128, 512], F32, "y2")
        ysq = A([128, 512], BF16, "ysq")
        yo = [A([128, 512], BF16, "dyo%d" % i) for i in range(2)]
        pE, pY = PS[2], PS[3]
        pS = PS[4][:, 0:256]
        pQ = PS[4][:, 256:260]
        pC = PS[0][:, 0:128]
        pSm = PS[1][:, 0:8]
        pT = PS[5]
        b.memset("dve", upad[:, 0:1], 0.0)
        b.memset("dve", upad[:, 2049:2050], 0.0)
        b.dma("sp", wdst, self.w_dt)
        b.copy("pool", wdtb, wdst)
        for half in range(2):
            p = PS[2 + half]
            for j in range(8):
                tt = half * 8 + j
                for kc in range(KC):
                    b.mm(p[:, j * 64:(j + 1) * 64], self.hT[:, kc, tt * 128:(tt + 1) * 128], wdtb[:, kc, :],
                         start=(kc == 0), stop=(kc == KC - 1))
            b.tt("dve", dt[:, half * 8:(half + 1) * 8, :], p.re("p (j h) -> p j h", h=64),
                 self.dtb_b.v(lambda a: a.unsqueeze(1)).bc([128, 8, 64]), ALU.add)
        b.act(dta, dt, AF.Exp)
        b.act(dt, dta, AF.Ln, bias=1.0)
        b.tt("dve", dta, dt, self.a_b.v(lambda a: a.unsqueeze(1)).bc([128, 16, 64]), ALU.mult)
        b.s.barrier()
        h4 = lambda t, d: t.re("p (h d) -> p h d", d=d)
        for u in range(8):
            g = u // 2
            for blk in range(2):
                self.conv_block(self.w_in1[80 + 2 * u + blk], 2 * u + blk, xsT[:, blk, :], upad, yq)
            for blk in range(2):
                wb = self.load_w(self.w_in1[64 + 2 * u + blk])
                self.proj(wb, lambda q, p, blk=blk: b.act(szT[:, blk, q * 512:(q + 1) * 512], p, AF.Silu))
            if u % 2 == 0:
                self.conv_block(self.w_in1[96 + g], 16 + g, BT, upad, yq)
                self.conv_block(self.w_in1[100 + g], 20 + g, CT, upad, yq)
                for q in range(4):
                    pb = self.PB[q % 2]
                    for j in range(4):
                        tt = 4 * q + j
                        b.tr(pb[:, j * 128:(j + 1) * 128], BT[:, tt * 128:(tt + 1) * 128], self.ident)
                    b.copy("act" if q % 2 else "dve", Btok[:, 4 * q:4 * q + 4, :], pb.re("p (j k) -> p j k", k=128))
            for tt in range(16):
                pb = self.PB[tt % 2]
                for blk in range(2):
                    b.tr(pb[:, blk * 128:(blk + 1) * 128], xsT[:, blk, tt * 128:(tt + 1) * 128], self.ident)
                b.copy("act" if tt % 2 else "dve", xstok[:, tt, :], pb[:, 0:256])
            for dr in range(2):
                order = list(range(16)) if dr == 0 else list(range(15, -1, -1))
                c0 = dr * 32 + 4 * u
                for idx, ch in enumerate(order):
                    chs = slice(ch * 128, (ch + 1) * 128)
                    dta_c = dta[:, ch, c0:c0 + 4]
                    dt_c = dt[:, ch, c0:c0 + 4]
                    b.mm(pSm[:, 0:4], self.tri[dr], dta_c)
                    b.mm(pSm[:, 4:8], self.ones_f, dta_c)
                    b.copy("dve", ac, pSm[:, 0:4])
                    b.act(ei, ac, AF.Exp)
                    b.tt("dve", dif, pSm[:, 4:8], ac, ALU.subtract)
                    b.act(wj, dif, AF.Exp)
                    b.act(et, pSm[:, 4:8], AF.Exp)
                    b.tt("dve", h4(X, 128), self.tri[dr].v(lambda a: a.unsqueeze(1)).bc([128, 4, 128]),
                         dta_c.v(lambda a: a.unsqueeze(2)).bc([128, 4, 128]), ALU.mult)
                    b.mm(pE, self.sgt[dr], X, start=True, stop=False)
                    b.mm(pE, self.ident, self.neg[dr], start=False, stop=True)
                    b.act(Dm, pE, AF.Exp)
                    b.mm(pC, BT[:, chs], CT[:, chs])
                    b.copy("act", cb, pC)
                    b.tt("dve", h4(Wm, 128), h4(Dm, 128), cb.v(lambda a: a.unsqueeze(1)).bc([128, 4, 128]), ALU.mult)
                    b.tt("dve", h4(xdt, 64), h4(xstok[:, ch, :], 64),
                         dt_c.v(lambda a: a.unsqueeze(2)).bc([128, 4, 64]), ALU.mult)
                    b.tt("dve", h4(xdtw, 64), h4(xdt, 64), wj.v(lambda a: a.unsqueeze(2)).bc([128, 4, 64]), ALU.mult)
                    for hl in range(4):
                        b.mm(pY[:, hl * 64:(hl + 1) * 64], Wm[:, hl * 128:(hl + 1) * 128],
                             xdt[:, hl * 64:(hl + 1) * 64])
                    ydst = yacc[:, ch, :]
                    if idx > 0:
                        b.mm(pY[:, 256:512], CT[:, chs], Sbf)
                        b.tt("dve", h4(tz, 64), h4(pY[:, 256:512], 64),
                             ei.v(lambda a: a.unsqueeze(2)).bc([128, 4, 64]), ALU.mult)
                        if dr == 0:
                            b.tt("dve", ydst, pY[:, 0:256], tz, ALU.add)
                        else:
                            b.tt("dve", ysum, pY[:, 0:256], tz, ALU.add)
                            b.tt("dve", ydst, ydst, ysum, ALU.add)
                    else:
                        if dr == 0:
                            b.copy("dve", ydst, pY[:, 0:256])
                        else:
                            b.tt("dve", ydst, pY[:, 0:256], ydst, ALU.add)
                    if idx < 15:
                        b.mm(pS, Btok[:, ch, :], xdtw)
                        if idx == 0:
                            b.copy("dve", S32, pS)
                        else:
                            b.tt("dve", h4(S32, 64), h4(S32, 64), et.v(lambda a: a.unsqueeze(2)).bc([128, 4, 64]),
                                 ALU.mult)
                            b.tt("dve", S32, pS, S32, ALU.add)
                        b.copy("act", Sbf, S32)
            for blk in range(2):
                chb = 2 * u + blk
                for q in range(4):
                    qs = slice(q * 512, (q + 1) * 512)
                    for j in range(4):
                        tt = 4 * q + j
                        b.tr(pT[:, j * 128:(j + 1) * 128], yacc[:, tt, blk * 128:(blk + 1) * 128], self.identf)
                    b.stt(y1, xsT[:, blk, qs], self.cols[:, C_DD + chb:C_DD + chb + 1], pT, ALU.mult, ALU.add)
                    b.tt("dve", y2, y1, szT[:, blk, qs], ALU.mult)
                    b.act(ysq, y2, AF.Square)
                    for j in range(4):
                        b.mm(pQ[:, j:j + 1], ysq[:, j * 128:(j + 1) * 128], self.ones_bf[:, 0:1])
                    b.tt("dve", self.ssq[:, 4 * q:4 * q + 4], pQ, self.ssq[:, 4 * q:4 * q + 4], ALU.add)
                    y = yo[q % 2]
                    b.ts("dve", y, y2, self.cols[:, C_DNW + chb:C_DNW + chb + 1], None, ALU.mult)
                    r0 = 2048 + chb * 128
                    self.b.s.dma("act", [(self.yT_d.ap[r0:r0 + 128, qs], y.ap)], reads=[y.tok],
                                 writes=[self.yT_d.tok], partial=True)


def _colize(v):
    v = np.asarray(v, np.float32)
    v = v.reshape(-1, v.shape[-1] // 128, 128)
    return np.ascontiguousarray(v.transpose(2, 0, 1).reshape(128, -1))


_CONST_CACHE = {}


def _constants():
    if _CONST_CACHE:
        return _CONST_CACHE
    j = np.arange(128)[:, None]
    i = np.arange(128)[None, :]
    same = (j // 32) == (i // 32)
    cst = np.zeros((128, NCST), np.float32)
    cst[:, K_ID:K_ID + 128] = np.eye(128)
    cst[:, K_MF:K_MF + 128] = (same & (j <= i))
    cst[:, K_MB:K_MB + 128] = (same & (j >= i))
    cst[:, K_TF:K_TF + 128] = (j <= i)
    cst[:, K_TB:K_TB + 128] = (j >= i)
    cst[:, K_GF:K_GF + 128] = (j > i)
    cst[:, K_GB:K_GB + 128] = (j < i)
    cst[:, K_NF:K_NF + 128] = np.where(j <= i, 0.0, -30000.0)
    cst[:, K_NB:K_NB + 128] = np.where(j >= i, 0.0, -30000.0)
    rst = np.ones((128, 512), np.float32)
    rst[:, ::32] = 0.0
    cst[:, K_RST:K_RST + 512] = rst
    c = np.arange(256)
    m = (c[:, None] * c[None, :]) % 256
    ang = 2.0 * np.pi * m.astype(np.float64) / 256.0
    scale = 1.0 / np.sqrt(2048.0 * 256.0)
    for w, fn in enumerate((np.cos, np.sin)):
        tab = (fn(ang) * scale).astype(np.float32)
        cst[:, K_CC + w * 512:K_CC + (w + 1) * 512] = tab.reshape(2, 128, 256).transpose(1, 0, 2).reshape(128, 512)
    s = np.arange(2048, dtype=np.int64)
    m = (s[:, None] * s[None, :]) % 2048
    ang = 2.0 * np.pi * m.astype(np.float64) / 2048.0
    tabf = np.zeros((2, 4, 2, 128, 8, 512), np.float32)
    for w, fn in enumerate((np.cos, np.sin)):
        tab = fn(ang).astype(np.float32)
        tabf[w] = tab.reshape(2, 8, 128, 4, 512).transpose(3, 0, 2, 1, 4)
    _CONST_CACHE["cst"] = cst
    _CONST_CACHE["tabf"] = np.ascontiguousarray(tabf.reshape(16, 128, 8, 512))
    return _CONST_CACHE


def prepare_inputs(x, norm_w, final_norm_w, ev_w_in, ev_w_out, hgrn_lb_logits, hgrn_norm_w,
                   fnet_w, fnet_b, od_w_in, od_w_out, sconv_w, ssd_conv_w, ssd_conv_b,
                   ssd_dt_bias, ssd_a_log, ssd_d, ssd_norm_w):
    f = lambda a: np.asarray(a, np.float32)
    cs = _constants()
    w0 = f(ev_w_in)[0]
    w_in0 = np.ascontiguousarray(w0.reshape(16, 128, 112, 128).transpose(2, 1, 0, 3))
    w1 = f(od_w_in)[0]
    w_in1 = np.ascontiguousarray(w1[:, :13312].reshape(16, 128, 104, 128).transpose(2, 1, 0, 3))
    w_dt = np.ascontiguousarray(w1[:, 13312:13376].reshape(16, 128, 64).transpose(1, 0, 2))
    wo = lambda w: np.ascontiguousarray(f(w)[0].reshape(32, 128, 4, 512).transpose(2, 1, 0, 3))
    cols = np.concatenate([
        _colize(f(norm_w)[0:1]), _colize(f(norm_w)[1:2]), _colize(f(hgrn_lb_logits)),
        _colize(f(hgrn_norm_w)[0:1]), _colize(f(fnet_b)[0:1]), _colize(f(sconv_w)[0]),
        _colize(f(ssd_conv_w)[0]), _colize(f(ssd_conv_b)[0:1]),
        _colize(np.repeat(f(ssd_d)[0], 64)[None, :]), _colize(f(ssd_norm_w)[0:1])], axis=1)
    assert cols.shape == (128, NCOLS), cols.shape
    shared = {
        "w_in0": w_in0, "w_out0": wo(ev_w_out), "w_in1": w_in1, "w_dt": w_dt, "w_out1": wo(od_w_out),
        "cols": np.ascontiguousarray(cols), "cst": cs["cst"], "fnw": f(final_norm_w).reshape(1, D),
        "dtb": f(ssd_dt_bias).reshape(1, 64), "alog": f(ssd_a_log).reshape(1, 64),
        "fw": np.ascontiguousarray(f(fnet_w)[0].reshape(8, 2, 128, 256).transpose(0, 2, 1, 3)),
        "tabf": cs["tabf"],
    }
    xs = f(x)
    return [dict(shared, x=np.ascontiguousarray(xs[i])) for i in range(xs.shape[0])]


def build_program(debug=False, stop_after=None):
    nc = bass.Bass("TRN2", target_bir_lowering=False)
    Builder(nc, debug=debug, stop_after=stop_after).build()
    return nc


def kernel(**inputs):
    in_maps = prepare_inputs(**inputs)
    nc = build_program()
    res = run_bass_kernel_spmd(nc, in_maps, core_ids=list(range(8)))
    return np.stack([np.asarray(r["out"], np.float32) for r in res.results], axis=0)
```

```python
from contextlib import ExitStack
import numpy as np
import concourse.bass as bass
import concourse.mybir as mybir
from concourse.bass_utils import run_bass_kernel_spmd

F32 = mybir.dt.float32
BF16 = mybir.dt.bfloat16
AF = mybir.ActivationFunctionType
ALU = mybir.AluOpType
AX = mybir.AxisListType

L = 2048
D = 2048
NT = 16
KC = 16
EPS = 1e-6

ENGS = ("pe", "act", "dve", "pool", "sp")


class Tok:
    __slots__ = ("name", "w", "r", "war")

    def __init__(self, name=""):
        self.name = name
        self.w = {}
        self.r = {}
        self.war = {}


def _merge(dst, src):
    for k, v in src.items():
        if dst.get(k, -1) < v:
            dst[k] = v


class T:
    __slots__ = ("ap", "tok")

    def __init__(self, ap, tok=None, name=""):
        self.ap = ap
        self.tok = tok if tok is not None else Tok(name)

    def __getitem__(self, idx):
        return T(self.ap[idx], self.tok)

    def v(self, fn):
        return T(fn(self.ap), self.tok)

    def bc(self, shape):
        return T(self.ap.broadcast_to(list(shape)), self.tok)

    def re(self, s, **kw):
        return T(self.ap.rearrange(s, **kw), self.tok)


def _ap(x):
    return x.ap if isinstance(x, T) else x


def _toks(*xs):
    return [x.tok for x in xs if isinstance(x, T)]


class Op:
    __slots__ = ("eng", "fn", "deps", "seq", "signal", "sigval", "dma", "dsem", "dprev", "dval")


class Sched:
    def __init__(self, nc, n_dma_sems=48):
        self.nc = nc
        self.eng_ops = {e: [] for e in ENGS}
        self.n_dma_sems = n_dma_sems
        self.dma_val = [0] * n_dma_sems
        self.dma_rr = 0
        self.out_deps = {}

    def _deps(self, reads, writes, partial):
        deps = {}
        for t in reads:
            _merge(deps, t.w)
        for t in writes:
            _merge(deps, t.r)
            _merge(deps, t.war)
            if not partial:
                _merge(deps, t.w)
        return deps

    def _update(self, key, val, reads, writes, partial):
        me = {key: val}
        for t in reads:
            _merge(t.r, me)
        for t in writes:
            if t.r:
                t.war = t.r
                t.r = {}
                t.w = dict(me)
            elif partial:
                _merge(t.w, me)
            else:
                t.w = dict(me)

    def op(self, eng, fn, reads=(), writes=(), partial=False):
        o = Op()
        o.eng = eng
        o.fn = fn
        o.dma = None
        o.signal = False
        o.deps = self._deps(reads, writes, partial)
        lst = self.eng_ops[eng]
        lst.append(o)
        o.seq = len(lst)
        self._update(eng, o.seq, reads, writes, partial)
        return o

    def dma(self, queue, parts, reads=(), writes=(), partial=False, is_output=False):
        o = Op()
        o.eng = queue
        o.fn = None
        o.dma = parts
        o.signal = False
        o.deps = self._deps(reads, writes, partial)
        s = self.dma_rr
        self.dma_rr = (self.dma_rr + 1) % self.n_dma_sems
        o.dsem = s
        o.dprev = self.dma_val[s]
        self.dma_val[s] += 16 * len(parts)
        o.dval = self.dma_val[s]
        lst = self.eng_ops[queue]
        lst.append(o)
        o.seq = len(lst)
        self._update(("d", s), o.dval, reads, writes, partial)
        if is_output:
            _merge(self.out_deps, {("d", s): o.dval})
        return o

    def barrier(self):
        deps = {e: len(self.eng_ops[e]) for e in ENGS if len(self.eng_ops[e]) > 0 and e != "sp"}
        for s in range(self.n_dma_sems):
            if self.dma_val[s] > 0:
                deps[("d", s)] = self.dma_val[s]
        for e in ENGS:
            o = Op()
            o.eng = e
            o.fn = None
            o.dma = None
            o.signal = False
            o.deps = {k: v for k, v in deps.items() if k != e}
            lst = self.eng_ops[e]
            lst.append(o)
            o.seq = len(lst)

    def emit(self, es):
        nc = self.nc
        for e in ENGS:
            for o in self.eng_ops[e]:
                nd = {}
                for k, v in o.deps.items():
                    if isinstance(k, str):
                        if k == "pe" and e == "pe":
                            continue
                        while v > 0 and self.eng_ops[k][v - 1].fn is None and self.eng_ops[k][v - 1].dma is None:
                            v -= 1
                        if v == 0:
                            continue
                        p = self.eng_ops[k][v - 1]
                        if p.dma is not None:
                            _merge(nd, {("d", p.dsem): p.dval})
                        else:
                            p.signal = True
                            _merge(nd, {k: v})
                    else:
                        _merge(nd, {k: v})
                o.deps = nd
        for e in ENGS:
            c = 0
            for o in self.eng_ops[e]:
                if o.signal:
                    c += 1
                    o.sigval = c
        esem = {e: es.enter_context(nc.semaphore("s_" + e)) for e in ENGS}
        dsems = [es.enter_context(nc.semaphore("d%d" % i)) for i in range(self.n_dma_sems)]
        block = es.enter_context(nc.Block())
        sched = self

        def run(en, e):
            waited = {}
            for o in sched.eng_ops[en]:
                for k, v in o.deps.items():
                    if isinstance(k, str):
                        kk = k
                        tgt = sched.eng_ops[k][v - 1].sigval
                        sem = esem[k]
                    else:
                        kk = k
                        tgt = v
                        sem = dsems[k[1]]
                    if waited.get(kk, 0) >= tgt:
                        continue
                    waited[kk] = tgt
                    e.wait_ge(sem, tgt)
                if o.dma is not None:
                    kk = ("d", o.dsem)
                    if o.dprev > 0 and waited.get(kk, 0) < o.dprev:
                        e.wait_ge(dsems[o.dsem], o.dprev)
                        waited[kk] = o.dprev
                    for (oa, ia) in o.dma:
                        e.dma_start(out=oa, in_=ia).then_inc(dsems[o.dsem], 16)
                elif o.fn is not None:
                    inst = o.fn(e)
                    if o.signal:
                        inst.then_inc(esem[en], 1)
            if en == "sp":
                for k, v in sched.out_deps.items():
                    if waited.get(k, 0) < v:
                        e.wait_ge(dsems[k[1]], v)

        @block.sync
        def _(e):
            run("sp", e)

        @block.scalar
        def _(e):
            run("act", e)

        @block.vector
        def _(e):
            run("dve", e)

        @block.gpsimd
        def _(e):
            run("pool", e)

        @block.tensor
        def _(e):
            run("pe", e)


class B:
    def __init__(self, nc):
        self.nc = nc
        self.s = Sched(nc)

    def mm(self, out, lhsT, rhs, start=True, stop=True, tile_position=None):
        o, a, b = _ap(out), _ap(lhsT), _ap(rhs)
        kw = {}
        if tile_position is not None:
            kw["tile_position"] = tile_position
        self.s.op("pe", lambda e: e.matmul(o, lhsT=a, rhs=b, start=start, stop=stop, **kw),
                  reads=_toks(lhsT, rhs), writes=_toks(out), partial=True)

    def tr(self, out, in_, ident):
        o, a, i = _ap(out), _ap(in_), _ap(ident)
        self.s.op("pe", lambda e: e.transpose(o, a, i), reads=_toks(in_, ident), writes=_toks(out), partial=True)

    def act(self, out, in_, func, bias=None, scale=None, accum=None, extra_w=()):
        o, a = _ap(out), _ap(in_)
        kw = {}
        if bias is not None:
            kw["bias"] = _ap(bias)
        if scale is not None:
            kw["scale"] = _ap(scale)
        if accum is not None:
            kw["accum_out"] = _ap(accum)
        self.s.op("act", lambda e: e.activation(o, a, func, **kw),
                  reads=_toks(in_, bias, scale), writes=_toks(out, accum) + list(extra_w))

    def tt(self, eng, out, in0, in1, op):
        o, a, b = _ap(out), _ap(in0), _ap(in1)
        self.s.op(eng, lambda e: e.tensor_tensor(o, a, b, op), reads=_toks(in0, in1), writes=_toks(out))

    def ts(self, eng, out, in0, s1, s2=None, op0=ALU.mult, op1=None):
        o, a = _ap(out), _ap(in0)
        x1, x2 = _ap(s1), _ap(s2)
        if op1 is None:
            self.s.op(eng, lambda e: e.tensor_scalar(o, a, x1, None, op0),
                      reads=_toks(in0, s1), writes=_toks(out))
        else:
            self.s.op(eng, lambda e: e.tensor_scalar(o, a, x1, x2, op0, op1),
                      reads=_toks(in0, s1, s2), writes=_toks(out))

    def stt(self, out, in0, scalar, in1, op0, op1):
        o, a, c, b = _ap(out), _ap(in0), _ap(scalar), _ap(in1)
        self.s.op("dve", lambda e: e.scalar_tensor_tensor(o, a, c, b, op0, op1),
                  reads=_toks(in0, scalar, in1), writes=_toks(out))

    def scan(self, out, d0, d1, initial, op0=ALU.mult, op1=ALU.add):
        o, a, b, i = _ap(out), _ap(d0), _ap(d1), _ap(initial)
        self.s.op("dve", lambda e: e.tensor_tensor_scan(o, a, b, i, op0, op1),
                  reads=_toks(d0, d1, initial), writes=_toks(out))

    def copy(self, eng, out, in_):
        o, a = _ap(out), _ap(in_)
        if eng == "act":
            self.s.op("act", lambda e: e.copy(o, a), reads=_toks(in_), writes=_toks(out))
        else:
            self.s.op(eng, lambda e: e.tensor_copy(o, a), reads=_toks(in_), writes=_toks(out))

    def recip(self, out, in_):
        o, a = _ap(out), _ap(in_)
        self.s.op("dve", lambda e: e.reciprocal(o, a), reads=_toks(in_), writes=_toks(out))

    def memset(self, eng, out, val):
        o = _ap(out)
        self.s.op(eng, lambda e: e.memset(o, val), writes=_toks(out))

    def dma(self, queue, out, in_, is_output=False):
        self.s.dma(queue, [(_ap(out), _ap(in_))], reads=_toks(in_), writes=_toks(out), is_output=is_output)

    def dmas(self, queue, parts, is_output=False):
        self.s.dma(queue, [(_ap(o), _ap(i)) for o, i in parts],
                   reads=_toks(*[i for _, i in parts]), writes=_toks(*[o for o, _ in parts]),
                   is_output=is_output)


ARENA_BYTES = 206 * 1024
R_CONST = (0, 13312)
R_WEIGHT = (13312, 50176)
R_HT = (50176, 115712)
R_PHASE = (115712, ARENA_BYTES)
R_BIG = (13312, ARENA_BYTES)

C_NW0, C_NW1, C_LB, C_HNW, C_FB, C_SW, C_CW, C_CB, C_DD, C_DNW = 0, 16, 32, 80, 96, 112, 160, 232, 256, 272
NCOLS = 288
K_ID, K_MF, K_MB, K_TF, K_TB, K_GF, K_GB, K_NF, K_NB, K_RST, K_CC = 0, 128, 256, 384, 512, 640, 768, 896, 1024, 1152, 1664
NCST = 1664 + 1024


def _esize(dt):
    return 4 if dt == F32 else 2


class Arena:
    def __init__(self, arena_ap, region):
        self.a = arena_ap
        self.lo, self.hi = region
        self.p = self.lo

    def __call__(self, shape, dt=F32, name=""):
        n = 1
        for s in shape[1:]:
            n *= s
        nb = n * _esize(dt)
        nb_al = (nb + 31) // 32 * 32
        assert self.p + nb_al <= self.hi, "arena overflow %s: need %d have %d" % (name, nb_al, self.hi - self.p)
        v = self.a[0:shape[0], self.p:self.p + nb].bitcast(dt)
        self.p += nb_al
        if len(shape) == 3:
            v = v.rearrange("p (a b) -> p a b", a=shape[1])
        elif len(shape) == 4:
            v = v.rearrange("p (a b c) -> p a b c", a=shape[1], b=shape[2])
        return T(v, name=name)


class Builder:
    def __init__(self, nc, debug=False, stop_after=None):
        self.nc = nc
        self.b = B(nc)
        self.debug = debug
        self.stop_after = stop_after
        self.wi = 0
        self.pi = 0

    def din(self, name, shape, dt=F32):
        return T(self.nc.dram_tensor(name, list(shape), dt, kind="ExternalInput").ap(), name=name)

    def dscr(self, name, shape, dt=F32, out=False):
        kind = "ExternalOutput" if (out or self.debug) else "Internal"
        return T(self.nc.dram_tensor(name, list(shape), dt, kind=kind).ap(), name=name)

    def build(self):
        nc, b = self.nc, self.b
        self.x_d = self.din("x", [L, D])
        self.w_in0 = self.din("w_in0", [112, 128, 16, 128])
        self.w_out0 = self.din("w_out0", [4, 128, 32, 512])
        self.w_in1 = self.din("w_in1", [104, 128, 16, 128])
        self.w_dt = self.din("w_dt", [128, 16, 64])
        self.w_out1 = self.din("w_out1", [4, 128, 32, 512])
        self.cols_d = self.din("cols", [128, NCOLS])
        self.cst_d = self.din("cst", [128, NCST])
        self.fnw_d = self.din("fnw", [1, D])
        self.dtb_d = self.din("dtb", [1, 64])
        self.alog_d = self.din("alog", [1, 64])
        self.fw_d = self.din("fw", [8, 128, 2, 256])
        self.tabf_d = self.din("tabf", [16, 128, 8, 512])
        self.tabb_d = self.dscr("tabb", [16, 128, 8, 512], BF16)
        self.yT_d = self.dscr("yT", [4096, L], BF16)
        self.x1_d = self.dscr("x1", [L, D])
        self.x2_d = self.dscr("x2", [L, D])
        self.out_d = T(nc.dram_tensor("out", [L, D], F32, kind="ExternalOutput").ap(), name="out")
        self.arena = nc.alloc_sbuf_tensor("arena", [128, ARENA_BYTES], mybir.dt.uint8).ap()
        self.PS = [T(nc.alloc_psum_tensor("ps%d" % i, [128, 512], F32).ap(), name="ps%d" % i) for i in range(6)]
        self.PB = [T(nc.alloc_psum_tensor("pb%d" % i, [128, 1024], BF16).ap()[:, 0:512], name="pb%d" % i)
                   for i in range(2)]
        C = Arena(self.arena, R_CONST)
        self.cols = C([128, NCOLS], F32, "cols")
        self.ident = C([128, 128], BF16, "ident")
        self.identf = C([128, 128], F32, "identf")
        self.ones_bf = C([128, 128], BF16, "ones_bf")
        self.ones_f = C([128, 128], F32, "ones_f")
        self.rst = C([128, 512], F32, "rst")
        self.maskF = C([128, 128], F32, "maskF")
        self.maskB = C([128, 128], F32, "maskB")
        self.tri = [C([128, 128], F32, "triF"), C([128, 128], F32, "triB")]
        self.sgt = [C([128, 128], F32, "sgtF"), C([128, 128], F32, "sgtB")]
        self.neg = [C([128, 512], BF16, "negF"), C([128, 512], BF16, "negB")]
        self.cc = [C([128, 2, 256], BF16, "ccC"), C([128, 2, 256], BF16, "ccS")]
        self.lb3 = C([128, 3, 16], F32, "lb3")
        self.dtb_b = C([128, 64], F32, "dtb_b")
        self.a_b = C([128, 64], F32, "a_b")
        self.ssq = C([128, 16], F32, "ssq")
        self.rstd = C([128, 16], F32, "rstd")
        W = Arena(self.arena, R_WEIGHT)
        self.wstage = [W([128, 16, 128], F32, "wst%d" % i) for i in range(2)]
        self.wring = [W([128, 16, 128], BF16, "wr%d" % i) for i in range(5)]
        H = Arena(self.arena, R_HT)
        self.hT = H([128, 16, 2048], BF16, "hT")

        self.setup_phase()
        b.s.barrier()
        if self.stop_after == "setup":
            return self.finish()
        self.norm_phase(self.x_d, C_NW0)
        b.s.barrier()
        if self.stop_after == "norm0":
            return self.finish()
        self.hgrn_phase()
        b.s.barrier()
        if self.stop_after == "hgrn":
            return self.finish()
        self.fnet_phase()
        b.s.barrier()
        if self.stop_after == "fnet":
            return self.finish()
        self.outproj_phase(self.w_out0, self.x_d, self.x1_d, split=False)
        b.s.barrier()
        if self.stop_after == "out0":
            return self.finish()
        self.norm_phase(self.x1_d, C_NW1)
        b.s.barrier()
        self.sconv_phase()
        b.s.barrier()
        if self.stop_after == "sconv":
            return self.finish()
        self.ssd_phase()
        b.s.barrier()
        if self.stop_after == "ssd":
            return self.finish()
        self.outproj_phase(self.w_out1, self.x1_d, self.x2_d, split=True)
        b.s.barrier()
        self.final_norm(self.x2_d, self.out_d)
        return self.finish()

    def finish(self):
        if self.debug and self.stop_after in ("norm0", "hgrn", "fnet", "sconv", "ssd"):
            self.b.s.barrier()
            hd = T(self.nc.dram_tensor("dbg_hT", [128, 16, 2048], BF16, kind="ExternalOutput").ap())
            self.b.dma("sp", hd, self.hT, is_output=True)
        if self.stop_after is not None:
            A = Arena(self.arena, R_PHASE)
            t = A([128, 512], F32, "fin")
            self.b.s.barrier()
            self.b.memset("dve", t, 0.0)
            self.b.dma("sp", self.out_d[0:128, 0:512], t, is_output=True)
        self._es = ExitStack()
        self.b.s.emit(self._es)
        self._es.close()
        return self.nc

    def load_w(self, src):
        b = self.b
        st = self.wstage[self.wi % 2]
        wb = self.wring[self.wi % len(self.wring)]
        self.wi += 1
        b.dma("sp", st, src)
        b.copy("pool", wb, st)
        return wb

    def proj(self, wb, consumer):
        b = self.b
        for q in range(4):
            p = self.PS[self.pi % 2]
            self.pi += 1
            for kc in range(KC):
                b.mm(p, wb[:, kc, :], self.hT[:, kc, q * 512:(q + 1) * 512], start=(kc == 0), stop=(kc == KC - 1))
            consumer(q, p)

    def setup_phase(self):
        b = self.b
        A = Arena(self.arena, R_PHASE)
        cst = A([128, NCST], F32, "cst")
        b.dma("sp", self.cols, self.cols_d)
        b.dma("sp", cst, self.cst_d)
        b.dma("sp", self.dtb_b, self.dtb_d.bc([128, 64]))
        al = A([128, 64], F32, "al")
        b.dma("sp", al, self.alog_d.bc([128, 64]))
        b.copy("dve", self.ident, cst[:, K_ID:K_ID + 128])
        b.copy("dve", self.identf, cst[:, K_ID:K_ID + 128])
        b.memset("dve", self.ones_bf, 1.0)
        b.memset("dve", self.ones_f, 1.0)
        b.copy("dve", self.rst, cst[:, K_RST:K_RST + 512])
        b.copy("dve", self.maskF, cst[:, K_MF:K_MF + 128])
        b.copy("dve", self.maskB, cst[:, K_MB:K_MB + 128])
        b.copy("dve", self.tri[0], cst[:, K_TF:K_TF + 128])
        b.copy("dve", self.tri[1], cst[:, K_TB:K_TB + 128])
        b.copy("dve", self.sgt[0], cst[:, K_GF:K_GF + 128])
        b.copy("dve", self.sgt[1], cst[:, K_GB:K_GB + 128])
        for d, k in ((0, K_NF), (1, K_NB)):
            b.copy("dve", self.neg[d].re("p (h i) -> p h i", h=4),
                   cst[:, k:k + 128].v(lambda a: a.unsqueeze(1)).bc([128, 4, 128]))
        b.copy("dve", self.cc[0], cst[:, K_CC:K_CC + 512].re("p (k d) -> p k d", k=2))
        b.copy("dve", self.cc[1], cst[:, K_CC + 512:K_CC + 1024].re("p (k d) -> p k d", k=2))
        b.act(al, al, AF.Exp)
        b.ts("dve", self.a_b, al, -1.0, None, ALU.mult)
        l0 = self.cols[:, C_LB:C_LB + 16]
        l1 = self.cols[:, C_LB + 16:C_LB + 32]
        l2 = self.cols[:, C_LB + 32:C_LB + 48]
        t1 = A([128, 16], F32, "t1")
        t2 = A([128, 16], F32, "t2")
        b.tt("dve", t1, l1, l0, ALU.subtract)
        b.tt("dve", t2, l2, l0, ALU.subtract)
        b.act(t1, t1, AF.Exp)
        b.act(t2, t2, AF.Exp)
        b.tt("dve", t1, t1, t2, ALU.add)
        b.ts("dve", t1, t1, 1.0, None, ALU.add)
        b.recip(self.lb3[:, 0, :], t1)
        b.ts("dve", self.lb3[:, 1, :], self.lb3[:, 0, :], -1.0, 1.0, ALU.mult, ALU.add)
        b.ts("dve", self.lb3[:, 2, :], self.lb3[:, 0, :], -1.0, None, ALU.add)
        b.memset("dve", self.ssq, 0.0)
        st = [A([128, 8, 512], F32, "tst%d" % i) for i in range(2)]
        sb_ = [A([128, 8, 512], BF16, "tsb%d" % i) for i in range(2)]
        for i in range(16):
            b.dma("sp", st[i % 2], self.tabf_d[i])
            b.copy("pool" if i % 2 == 0 else "act", sb_[i % 2], st[i % 2])
            self.b.s.dma("sp", [(self.tabb_d.ap[i], sb_[i % 2].ap)], reads=[sb_[i % 2].tok],
                         writes=[self.tabb_d.tok], partial=True)

    def norm_phase(self, xsrc, ccol):
        b = self.b
        A = Arena(self.arena, R_PHASE)
        xt = [A([128, 2048], F32, "nxt%d" % i) for i in range(2)]
        junk = A([128, 2048], BF16, "njunk")
        xn = A([128, 4, 2048], BF16, "nxn")
        sm = [[A([128, 1], F32) for _ in range(4)] for _ in range(2)]
        for g in range(4):
            for j in range(4):
                tt = g * 4 + j
                x_t = xt[tt % 2]
                s1, s2, s3, s4 = sm[tt % 2]
                b.dma("sp", x_t, xsrc[tt * 128:(tt + 1) * 128, :])
                b.act(junk, x_t, AF.Square, accum=s1)
                b.ts("dve", s2, s1, 1.0 / D, EPS, ALU.mult, ALU.add)
                b.act(s3, s2, AF.Sqrt)
                b.recip(s4, s3)
                b.act(xn[:, j, :], x_t, AF.Copy, scale=s4)
            for c in range(16):
                pb = self.PB[c % 2]
                for j in range(4):
                    b.tr(pb[:, j * 128:(j + 1) * 128], xn[:, j, c * 128:(c + 1) * 128], self.ident)
                dst = self.hT[:, c, g * 512:(g + 1) * 512]
                sc = self.cols[:, ccol + c:ccol + c + 1]
                b.act(dst, pb, AF.Copy, scale=sc)

    def hgrn_phase(self):
        b = self.b
        A = Arena(self.arena, R_PHASE)
        qf = A([128, 2048], F32, "qf")
        vT = A([128, 2048], BF16, "vT")
        V = A([128, 16, 128], BF16, "V")
        sg = A([128, 2048], BF16, "sg")
        qin = A([128, 2048], BF16, "qin")
        kin = A([128, 2048], BF16, "kin")
        kend = A([128, 16, 128], BF16, "kend")
        kendT = [A([128, 512], BF16, "kendT%d" % i) for i in range(2)]
        Sbf = A([128, 64, 128], BF16, "Sbf")
        Of = A([128, 2048], F32, "Of")
        sn, lf, bb, b2, E, En, kinf = [A([128, 512], F32, n) for n in ("sn", "lf", "bb", "b2", "E", "En", "kinf")]
        Elast = A([128, 64], F32, "Elast")
        S32 = [A([128, 128], F32, "S32_%d" % i) for i in range(2)]
        Am = [A([128, 128], BF16, "Am%d" % i) for i in range(2)]
        osq = A([128, 512], BF16, "osq")
        ot = A([128, 512], F32, "ot")
        rs1 = A([128, 512], F32, "rs1")
        rs2 = A([128, 512], F32, "rs2")
        yT = A([128, 2048], BF16, "yTt")
        PS = self.PS
        r3 = lambda t: t.re("p (c k) -> p c k", k=32)
        for hd in range(16):
            oml = self.lb3[:, 1, hd:hd + 1]
            noml = self.lb3[:, 2, hd:hd + 1]
            wq = self.load_w(self.w_in0[0 * 16 + hd])
            self.proj(wq, lambda q, p: b.copy("act", qf[:, q * 512:(q + 1) * 512], p))
            wv = self.load_w(self.w_in0[1 * 16 + hd])
            self.proj(wv, lambda q, p: b.copy("dve", vT[:, q * 512:(q + 1) * 512], p))
            for q in range(4):
                pb = self.PB[q % 2]
                for j in range(4):
                    b.tr(pb[:, j * 128:(j + 1) * 128], vT[:, (4 * q + j) * 128:(4 * q + j + 1) * 128], self.ident)
                b.copy("act" if q % 2 else "dve", V[:, 4 * q:4 * q + 4, :], pb.re("p (j k) -> p j k", k=128))
            wg = self.load_w(self.w_in0[4 * 16 + hd])
            self.proj(wg, lambda q, p: b.act(sg[:, q * 512:(q + 1) * 512], p, AF.Silu))
            for dr in (0, 1):
                wz = self.load_w(self.w_in0[(2 + dr) * 16 + hd])

                def cons(q, p, dr=dr):
                    qs = slice(q * 512, (q + 1) * 512)
                    b.act(sn, p, AF.Sigmoid, scale=-1.0)
                    b.act(lf, sn, AF.Ln, bias=1.0, scale=noml)
                    b.scan(bb, self.rst, lf, 0.0)
                    if dr == 0:
                        bsel = bb
                    else:
                        b.tt("dve", r3(b2), r3(bb)[:, :, 31:32].bc([128, 16, 32]), r3(bb), ALU.subtract)
                        b.tt("dve", b2, b2, lf, ALU.add)
                        bsel = b2
                    b.act(E, bsel, AF.Exp)
                    b.act(En, bsel, AF.Exp, scale=-1.0)
                    b.tt("dve", qin[:, qs], qf[:, qs], E, ALU.mult)
                    b.stt(kinf, sn, oml, En, ALU.mult, ALU.mult)
                    b.copy("act", kin[:, qs], kinf)
                    edge = r3(E)[:, :, 31:32] if dr == 0 else r3(E)[:, :, 0:1]
                    kT = kendT[q % 2]
                    b.tt("dve", r3(kT), r3(kinf), edge.bc([128, 16, 32]), ALU.mult)
                    b.copy("dve", Elast[:, 16 * q:16 * q + 16].v(lambda a: a.unsqueeze(2)), edge)
                    pb = self.PB[q % 2]
                    for j in range(4):
                        b.tr(pb[:, j * 128:(j + 1) * 128], kT[:, j * 128:(j + 1) * 128], self.ident)
                    b.copy("act", kend[:, 4 * q:4 * q + 4, :], pb.re("p (j k) -> p j k", k=128))

                self.proj(wz, cons)
                order = list(range(64)) if dr == 0 else list(range(63, -1, -1))
                b.memset("dve", Sbf[:, order[0], :], 0.0)
                for idx, c in enumerate(order[:-1]):
                    tile, k = c // 4, c % 4
                    pS = PS[2 + k][:, 0:128]
                    b.mm(pS, kend[32 * k:32 * k + 32, tile, :], V[32 * k:32 * k + 32, tile, :],
                         tile_position=(32 * k, 0))
                    new, old = S32[idx % 2], S32[(idx + 1) % 2]
                    if idx == 0:
                        b.copy("dve", new, pS)
                    else:
                        b.stt(new, old, Elast[:, c:c + 1], pS, ALU.mult, ALU.add)
                    b.copy("act", Sbf[:, order[idx + 1], :], new)
                mask = self.maskF if dr == 0 else self.maskB
                zero_c = order[0]
                for q in range(4):
                    qs = slice(q * 512, (q + 1) * 512)
                    pO = PS[4 + q % 2]
                    for bl in range(4):
                        bi = 4 * q + bl
                        bs = slice(bi * 128, (bi + 1) * 128)
                        pA = PS[2 + bi % 2][:, 128:256]
                        am = Am[bi % 2]
                        b.mm(pA, kin[:, bs], qin[:, bs])
                        b.tt("dve", am, pA, mask, ALU.mult)
                        o_sl = pO[:, bl * 128:(bl + 1) * 128]
                        cl = [4 * bi + k for k in range(4) if 4 * bi + k != zero_c]
                        b.mm(o_sl, V[:, bi, :], am, start=True, stop=False)
                        for c in cl:
                            k = c % 4
                            b.mm(o_sl[:, 32 * k:32 * k + 32], Sbf[:, c, :], qin[:, c * 32:(c + 1) * 32],
                                 start=False, stop=(c == cl[-1]))
                    if dr == 0:
                        b.copy("act", Of[:, qs], pO)
                    else:
                        b.tt("dve", ot, pO, Of[:, qs], ALU.add)
                        b.act(osq, ot, AF.Square)
                        pss = PS[2 + q % 2]
                        b.mm(pss, self.ones_bf, osq)
                        b.ts("dve", rs1, pss, 1.0 / 128.0, EPS, ALU.mult, ALU.add)
                        b.act(rs2, rs1, AF.Sqrt)
                        b.recip(rs1, rs2)
                        b.tt("dve", ot, ot, rs1, ALU.mult)
                        b.stt(yT[:, qs], ot, self.cols[:, C_HNW + hd:C_HNW + hd + 1], sg[:, qs], ALU.mult, ALU.mult)
            self.b.s.dma("act", [(self.yT_d.ap[hd * 128:(hd + 1) * 128, :], yT.ap)], reads=[yT.tok],
                         writes=[self.yT_d.tok], partial=True)

    def fnet_phase(self):
        b = self.b
        PS = self.PS
        A = Arena(self.arena, R_PHASE)
        uT = A([128, 2, 2048], BF16, "uT")
        sgB = A([128, 2, 2048], BF16, "sgB")
        UW = A([128, 16, 2, 256], BF16, "UW")
        tabs = [[A([128, 8, 512], BF16, "tab%d_%d" % (s, w)) for w in range(2)] for s in range(2)]
        fwst = A([128, 2, 256], F32, "fwst")
        fwb = A([128, 2, 256], BF16, "fwb")
        W12 = A([128, 2, 2, 256], BF16, "W12")
        yo = [A([128, 512], BF16, "fyo%d" % i) for i in range(2)]
        slot_i = 0
        for g in range(8):
            for co in range(2):
                wb = self.load_w(self.w_in0[80 + 2 * g + co])
                self.proj(wb, lambda q, p, co=co: b.copy("act", uT[:, co, q * 512:(q + 1) * 512], p))
            for co in range(2):
                wb = self.load_w(self.w_in0[96 + 2 * g + co])
                self.proj(wb, lambda q, p, co=co: b.act(sgB[:, co, q * 512:(q + 1) * 512], p, AF.Silu))
            b.dma("sp", fwst, self.fw_d[g])
            b.copy("dve", fwb, fwst)
            for co in range(2):
                p = PS[2]
                for w in range(2):
                    for k in range(2):
                        b.mm(p[:, w * 256:(w + 1) * 256], self.cc[w][:, k, co * 128:(co + 1) * 128], fwb[:, k, :],
                             start=(k == 0), stop=(k == 1))
                b.copy("act", W12[:, co, 0, :], p[:, 0:256])
                b.ts("dve", W12[:, co, 1, :], p[:, 256:512], -1.0, None, ALU.mult)
            for st in range(16):
                p = PS[2 + st % 2]
                for co in range(2):
                    b.mm(p, uT[:, co, st * 128:(st + 1) * 128], W12[:, co, :, :].re("p w d -> p (w d)"),
                         start=(co == 0), stop=(co == 1))
                b.copy("act" if st % 2 else "dve", UW[:, st, :, :], p.re("p (w d) -> p w d", w=2))
            for tt in range(4):
                banks = (PS[2], PS[3]) if tt % 2 == 0 else (PS[4], PS[5])
                for half in range(2):
                    slot = tabs[slot_i % 2]
                    slot_i += 1
                    for w in range(2):
                        b.dma("sp", slot[w], self.tabb_d[w * 8 + tt * 2 + half])
                    for dblk in range(2):
                        p = banks[dblk]
                        for j in range(8):
                            st = half * 8 + j
                            for w in range(2):
                                b.mm(p, UW[:, st, w, dblk * 128:(dblk + 1) * 128], slot[w][:, j, :],
                                     start=(half == 0 and j == 0 and w == 0), stop=(half == 1 and j == 7 and w == 1))
                for dblk in range(2):
                    y = yo[dblk]
                    ch = 2 * g + dblk
                    b.stt(y, banks[dblk], self.cols[:, C_FB + ch:C_FB + ch + 1], sgB[:, dblk, tt * 512:(tt + 1) * 512],
                          ALU.add, ALU.mult)
                    r0 = 2048 + ch * 128
                    self.b.s.dma("act", [(self.yT_d.ap[r0:r0 + 128, tt * 512:(tt + 1) * 512], y.ap)], reads=[y.tok],
                                 writes=[self.yT_d.tok], partial=True)

    def outproj_phase(self, wout, xsrc, xdst, split):
        b = self.b
        PS = self.PS
        A = Arena(self.arena, R_BIG)
        wsl = [A([128, 32, 512], BF16, "wsl%d" % i) for i in range(2)]
        ysl = [A([128, 32, 512], BF16, "ysl%d" % i) for i in range(2)]
        wst = [A([128, 4, 512], F32, "owst%d" % i) for i in range(2)]
        xr = [A([128, 512], F32, "xr%d" % i) for i in range(2)]
        xo = [A([128, 512], F32, "xo%d" % i) for i in range(2)]
        if split:
            b.ts("dve", self.rstd, self.ssq, 1.0 / 2048.0, EPS, ALU.mult, ALU.add)
            b.act(self.rstd, self.rstd, AF.Sqrt)
            b.recip(self.rstd, self.rstd)
        yv = self.yT_d.re("(kc p) t -> p kc t", p=128)
        cnt = 0
        for dt_ in range(4):
            w = wsl[dt_ % 2]
            for pc in range(8):
                st = wst[pc % 2]
                b.dma("sp", st, wout[dt_][:, pc * 4:(pc + 1) * 4, :])
                b.copy("pool", w[:, pc * 4:(pc + 1) * 4, :], st)
            for tq in range(4):
                ys = ysl[(dt_ * 4 + tq) % 2]
                self.b.s.dma("sp", [(ys.ap[:, i * 8:(i + 1) * 8, :], yv.ap[:, i * 8:(i + 1) * 8, tq * 512:(tq + 1) * 512])
                                    for i in range(4)], reads=[self.yT_d.tok], writes=[ys.tok])
                for j in range(4):
                    tt = tq * 4 + j
                    ts_ = slice(j * 128, (j + 1) * 128)
                    pa = PS[cnt % 3]
                    pbk = PS[3 + cnt % 3]
                    x_r = xr[cnt % 2]
                    x_o = xo[cnt % 2]
                    cnt += 1
                    b.dma("sp", x_r, xsrc[tt * 128:(tt + 1) * 128, dt_ * 512:(dt_ + 1) * 512])
                    if not split:
                        for kc in range(32):
                            b.mm(pa, ys[:, kc, ts_], w[:, kc, :], start=(kc == 0), stop=(kc == 31))
                        b.tt("dve", x_o, pa, x_r, ALU.add)
                    else:
                        for kc in range(16):
                            b.mm(pa, ys[:, kc, ts_], w[:, kc, :], start=(kc == 0), stop=(kc == 15))
                        for kc in range(16, 32):
                            b.mm(pbk, ys[:, kc, ts_], w[:, kc, :], start=(kc == 16), stop=(kc == 31))
                        b.tt("dve", x_o, pa, x_r, ALU.add)
                        b.stt(x_o, pbk, self.rstd[:, tt:tt + 1], x_o, ALU.mult, ALU.add)
                    self.b.s.dma("act", [(xdst.ap[tt * 128:(tt + 1) * 128, dt_ * 512:(dt_ + 1) * 512], x_o.ap)],
                                 reads=[x_o.tok], writes=[xdst.tok], partial=True)

    def final_norm(self, xsrc, out):
        b = self.b
        A = Arena(self.arena, R_BIG)
        fnwb = A([128, 2048], F32, "fnwb")
        xt = [A([128, 2048], F32, "fxt%d" % i) for i in range(2)]
        xo = [A([128, 2048], F32, "fxo%d" % i) for i in range(2)]
        junk = A([128, 2048], BF16, "fjunk")
        sm = [[A([128, 1], F32) for _ in range(4)] for _ in range(2)]
        b.dma("sp", fnwb, self.fnw_d.bc([128, 2048]))
        for tt in range(16):
            x_t, x_o = xt[tt % 2], xo[tt % 2]
            s1, s2, s3, s4 = sm[tt % 2]
            b.dma("sp", x_t, xsrc[tt * 128:(tt + 1) * 128, :])
            b.act(junk, x_t, AF.Square, accum=s1)
            b.ts("dve", s2, s1, 1.0 / D, EPS, ALU.mult, ALU.add)
            b.act(s3, s2, AF.Sqrt)
            b.recip(s4, s3)
            b.stt(x_o, x_t, s4, fnwb, ALU.mult, ALU.mult)
            self.b.s.dma("act", [(out.ap[tt * 128:(tt + 1) * 128, :], x_o.ap)], reads=[x_o.tok],
                         writes=[out.tok], partial=True, is_output=True)

    def sconv_phase(self):
        b = self.b
        A = Arena(self.arena, R_PHASE)
        cin = A([128, 2048], F32, "cin")
        upad = A([128, 2050], F32, "upad")
        ya = A([128, 2048], F32, "ya")
        yb = A([128, 2048], F32, "yb")
        sgc = [A([128, 512], F32, "sgc%d" % i) for i in range(2)]
        yo = A([128, 2048], BF16, "syo")
        b.memset("dve", upad[:, 0:1], 0.0)
        b.memset("dve", upad[:, 2049:2050], 0.0)
        for cb in range(16):
            w0 = self.cols[:, C_SW + cb:C_SW + cb + 1]
            w1 = self.cols[:, C_SW + 16 + cb:C_SW + 16 + cb + 1]
            w2 = self.cols[:, C_SW + 32 + cb:C_SW + 32 + cb + 1]
            wb = self.load_w(self.w_in1[cb])
            self.proj(wb, lambda q, p: b.copy("act", cin[:, q * 512:(q + 1) * 512], p))
            wb = self.load_w(self.w_in1[32 + cb])
            self.proj(wb, lambda q, p: b.tt("dve", upad[:, 1 + q * 512:1 + (q + 1) * 512], p,
                                            cin[:, q * 512:(q + 1) * 512], ALU.mult))
            b.ts("dve", ya, upad[:, 1:2049], w1, None, ALU.mult)
            b.stt(yb, upad[:, 0:2048], w0, ya, ALU.mult, ALU.add)
            b.stt(ya, upad[:, 2:2050], w2, yb, ALU.mult, ALU.add)
            wb = self.load_w(self.w_in1[16 + cb])
            self.proj(wb, lambda q, p: b.tt("dve", yb[:, q * 512:(q + 1) * 512], p, ya[:, q * 512:(q + 1) * 512],
                                            ALU.mult))
            wb = self.load_w(self.w_in1[48 + cb])

            def cons(q, p):
                s = sgc[q % 2]
                b.act(s, p, AF.Silu)
                b.tt("dve", yo[:, q * 512:(q + 1) * 512], yb[:, q * 512:(q + 1) * 512], s, ALU.mult)

            self.proj(wb, cons)
            self.b.s.dma("act", [(self.yT_d.ap[cb * 128:(cb + 1) * 128, :], yo.ap)], reads=[yo.tok],
                         writes=[self.yT_d.tok], partial=True)

    def conv_block(self, wsrc, wcol, out, upad, yq):
        b = self.b
        w0 = self.cols[:, C_CW + wcol:C_CW + wcol + 1]
        w1 = self.cols[:, C_CW + 24 + wcol:C_CW + 24 + wcol + 1]
        w2 = self.cols[:, C_CW + 48 + wcol:C_CW + 48 + wcol + 1]
        cbias = self.cols[:, C_CB + wcol:C_CB + wcol + 1]
        wb = self.load_w(wsrc)
        self.proj(wb, lambda q, p: b.copy("act", upad[:, 1 + q * 512:1 + (q + 1) * 512], p))
        for q in range(4):
            ya, yb = yq[q % 2]
            b.ts("dve", ya, upad[:, 1 + q * 512:1 + (q + 1) * 512], w1, None, ALU.mult)
            b.stt(yb, upad[:, q * 512:(q + 1) * 512], w0, ya, ALU.mult, ALU.add)
            b.stt(ya, upad[:, 2 + q * 512:2 + (q + 1) * 512], w2, yb, ALU.mult, ALU.add)
            b.act(out[:, q * 512:(q + 1) * 512], ya, AF.Silu, bias=cbias)

    def ssd_phase(self):
        b = self.b
        PS = self.PS
        A = Arena(self.arena, R_PHASE)
        dt = A([128, 16, 64], F32, "dt")
        dta = A([128, 16, 64], F32, "dta")
        A0 = Arena(self.arena, (A.p, ARENA_BYTES))
        wdst = A0([128, 16, 64], F32, "wdst")
        wdtb = A0([128, 16, 64], BF16, "wdtb")
        xsT = A([128, 2, 2048], BF16, "xsT")
        xstok = A([128, 16, 256], BF16, "xstok")
        szT = A([128, 2, 2048], BF16, "szT")
        yacc = A([128, 16, 256], F32, "yacc")
        BT = A([128, 2048], BF16, "BT")
        CT = A([128, 2048], BF16, "CT")
        Btok = A([128, 16, 128], BF16, "Btok")
        upad = A([128, 2050], F32, "upad")
        yq = [(A([128, 512], F32, "cya%d" % i), A([128, 512], F32, "cyb%d" % i)) for i in range(2)]
        class CS:
            pass
        sets = []
        for i in range(2):
            c = CS()
            c.X = A([128, 512], F32, "X%d" % i)
            c.Dm = A([128, 512], BF16, "Dm%d" % i)
            c.Wm = A([128, 512], BF16, "Wm%d" % i)
            c.cb = A([128, 128], F32, "cb%d" % i)
            c.xdt = A([128, 256], BF16, "xdt%d" % i)
            c.xdtw = A([128, 256], BF16, "xdtw%d" % i)
            c.tz = A([128, 256], F32, "tz%d" % i)
            c.ac, c.ei, c.wj, c.et, c.dif = [A([128, 4], F32, n + str(i)) for n in ("ac", "ei", "wj", "et", "dif")]
            sets.append(c)
        S32 = A([128, 256], F32, "S32")
        Sbf = A([128, 256], BF16, "Sbf")
        ysum = A([128, 256], F32, "ysum")
        y1 = sets[0].X
        y2 = sets[1].X
        ysq = sets[0].Dm
        yo = [sets[1].Dm, sets[1].Wm]
        pEs, pY = (PS[2], PS[5]), PS[3]
        pS = PS[4][:, 0:256]
        pQ = PS[4][:, 256:260]
        pC = PS[0][:, 0:128]
        pSm = PS[1][:, 0:8]
        pT = PS[5]
        b.memset("dve", upad[:, 0:1], 0.0)
        b.memset("dve", upad[:, 2049:2050], 0.0)
        b.dma("sp", wdst, self.w_dt)
        b.copy("pool", wdtb, wdst)
        for half in range(2):
            p = PS[2 + half]
            for j in range(8):
                tt = half * 8 + j
                for kc in range(KC):
                    b.mm(p[:, j * 64:(j + 1) * 64], self.hT[:, kc, tt * 128:(tt + 1) * 128], wdtb[:, kc, :],
                         start=(kc == 0), stop=(kc == KC - 1))
            b.tt("dve", dt[:, half * 8:(half + 1) * 8, :], p.re("p (j h) -> p j h", h=64),
                 self.dtb_b.v(lambda a: a.unsqueeze(1)).bc([128, 8, 64]), ALU.add)
        b.act(dta, dt, AF.Exp)
        b.act(dt, dta, AF.Ln, bias=1.0)
        b.tt("dve", dta, dt, self.a_b.v(lambda a: a.unsqueeze(1)).bc([128, 16, 64]), ALU.mult)
        b.s.barrier()
        h4 = lambda t, d: t.re("p (h d) -> p h d", d=d)
        for u in range(8):
            g = u // 2
            for blk in range(2):
                self.conv_block(self.w_in1[80 + 2 * u + blk], 2 * u + blk, xsT[:, blk, :], upad, yq)
            for blk in range(2):
                wb = self.load_w(self.w_in1[64 + 2 * u + blk])
                self.proj(wb, lambda q, p, blk=blk: b.act(szT[:, blk, q * 512:(q + 1) * 512], p, AF.Silu))
            if u % 2 == 0:
                self.conv_block(self.w_in1[96 + g], 16 + g, BT, upad, yq)
                self.conv_block(self.w_in1[100 + g], 20 + g, CT, upad, yq)
                for q in range(4):
                    pb = self.PB[q % 2]
                    for j in range(4):
                        tt = 4 * q + j
                        b.tr(pb[:, j * 128:(j + 1) * 128], BT[:, tt * 128:(tt + 1) * 128], self.ident)
                    b.copy("act" if q % 2 else "dve", Btok[:, 4 * q:4 * q + 4, :], pb.re("p (j k) -> p j k", k=128))
            for tt in range(16):
                pb = self.PB[tt % 2]
                for blk in range(2):
                    b.tr(pb[:, blk * 128:(blk + 1) * 128], xsT[:, blk, tt * 128:(tt + 1) * 128], self.ident)
                b.copy("act" if tt % 2 else "dve", xstok[:, tt, :], pb[:, 0:256])
            for dr in range(2):
                order = list(range(16)) if dr == 0 else list(range(15, -1, -1))
                c0 = dr * 32 + 4 * u

                def front(idx, dr=dr, order=order, c0=c0):
                    ch = order[idx]
                    c = sets[idx % 2]
                    pE = pEs[idx % 2]
                    chs = slice(ch * 128, (ch + 1) * 128)
                    dta_c = dta[:, ch, c0:c0 + 4]
                    dt_c = dt[:, ch, c0:c0 + 4]
                    b.mm(pSm[:, 0:4], self.tri[dr], dta_c)
                    b.mm(pSm[:, 4:8], self.ones_f, dta_c)
                    b.copy("dve", c.ac, pSm[:, 0:4])
                    b.act(c.ei, c.ac, AF.Exp)
                    b.tt("dve", c.dif, pSm[:, 4:8], c.ac, ALU.subtract)
                    b.act(c.wj, c.dif, AF.Exp)
                    b.act(c.et, pSm[:, 4:8], AF.Exp)
                    b.tt("dve", h4(c.X, 128), self.tri[dr].v(lambda a: a.unsqueeze(1)).bc([128, 4, 128]),
                         dta_c.v(lambda a: a.unsqueeze(2)).bc([128, 4, 128]), ALU.mult)
                    b.mm(pE, self.sgt[dr], c.X, start=True, stop=False)
                    b.mm(pE, self.ident, self.neg[dr], start=False, stop=True)
                    b.act(c.Dm, pE, AF.Exp)
                    b.mm(pC, BT[:, chs], CT[:, chs])
                    b.copy("act", c.cb, pC)
                    b.tt("dve", h4(c.Wm, 128), h4(c.Dm, 128), c.cb.v(lambda a: a.unsqueeze(1)).bc([128, 4, 128]),
                         ALU.mult)
                    b.tt("dve", h4(c.xdt, 64), h4(xstok[:, ch, :], 64),
                         dt_c.v(lambda a: a.unsqueeze(2)).bc([128, 4, 64]), ALU.mult)
                    b.tt("dve", h4(c.xdtw, 64), h4(c.xdt, 64), c.wj.v(lambda a: a.unsqueeze(2)).bc([128, 4, 64]),
                         ALU.mult)

                def back(idx, dr=dr, order=order):
                    ch = order[idx]
                    c = sets[idx % 2]
                    chs = slice(ch * 128, (ch + 1) * 128)
                    for hl in range(4):
                        b.mm(pY[:, hl * 64:(hl + 1) * 64], c.Wm[:, hl * 128:(hl + 1) * 128],
                             c.xdt[:, hl * 64:(hl + 1) * 64])
                    ydst = yacc[:, ch, :]
                    if idx > 0:
                        b.mm(pY[:, 256:512], CT[:, chs], Sbf)
                        b.tt("dve", h4(c.tz, 64), h4(pY[:, 256:512], 64),
                             c.ei.v(lambda a: a.unsqueeze(2)).bc([128, 4, 64]), ALU.mult)
                        if dr == 0:
                            b.tt("dve", ydst, pY[:, 0:256], c.tz, ALU.add)
                        else:
                            b.tt("dve", ysum, pY[:, 0:256], c.tz, ALU.add)
                            b.tt("dve", ydst, ydst, ysum, ALU.add)
                    else:
                        if dr == 0:
                            b.copy("dve", ydst, pY[:, 0:256])
                        else:
                            b.tt("dve", ydst, pY[:, 0:256], ydst, ALU.add)
                    if idx < 15:
                        b.mm(pS, Btok[:, ch, :], c.xdtw)
                        if idx == 0:
                            b.copy("dve", S32, pS)
                        else:
                            b.tt("dve", h4(S32, 64), h4(S32, 64), c.et.v(lambda a: a.unsqueeze(2)).bc([128, 4, 64]),
                                 ALU.mult)
                            b.tt("dve", S32, pS, S32, ALU.add)
                        b.copy("act", Sbf, S32)

                front(0)
                for idx in range(16):
                    if idx + 1 < 16:
                        front(idx + 1)
                    back(idx)
            for blk in range(2):
                chb = 2 * u + blk
                for q in range(4):
                    qs = slice(q * 512, (q + 1) * 512)
                    for j in range(4):
                        tt = 4 * q + j
                        b.tr(pT[:, j * 128:(j + 1) * 128], yacc[:, tt, blk * 128:(blk + 1) * 128], self.identf)
                    b.stt(y1, xsT[:, blk, qs], self.cols[:, C_DD + chb:C_DD + chb + 1], pT, ALU.mult, ALU.add)
                    b.tt("dve", y2, y1, szT[:, blk, qs], ALU.mult)
                    b.act(ysq, y2, AF.Square)
                    for j in range(4):
                        b.mm(pQ[:, j:j + 1], ysq[:, j * 128:(j + 1) * 128], self.ones_bf[:, 0:1])
                    b.tt("dve", self.ssq[:, 4 * q:4 * q + 4], pQ, self.ssq[:, 4 * q:4 * q + 4], ALU.add)
                    y = yo[q % 2]
                    b.ts("dve", y, y2, self.cols[:, C_DNW + chb:C_DNW + chb + 1], None, ALU.mult)
                    r0 = 2048 + chb * 128
                    self.b.s.dma("act", [(self.yT_d.ap[r0:r0 + 128, qs], y.ap)], reads=[y.tok],
                                 writes=[self.yT_d.tok], partial=True)


def _colize(v):
    v = np.asarray(v, np.float32)
    v = v.reshape(-1, v.shape[-1] // 128, 128)
    return np.ascontiguousarray(v.transpose(2, 0, 1).reshape(128, -1))


_CONST_CACHE = {}


def _constants():
    if _CONST_CACHE:
        return _CONST_CACHE
    j = np.arange(128)[:, None]
    i = np.arange(128)[None, :]
    same = (j // 32) == (i // 32)
    cst = np.zeros((128, NCST), np.float32)
    cst[:, K_ID:K_ID + 128] = np.eye(128)
    cst[:, K_MF:K_MF + 128] = (same & (j <= i))
    cst[:, K_MB:K_MB + 128] = (same & (j >= i))
    cst[:, K_TF:K_TF + 128] = (j <= i)
    cst[:, K_TB:K_TB + 128] = (j >= i)
    cst[:, K_GF:K_GF + 128] = (j > i)
    cst[:, K_GB:K_GB + 128] = (j < i)
    cst[:, K_NF:K_NF + 128] = np.where(j <= i, 0.0, -30000.0)
    cst[:, K_NB:K_NB + 128] = np.where(j >= i, 0.0, -30000.0)
    rst = np.ones((128, 512), np.float32)
    rst[:, ::32] = 0.0
    cst[:, K_RST:K_RST + 512] = rst
    c = np.arange(256)
    m = (c[:, None] * c[None, :]) % 256
    ang = 2.0 * np.pi * m.astype(np.float64) / 256.0
    scale = 1.0 / np.sqrt(2048.0 * 256.0)
    for w, fn in enumerate((np.cos, np.sin)):
        tab = (fn(ang) * scale).astype(np.float32)
        cst[:, K_CC + w * 512:K_CC + (w + 1) * 512] = tab.reshape(2, 128, 256).transpose(1, 0, 2).reshape(128, 512)
    s = np.arange(2048, dtype=np.int64)
    m = (s[:, None] * s[None, :]) % 2048
    ang = 2.0 * np.pi * m.astype(np.float64) / 2048.0
    tabf = np.zeros((2, 4, 2, 128, 8, 512), np.float32)
    for w, fn in enumerate((np.cos, np.sin)):
        tab = fn(ang).astype(np.float32)
        tabf[w] = tab.reshape(2, 8, 128, 4, 512).transpose(3, 0, 2, 1, 4)
    _CONST_CACHE["cst"] = cst
    _CONST_CACHE["tabf"] = np.ascontiguousarray(tabf.reshape(16, 128, 8, 512))
    return _CONST_CACHE


def prepare_inputs(x, norm_w, final_norm_w, ev_w_in, ev_w_out, hgrn_lb_logits, hgrn_norm_w,
                   fnet_w, fnet_b, od_w_in, od_w_out, sconv_w, ssd_conv_w, ssd_conv_b,
                   ssd_dt_bias, ssd_a_log, ssd_d, ssd_norm_w):
    f = lambda a: np.asarray(a, np.float32)
    cs = _constants()
    w0 = f(ev_w_in)[0]
    w_in0 = np.ascontiguousarray(w0.reshape(16, 128, 112, 128).transpose(2, 1, 0, 3))
    w1 = f(od_w_in)[0]
    w_in1 = np.ascontiguousarray(w1[:, :13312].reshape(16, 128, 104, 128).transpose(2, 1, 0, 3))
    w_dt = np.ascontiguousarray(w1[:, 13312:13376].reshape(16, 128, 64).transpose(1, 0, 2))
    wo = lambda w: np.ascontiguousarray(f(w)[0].reshape(32, 128, 4, 512).transpose(2, 1, 0, 3))
    cols = np.concatenate([
        _colize(f(norm_w)[0:1]), _colize(f(norm_w)[1:2]), _colize(f(hgrn_lb_logits)),
        _colize(f(hgrn_norm_w)[0:1]), _colize(f(fnet_b)[0:1]), _colize(f(sconv_w)[0]),
        _colize(f(ssd_conv_w)[0]), _colize(f(ssd_conv_b)[0:1]),
        _colize(np.repeat(f(ssd_d)[0], 64)[None, :]), _colize(f(ssd_norm_w)[0:1])], axis=1)
    assert cols.shape == (128, NCOLS), cols.shape
    shared = {
        "w_in0": w_in0, "w_out0": wo(ev_w_out), "w_in1": w_in1, "w_dt": w_dt, "w_out1": wo(od_w_out),
        "cols": np.ascontiguousarray(cols), "cst": cs["cst"], "fnw": f(final_norm_w).reshape(1, D),
        "dtb": f(ssd_dt_bias).reshape(1, 64), "alog": f(ssd_a_log).reshape(1, 64),
        "fw": np.ascontiguousarray(f(fnet_w)[0].reshape(8, 2, 128, 256).transpose(0, 2, 1, 3)),
        "tabf": cs["tabf"],
    }
    xs = f(x)
    return [dict(shared, x=np.ascontiguousarray(xs[i])) for i in range(xs.shape[0])]


def build_program(debug=False, stop_after=None):
    nc = bass.Bass("TRN2", target_bir_lowering=False)
    Builder(nc, debug=debug, stop_after=stop_after).build()
    return nc


def kernel(**inputs):
    in_maps = prepare_inputs(**inputs)
    nc = build_program()
    res = run_bass_kernel_spmd(nc, in_maps, core_ids=list(range(8)))
    return np.stack([np.asarray(r["out"], np.float32) for r in res.results], axis=0)
```

```python
from contextlib import ExitStack
import numpy as np
import concourse.bass as bass
import concourse.mybir as mybir
from concourse.bass_utils import run_bass_kernel_spmd

F32 = mybir.dt.float32
BF16 = mybir.dt.bfloat16
AF = mybir.ActivationFunctionType
ALU = mybir.AluOpType
AX = mybir.AxisListType

L = 2048
D = 2048
NT = 16
KC = 16
EPS = 1e-6

ENGS = ("pe", "act", "dve", "pool", "sp")


class Tok:
    __slots__ = ("name", "w", "r", "war")

    def __init__(self, name=""):
        self.name = name
        self.w = {}
        self.r = {}
        self.war = {}


def _merge(dst, src):
    for k, v in src.items():
        if dst.get(k, -1) < v:
            dst[k] = v


class T:
    __slots__ = ("ap", "tok")

    def __init__(self, ap, tok=None, name=""):
        self.ap = ap
        self.tok = tok if tok is not None else Tok(name)

    def __getitem__(self, idx):
        return T(self.ap[idx], self.tok)

    def v(self, fn):
        return T(fn(self.ap), self.tok)

    def bc(self, shape):
        return T(self.ap.broadcast_to(list(shape)), self.tok)

    def re(self, s, **kw):
        return T(self.ap.rearrange(s, **kw), self.tok)


def _ap(x):
    return x.ap if isinstance(x, T) else x


def _toks(*xs):
    return [x.tok for x in xs if isinstance(x, T)]


class Op:
    __slots__ = ("eng", "fn", "deps", "seq", "signal", "sigval", "dma", "dsem", "dprev", "dval")


class Sched:
    def __init__(self, nc, n_dma_sems=48):
        self.nc = nc
        self.eng_ops = {e: [] for e in ENGS}
        self.n_dma_sems = n_dma_sems
        self.dma_val = [0] * n_dma_sems
        self.dma_rr = 0
        self.out_deps = {}

    def _deps(self, reads, writes, partial):
        deps = {}
        for t in reads:
            _merge(deps, t.w)
        for t in writes:
            _merge(deps, t.r)
            _merge(deps, t.war)
            if not partial:
                _merge(deps, t.w)
        return deps

    def _update(self, key, val, reads, writes, partial):
        me = {key: val}
        for t in reads:
            _merge(t.r, me)
        for t in writes:
            if t.r:
                t.war = t.r
                t.r = {}
                t.w = dict(me)
            elif partial:
                _merge(t.w, me)
            else:
                t.w = dict(me)

    def op(self, eng, fn, reads=(), writes=(), partial=False):
        o = Op()
        o.eng = eng
        o.fn = fn
        o.dma = None
        o.signal = False
        o.deps = self._deps(reads, writes, partial)
        lst = self.eng_ops[eng]
        lst.append(o)
        o.seq = len(lst)
        self._update(eng, o.seq, reads, writes, partial)
        return o

    def dma(self, queue, parts, reads=(), writes=(), partial=False, is_output=False):
        o = Op()
        o.eng = queue
        o.fn = None
        o.dma = parts
        o.signal = False
        o.deps = self._deps(reads, writes, partial)
        s = self.dma_rr
        self.dma_rr = (self.dma_rr + 1) % self.n_dma_sems
        o.dsem = s
        o.dprev = self.dma_val[s]
        self.dma_val[s] += 16 * len(parts)
        o.dval = self.dma_val[s]
        lst = self.eng_ops[queue]
        lst.append(o)
        o.seq = len(lst)
        self._update(("d", s), o.dval, reads, writes, partial)
        if is_output:
            _merge(self.out_deps, {("d", s): o.dval})
        return o

    def barrier(self):
        deps = {e: len(self.eng_ops[e]) for e in ENGS if len(self.eng_ops[e]) > 0 and e != "sp"}
        for s in range(self.n_dma_sems):
            if self.dma_val[s] > 0:
                deps[("d", s)] = self.dma_val[s]
        for e in ENGS:
            o = Op()
            o.eng = e
            o.fn = None
            o.dma = None
            o.signal = False
            o.deps = {k: v for k, v in deps.items() if k != e}
            lst = self.eng_ops[e]
            lst.append(o)
            o.seq = len(lst)

    def emit(self, es):
        nc = self.nc
        for e in ENGS:
            for o in self.eng_ops[e]:
                nd = {}
                for k, v in o.deps.items():
                    if isinstance(k, str):
                        if k == "pe" and e == "pe":
                            continue
                        while v > 0 and self.eng_ops[k][v - 1].fn is None and self.eng_ops[k][v - 1].dma is None:
                            v -= 1
                        if v == 0:
                            continue
                        p = self.eng_ops[k][v - 1]
                        if p.dma is not None:
                            _merge(nd, {("d", p.dsem): p.dval})
                        else:
                            p.signal = True
                            _merge(nd, {k: v})
                    else:
                        _merge(nd, {k: v})
                o.deps = nd
        for e in ENGS:
            c = 0
            for o in self.eng_ops[e]:
                if o.signal:
                    c += 1
                    o.sigval = c
        esem = {e: es.enter_context(nc.semaphore("s_" + e)) for e in ENGS}
        dsems = [es.enter_context(nc.semaphore("d%d" % i)) for i in range(self.n_dma_sems)]
        block = es.enter_context(nc.Block())
        sched = self

        def run(en, e):
            waited = {}
            for o in sched.eng_ops[en]:
                for k, v in o.deps.items():
                    if isinstance(k, str):
                        kk = k
                        tgt = sched.eng_ops[k][v - 1].sigval
                        sem = esem[k]
                    else:
                        kk = k
                        tgt = v
                        sem = dsems[k[1]]
                    if waited.get(kk, 0) >= tgt:
                        continue
                    waited[kk] = tgt
                    e.wait_ge(sem, tgt)
                if o.dma is not None:
                    kk = ("d", o.dsem)
                    if o.dprev > 0 and waited.get(kk, 0) < o.dprev:
                        e.wait_ge(dsems[o.dsem], o.dprev)
                        waited[kk] = o.dprev
                    for (oa, ia) in o.dma:
                        e.dma_start(out=oa, in_=ia).then_inc(dsems[o.dsem], 16)
                elif o.fn is not None:
                    inst = o.fn(e)
                    if o.signal:
                        inst.then_inc(esem[en], 1)
            if en == "sp":
                for k, v in sched.out_deps.items():
                    if waited.get(k, 0) < v:
                        e.wait_ge(dsems[k[1]], v)

        @block.sync
        def _(e):
            run("sp", e)

        @block.scalar
        def _(e):
            run("act", e)

        @block.vector
        def _(e):
            run("dve", e)

        @block.gpsimd
        def _(e):
            run("pool", e)

        @block.tensor
        def _(e):
            run("pe", e)


class B:
    def __init__(self, nc):
        self.nc = nc
        self.s = Sched(nc)

    def mm(self, out, lhsT, rhs, start=True, stop=True, tile_position=None):
        o, a, b = _ap(out), _ap(lhsT), _ap(rhs)
        kw = {}
        if tile_position is not None:
            kw["tile_position"] = tile_position
        self.s.op("pe", lambda e: e.matmul(o, lhsT=a, rhs=b, start=start, stop=stop, **kw),
                  reads=_toks(lhsT, rhs), writes=_toks(out), partial=True)

    def tr(self, out, in_, ident):
        o, a, i = _ap(out), _ap(in_), _ap(ident)
        self.s.op("pe", lambda e: e.transpose(o, a, i), reads=_toks(in_, ident), writes=_toks(out), partial=True)

    def act(self, out, in_, func, bias=None, scale=None, accum=None, extra_w=()):
        o, a = _ap(out), _ap(in_)
        kw = {}
        if bias is not None:
            kw["bias"] = _ap(bias)
        if scale is not None:
            kw["scale"] = _ap(scale)
        if accum is not None:
            kw["accum_out"] = _ap(accum)
        self.s.op("act", lambda e: e.activation(o, a, func, **kw),
                  reads=_toks(in_, bias, scale), writes=_toks(out, accum) + list(extra_w))

    def tt(self, eng, out, in0, in1, op):
        o, a, b = _ap(out), _ap(in0), _ap(in1)
        self.s.op(eng, lambda e: e.tensor_tensor(o, a, b, op), reads=_toks(in0, in1), writes=_toks(out))

    def ts(self, eng, out, in0, s1, s2=None, op0=ALU.mult, op1=None):
        o, a = _ap(out), _ap(in0)
        x1, x2 = _ap(s1), _ap(s2)
        if op1 is None:
            self.s.op(eng, lambda e: e.tensor_scalar(o, a, x1, None, op0),
                      reads=_toks(in0, s1), writes=_toks(out))
        else:
            self.s.op(eng, lambda e: e.tensor_scalar(o, a, x1, x2, op0, op1),
                      reads=_toks(in0, s1, s2), writes=_toks(out))

    def stt(self, out, in0, scalar, in1, op0, op1):
        o, a, c, b = _ap(out), _ap(in0), _ap(scalar), _ap(in1)
        self.s.op("dve", lambda e: e.scalar_tensor_tensor(o, a, c, b, op0, op1),
                  reads=_toks(in0, scalar, in1), writes=_toks(out))

    def scan(self, out, d0, d1, initial, op0=ALU.mult, op1=ALU.add):
        o, a, b, i = _ap(out), _ap(d0), _ap(d1), _ap(initial)
        self.s.op("dve", lambda e: e.tensor_tensor_scan(o, a, b, i, op0, op1),
                  reads=_toks(d0, d1, initial), writes=_toks(out))

    def copy(self, eng, out, in_):
        o, a = _ap(out), _ap(in_)
        if eng == "act":
            self.s.op("act", lambda e: e.copy(o, a), reads=_toks(in_), writes=_toks(out))
        else:
            self.s.op(eng, lambda e: e.tensor_copy(o, a), reads=_toks(in_), writes=_toks(out))

    def recip(self, out, in_):
        o, a = _ap(out), _ap(in_)
        self.s.op("dve", lambda e: e.reciprocal(o, a), reads=_toks(in_), writes=_toks(out))

    def memset(self, eng, out, val):
        o = _ap(out)
        self.s.op(eng, lambda e: e.memset(o, val), writes=_toks(out))

    def dma(self, queue, out, in_, is_output=False):
        self.s.dma(queue, [(_ap(out), _ap(in_))], reads=_toks(in_), writes=_toks(out), is_output=is_output)

    def dmas(self, queue, parts, is_output=False):
        self.s.dma(queue, [(_ap(o), _ap(i)) for o, i in parts],
                   reads=_toks(*[i for _, i in parts]), writes=_toks(*[o for o, _ in parts]),
                   is_output=is_output)


ARENA_BYTES = 206 * 1024
R_CONST = (0, 13312)
R_WEIGHT = (13312, 50176)
R_HT = (50176, 115712)
R_PHASE = (115712, ARENA_BYTES)
R_BIG = (13312, ARENA_BYTES)

C_NW0, C_NW1, C_LB, C_HNW, C_FB, C_SW, C_CW, C_CB, C_DD, C_DNW = 0, 16, 32, 80, 96, 112, 160, 232, 256, 272
NCOLS = 288
K_ID, K_MF, K_MB, K_TF, K_TB, K_GF, K_GB, K_NF, K_NB, K_RST, K_CC = 0, 128, 256, 384, 512, 640, 768, 896, 1024, 1152, 1664
NCST = 1664 + 1024


def _esize(dt):
    return 4 if dt == F32 else 2


class Arena:
    def __init__(self, arena_ap, region):
        self.a = arena_ap
        self.lo, self.hi = region
        self.p = self.lo

    def __call__(self, shape, dt=F32, name=""):
        n = 1
        for s in shape[1:]:
            n *= s
        nb = n * _esize(dt)
        nb_al = (nb + 31) // 32 * 32
        assert self.p + nb_al <= self.hi, "arena overflow %s: need %d have %d" % (name, nb_al, self.hi - self.p)
        v = self.a[0:shape[0], self.p:self.p + nb].bitcast(dt)
        self.p += nb_al
        if len(shape) == 3:
            v = v.rearrange("p (a b) -> p a b", a=shape[1])
        elif len(shape) == 4:
            v = v.rearrange("p (a b c) -> p a b c", a=shape[1], b=shape[2])
        return T(v, name=name)


class Builder:
    def __init__(self, nc, debug=False, stop_after=None):
        self.nc = nc
        self.b = B(nc)
        self.debug = debug
        self.stop_after = stop_after
        self.wi = 0
        self.pi = 0

    def din(self, name, shape, dt=F32):
        return T(self.nc.dram_tensor(name, list(shape), dt, kind="ExternalInput").ap(), name=name)

    def dscr(self, name, shape, dt=F32, out=False):
        kind = "ExternalOutput" if (out or self.debug) else "Internal"
        return T(self.nc.dram_tensor(name, list(shape), dt, kind=kind).ap(), name=name)

    def build(self):
        nc, b = self.nc, self.b
        self.x_d = self.din("x", [L, D])
        self.w_in0 = self.din("w_in0", [112, 128, 16, 128])
        self.w_out0 = self.din("w_out0", [4, 128, 32, 512])
        self.w_in1 = self.din("w_in1", [104, 128, 16, 128])
        self.w_dt = self.din("w_dt", [128, 16, 64])
        self.w_out1 = self.din("w_out1", [4, 128, 32, 512])
        self.cols_d = self.din("cols", [128, NCOLS])
        self.cst_d = self.din("cst", [128, NCST])
        self.fnw_d = self.din("fnw", [1, D])
        self.dtb_d = self.din("dtb", [1, 64])
        self.alog_d = self.din("alog", [1, 64])
        self.fw_d = self.din("fw", [8, 128, 2, 256])
        self.tabf_d = self.din("tabf", [16, 128, 8, 512])
        self.tabb_d = self.dscr("tabb", [16, 128, 8, 512], BF16)
        self.yT_d = self.dscr("yT", [4096, L], BF16)
        self.x1_d = self.dscr("x1", [L, D])
        self.x2_d = self.dscr("x2", [L, D])
        self.out_d = T(nc.dram_tensor("out", [L, D], F32, kind="ExternalOutput").ap(), name="out")
        self.arena = nc.alloc_sbuf_tensor("arena", [128, ARENA_BYTES], mybir.dt.uint8).ap()
        self.PS = [T(nc.alloc_psum_tensor("ps%d" % i, [128, 512], F32).ap(), name="ps%d" % i) for i in range(6)]
        self.PB = [T(nc.alloc_psum_tensor("pb%d" % i, [128, 1024], BF16).ap()[:, 0:512], name="pb%d" % i)
                   for i in range(2)]
        C = Arena(self.arena, R_CONST)
        self.cols = C([128, NCOLS], F32, "cols")
        self.ident = C([128, 128], BF16, "ident")
        self.identf = C([128, 128], F32, "identf")
        self.ones_bf = C([128, 128], BF16, "ones_bf")
        self.ones_f = C([128, 128], F32, "ones_f")
        self.rst = C([128, 512], F32, "rst")
        self.maskF = C([128, 128], F32, "maskF")
        self.maskB = C([128, 128], F32, "maskB")
        self.tri = [C([128, 128], F32, "triF"), C([128, 128], F32, "triB")]
        self.sgt = [C([128, 128], F32, "sgtF"), C([128, 128], F32, "sgtB")]
        self.neg = [C([128, 512], BF16, "negF"), C([128, 512], BF16, "negB")]
        self.cc = [C([128, 2, 256], BF16, "ccC"), C([128, 2, 256], BF16, "ccS")]
        self.lb3 = C([128, 3, 16], F32, "lb3")
        self.dtb_b = C([128, 64], F32, "dtb_b")
        self.a_b = C([128, 64], F32, "a_b")
        self.epsc = C([128, 1], F32, "epsc")
        self.ssq = C([128, 16], F32, "ssq")
        self.rstd = C([128, 16], F32, "rstd")
        W = Arena(self.arena, R_WEIGHT)
        self.wstage = [W([128, 16, 128], F32, "wst%d" % i) for i in range(2)]
        self.wring = [W([128, 16, 128], BF16, "wr%d" % i) for i in range(5)]
        H = Arena(self.arena, R_HT)
        self.hT = H([128, 16, 2048], BF16, "hT")

        self.setup_phase()
        b.s.barrier()
        if self.stop_after == "setup":
            return self.finish()
        self.norm_phase(self.x_d, C_NW0)
        b.s.barrier()
        if self.stop_after == "norm0":
            return self.finish()
        self.hgrn_phase()
        b.s.barrier()
        if self.stop_after == "hgrn":
            return self.finish()
        self.fnet_phase()
        b.s.barrier()
        if self.stop_after == "fnet":
            return self.finish()
        self.outproj_phase(self.w_out0, self.x_d, self.x1_d, split=False)
        b.s.barrier()
        if self.stop_after == "out0":
            return self.finish()
        self.norm_phase(self.x1_d, C_NW1)
        b.s.barrier()
        self.sconv_phase()
        b.s.barrier()
        if self.stop_after == "sconv":
            return self.finish()
        self.ssd_phase()
        b.s.barrier()
        if self.stop_after == "ssd":
            return self.finish()
        self.outproj_phase(self.w_out1, self.x1_d, self.x2_d, split=True)
        b.s.barrier()
        self.final_norm(self.x2_d, self.out_d)
        return self.finish()

    def finish(self):
        if self.debug and self.stop_after in ("norm0", "hgrn", "fnet", "sconv", "ssd"):
            self.b.s.barrier()
            hd = T(self.nc.dram_tensor("dbg_hT", [128, 16, 2048], BF16, kind="ExternalOutput").ap())
            self.b.dma("sp", hd, self.hT, is_output=True)
        if self.stop_after is not None:
            A = Arena(self.arena, R_PHASE)
            t = A([128, 512], F32, "fin")
            self.b.s.barrier()
            self.b.memset("dve", t, 0.0)
            self.b.dma("sp", self.out_d[0:128, 0:512], t, is_output=True)
        self._es = ExitStack()
        self.b.s.emit(self._es)
        self._es.close()
        return self.nc

    def load_w(self, src):
        b = self.b
        st = self.wstage[self.wi % 2]
        wb = self.wring[self.wi % len(self.wring)]
        self.wi += 1
        b.dma("sp", st, src)
        b.copy("pool", wb, st)
        return wb

    def proj(self, wb, consumer):
        b = self.b
        for q in range(4):
            p = self.PS[self.pi % 2]
            self.pi += 1
            for kc in range(KC):
                b.mm(p, wb[:, kc, :], self.hT[:, kc, q * 512:(q + 1) * 512], start=(kc == 0), stop=(kc == KC - 1))
            consumer(q, p)

    def setup_phase(self):
        b = self.b
        A = Arena(self.arena, R_PHASE)
        cst = A([128, NCST], F32, "cst")
        b.dma("sp", self.cols, self.cols_d)
        b.dma("sp", cst, self.cst_d)
        b.dma("sp", self.dtb_b, self.dtb_d.bc([128, 64]))
        al = A([128, 64], F32, "al")
        b.dma("sp", al, self.alog_d.bc([128, 64]))
        b.copy("dve", self.ident, cst[:, K_ID:K_ID + 128])
        b.copy("dve", self.identf, cst[:, K_ID:K_ID + 128])
        b.memset("dve", self.ones_bf, 1.0)
        b.memset("dve", self.ones_f, 1.0)
        b.copy("dve", self.rst, cst[:, K_RST:K_RST + 512])
        b.copy("dve", self.maskF, cst[:, K_MF:K_MF + 128])
        b.copy("dve", self.maskB, cst[:, K_MB:K_MB + 128])
        b.copy("dve", self.tri[0], cst[:, K_TF:K_TF + 128])
        b.copy("dve", self.tri[1], cst[:, K_TB:K_TB + 128])
        b.copy("dve", self.sgt[0], cst[:, K_GF:K_GF + 128])
        b.copy("dve", self.sgt[1], cst[:, K_GB:K_GB + 128])
        for d, k in ((0, K_NF), (1, K_NB)):
            b.copy("dve", self.neg[d].re("p (h i) -> p h i", h=4),
                   cst[:, k:k + 128].v(lambda a: a.unsqueeze(1)).bc([128, 4, 128]))
        b.copy("dve", self.cc[0], cst[:, K_CC:K_CC + 512].re("p (k d) -> p k d", k=2))
        b.copy("dve", self.cc[1], cst[:, K_CC + 512:K_CC + 1024].re("p (k d) -> p k d", k=2))
        b.act(al, al, AF.Exp)
        b.ts("dve", self.a_b, al, -1.0, None, ALU.mult)
        l0 = self.cols[:, C_LB:C_LB + 16]
        l1 = self.cols[:, C_LB + 16:C_LB + 32]
        l2 = self.cols[:, C_LB + 32:C_LB + 48]
        t1 = A([128, 16], F32, "t1")
        t2 = A([128, 16], F32, "t2")
        b.tt("dve", t1, l1, l0, ALU.subtract)
        b.tt("dve", t2, l2, l0, ALU.subtract)
        b.act(t1, t1, AF.Exp)
        b.act(t2, t2, AF.Exp)
        b.tt("dve", t1, t1, t2, ALU.add)
        b.ts("dve", t1, t1, 1.0, None, ALU.add)
        b.recip(self.lb3[:, 0, :], t1)
        b.ts("dve", self.lb3[:, 1, :], self.lb3[:, 0, :], -1.0, 1.0, ALU.mult, ALU.add)
        b.ts("dve", self.lb3[:, 2, :], self.lb3[:, 0, :], -1.0, None, ALU.add)
        b.memset("dve", self.ssq, 0.0)
        b.memset("dve", self.epsc, EPS)
        st = [A([128, 8, 512], F32, "tst%d" % i) for i in range(2)]
        sb_ = [A([128, 8, 512], BF16, "tsb%d" % i) for i in range(2)]
        for i in range(16):
            b.dma("sp", st[i % 2], self.tabf_d[i])
            b.copy("pool" if i % 2 == 0 else "act", sb_[i % 2], st[i % 2])
            self.b.s.dma("sp", [(self.tabb_d.ap[i], sb_[i % 2].ap)], reads=[sb_[i % 2].tok],
                         writes=[self.tabb_d.tok], partial=True)

    def norm_phase(self, xsrc, ccol):
        b = self.b
        A = Arena(self.arena, R_PHASE)
        xt = [A([128, 2048], F32, "nxt%d" % i) for i in range(2)]
        junk = A([128, 2048], BF16, "njunk")
        xn = A([128, 4, 2048], BF16, "nxn")
        sm = [[A([128, 1], F32) for _ in range(4)] for _ in range(2)]
        for g in range(4):
            for j in range(4):
                tt = g * 4 + j
                x_t = xt[tt % 2]
                s1, s2, s3, s4 = sm[tt % 2]
                b.dma("sp", x_t, xsrc[tt * 128:(tt + 1) * 128, :])
                b.act(junk, x_t, AF.Square, accum=s1)
                b.ts("dve", s2, s1, 1.0 / D, EPS, ALU.mult, ALU.add)
                b.act(s3, s2, AF.Sqrt)
                b.recip(s4, s3)
                b.act(xn[:, j, :], x_t, AF.Copy, scale=s4)
            for c in range(16):
                pb = self.PB[c % 2]
                for j in range(4):
                    b.tr(pb[:, j * 128:(j + 1) * 128], xn[:, j, c * 128:(c + 1) * 128], self.ident)
                dst = self.hT[:, c, g * 512:(g + 1) * 512]
                sc = self.cols[:, ccol + c:ccol + c + 1]
                b.act(dst, pb, AF.Copy, scale=sc)

    def hgrn_phase(self):
        b = self.b
        A = Arena(self.arena, R_PHASE)
        qf = A([128, 2048], F32, "qf")
        vT = A([128, 2048], BF16, "vT")
        V = A([128, 16, 128], BF16, "V")
        sg = A([128, 2048], BF16, "sg")
        qin = A([128, 2048], BF16, "qin")
        kin = A([128, 2048], BF16, "kin")
        kend = A([128, 16, 128], BF16, "kend")
        kendT = [A([128, 512], BF16, "kendT%d" % i) for i in range(2)]
        Sbf = A([128, 64, 128], BF16, "Sbf")
        Of = A([128, 2048], F32, "Of")
        tsets = [[A([128, 512], F32, n + str(i)) for n in ("sn", "lf", "bb", "E", "En", "kinf")] for i in range(2)]
        qcnt = [0]
        Elast = A([128, 64], F32, "Elast")
        S32 = [A([128, 128], F32, "S32_%d" % i) for i in range(2)]
        Am = [A([128, 128], BF16, "Am%d" % i) for i in range(2)]
        ot, rs1, rs2 = tsets[0][0], tsets[0][1], tsets[0][2]
        osq = T(tsets[0][3].ap.bitcast(BF16)[:, 0:512], tok=tsets[0][3].tok)
        yT = A([128, 2048], BF16, "yTt")
        PS = self.PS
        r3 = lambda t: t.re("p (c k) -> p c k", k=32)
        for hd in range(16):
            oml = self.lb3[:, 1, hd:hd + 1]
            noml = self.lb3[:, 2, hd:hd + 1]
            wq = self.load_w(self.w_in0[0 * 16 + hd])
            self.proj(wq, lambda q, p: b.copy("act", qf[:, q * 512:(q + 1) * 512], p))
            wv = self.load_w(self.w_in0[1 * 16 + hd])
            self.proj(wv, lambda q, p: b.copy("dve", vT[:, q * 512:(q + 1) * 512], p))
            for q in range(4):
                pb = self.PB[q % 2]
                for j in range(4):
                    b.tr(pb[:, j * 128:(j + 1) * 128], vT[:, (4 * q + j) * 128:(4 * q + j + 1) * 128], self.ident)
                b.copy("act" if q % 2 else "dve", V[:, 4 * q:4 * q + 4, :], pb.re("p (j k) -> p j k", k=128))
            wg = self.load_w(self.w_in0[4 * 16 + hd])
            self.proj(wg, lambda q, p: b.act(sg[:, q * 512:(q + 1) * 512], p, AF.Silu))
            for dr in (0, 1):
                wz = self.load_w(self.w_in0[(2 + dr) * 16 + hd])

                def partB(q, ts_, dr=dr):
                    qs = slice(q * 512, (q + 1) * 512)
                    sn, lf, bb, E, En, kinf = ts_
                    b.scan(bb, self.rst, lf, 0.0)
                    if dr == 1:
                        b.tt("dve", r3(En), r3(bb)[:, :, 31:32].bc([128, 16, 32]), r3(bb), ALU.subtract)
                        b.tt("dve", bb, En, lf, ALU.add)
                    b.act(E, bb, AF.Exp)
                    b.act(En, bb, AF.Exp, scale=-1.0)
                    b.tt("dve", qin[:, qs], qf[:, qs], E, ALU.mult)
                    b.stt(kinf, sn, oml, En, ALU.mult, ALU.mult)
                    b.copy("act", kin[:, qs], kinf)
                    edge = r3(E)[:, :, 31:32] if dr == 0 else r3(E)[:, :, 0:1]
                    kT = kendT[q % 2]
                    b.tt("dve", r3(kT), r3(kinf), edge.bc([128, 16, 32]), ALU.mult)
                    b.copy("dve", Elast[:, 16 * q:16 * q + 16].v(lambda a: a.unsqueeze(2)), edge)
                    pb = self.PB[q % 2]
                    for j in range(4):
                        b.tr(pb[:, j * 128:(j + 1) * 128], kT[:, j * 128:(j + 1) * 128], self.ident)
                    b.copy("act", kend[:, 4 * q:4 * q + 4, :], pb.re("p (j k) -> p j k", k=128))

                pend = []

                def cons(q, p, dr=dr, pend=pend):
                    ts_ = tsets[qcnt[0] % 2]
                    qcnt[0] += 1
                    sn, lf = ts_[0], ts_[1]
                    b.act(sn, p, AF.Exp)
                    b.act(lf, sn, AF.Ln, bias=1.0)
                    b.act(sn, lf, AF.Exp, scale=-1.0)
                    b.act(lf, sn, AF.Ln, bias=1.0, scale=noml)
                    if pend:
                        partB(*pend.pop())
                    pend.append((q, ts_))

                self.proj(wz, cons)
                partB(*pend.pop())
                order = list(range(64)) if dr == 0 else list(range(63, -1, -1))
                b.memset("dve", Sbf[:, order[0], :], 0.0)
                for idx, c in enumerate(order[:-1]):
                    tile, k = c // 4, c % 4
                    pS = PS[2 + k][:, 0:128]
                    b.mm(pS, kend[32 * k:32 * k + 32, tile, :], V[32 * k:32 * k + 32, tile, :],
                         tile_position=(32 * k, 0))
                    new, old = S32[idx % 2], S32[(idx + 1) % 2]
                    if idx == 0:
                        b.copy("dve", new, pS)
                    else:
                        b.stt(new, old, Elast[:, c:c + 1], pS, ALU.mult, ALU.add)
                    b.copy("act", Sbf[:, order[idx + 1], :], new)
                mask = self.maskF if dr == 0 else self.maskB
                zero_c = order[0]
                for q in range(4):
                    qs = slice(q * 512, (q + 1) * 512)
                    pO = PS[4 + q % 2]
                    for bl in range(4):
                        bi = 4 * q + bl
                        bs = slice(bi * 128, (bi + 1) * 128)
                        pA = PS[2 + bi % 2][:, 128:256]
                        am = Am[bi % 2]
                        b.mm(pA, kin[:, bs], qin[:, bs])
                        b.tt("dve", am, pA, mask, ALU.mult)
                        o_sl = pO[:, bl * 128:(bl + 1) * 128]
                        cl = [4 * bi + k for k in range(4) if 4 * bi + k != zero_c]
                        b.mm(o_sl, V[:, bi, :], am, start=True, stop=False)
                        for c in cl:
                            k = c % 4
                            b.mm(o_sl[:, 32 * k:32 * k + 32], Sbf[:, c, :], qin[:, c * 32:(c + 1) * 32],
                                 start=False, stop=(c == cl[-1]))
                    if dr == 0:
                        b.copy("act", Of[:, qs], pO)
                    else:
                        b.tt("dve", ot, pO, Of[:, qs], ALU.add)
                        b.tt("pool", osq, ot, ot, ALU.mult)
                        pss = PS[2 + q % 2]
                        b.mm(pss, self.ones_bf, osq)
                        b.act(rs2, pss, AF.Ln, bias=self.epsc, scale=1.0 / 128.0)
                        b.act(rs1, rs2, AF.Exp, scale=-0.5)
                        b.tt("dve", ot, ot, rs1, ALU.mult)
                        b.stt(yT[:, qs], ot, self.cols[:, C_HNW + hd:C_HNW + hd + 1], sg[:, qs], ALU.mult, ALU.mult)
            self.b.s.dma("act", [(self.yT_d.ap[hd * 128:(hd + 1) * 128, :], yT.ap)], reads=[yT.tok],
                         writes=[self.yT_d.tok], partial=True)

    def fnet_phase(self):
        b = self.b
        PS = self.PS
        A = Arena(self.arena, R_PHASE)
        uT = A([128, 2, 2048], BF16, "uT")
        sgB = A([128, 2, 2048], BF16, "sgB")
        UW = A([128, 16, 2, 256], BF16, "UW")
        tabs = [[A([128, 8, 512], BF16, "tab%d_%d" % (s, w)) for w in range(2)] for s in range(2)]
        fwst = A([128, 2, 256], F32, "fwst")
        fwb = A([128, 2, 256], BF16, "fwb")
        W12 = A([128, 2, 2, 256], BF16, "W12")
        yo = [A([128, 512], BF16, "fyo%d" % i) for i in range(2)]
        slot_i = 0
        for g in range(8):
            for co in range(2):
                wb = self.load_w(self.w_in0[80 + 2 * g + co])
                self.proj(wb, lambda q, p, co=co: b.copy("act", uT[:, co, q * 512:(q + 1) * 512], p))
            for co in range(2):
                wb = self.load_w(self.w_in0[96 + 2 * g + co])
                self.proj(wb, lambda q, p, co=co: b.act(sgB[:, co, q * 512:(q + 1) * 512], p, AF.Silu))
            b.dma("sp", fwst, self.fw_d[g])
            b.copy("dve", fwb, fwst)
            for co in range(2):
                p = PS[2]
                for w in range(2):
                    for k in range(2):
                        b.mm(p[:, w * 256:(w + 1) * 256], self.cc[w][:, k, co * 128:(co + 1) * 128], fwb[:, k, :],
                             start=(k == 0), stop=(k == 1))
                b.copy("act", W12[:, co, 0, :], p[:, 0:256])
                b.ts("dve", W12[:, co, 1, :], p[:, 256:512], -1.0, None, ALU.mult)
            for st in range(16):
                p = PS[2 + st % 2]
                for co in range(2):
                    b.mm(p, uT[:, co, st * 128:(st + 1) * 128], W12[:, co, :, :].re("p w d -> p (w d)"),
                         start=(co == 0), stop=(co == 1))
                b.copy("act" if st % 2 else "dve", UW[:, st, :, :], p.re("p (w d) -> p w d", w=2))
            for tt in range(4):
                banks = (PS[2], PS[3]) if tt % 2 == 0 else (PS[4], PS[5])
                for half in range(2):
                    slot = tabs[slot_i % 2]
                    slot_i += 1
                    for w in range(2):
                        b.dma("sp", slot[w], self.tabb_d[w * 8 + tt * 2 + half])
                    for dblk in range(2):
                        p = banks[dblk]
                        for j in range(8):
                            st = half * 8 + j
                            for w in range(2):
                                b.mm(p, UW[:, st, w, dblk * 128:(dblk + 1) * 128], slot[w][:, j, :],
                                     start=(half == 0 and j == 0 and w == 0), stop=(half == 1 and j == 7 and w == 1))
                for dblk in range(2):
                    y = yo[dblk]
                    ch = 2 * g + dblk
                    b.stt(y, banks[dblk], self.cols[:, C_FB + ch:C_FB + ch + 1], sgB[:, dblk, tt * 512:(tt + 1) * 512],
                          ALU.add, ALU.mult)
                    r0 = 2048 + ch * 128
                    self.b.s.dma("act", [(self.yT_d.ap[r0:r0 + 128, tt * 512:(tt + 1) * 512], y.ap)], reads=[y.tok],
                                 writes=[self.yT_d.tok], partial=True)

    def outproj_phase(self, wout, xsrc, xdst, split):
        b = self.b
        PS = self.PS
        A = Arena(self.arena, R_BIG)
        wsl = [A([128, 32, 512], BF16, "wsl%d" % i) for i in range(2)]
        ysl = [A([128, 32, 512], BF16, "ysl%d" % i) for i in range(2)]
        wst = [A([128, 4, 512], F32, "owst%d" % i) for i in range(2)]
        xr = [A([128, 512], F32, "xr%d" % i) for i in range(2)]
        xo = [A([128, 512], F32, "xo%d" % i) for i in range(2)]
        if split:
            b.ts("dve", self.rstd, self.ssq, 1.0 / 2048.0, EPS, ALU.mult, ALU.add)
            b.act(self.rstd, self.rstd, AF.Sqrt)
            b.recip(self.rstd, self.rstd)
        yv = self.yT_d.re("(kc p) t -> p kc t", p=128)
        cnt = 0
        for dt_ in range(4):
            w = wsl[dt_ % 2]
            for pc in range(8):
                st = wst[pc % 2]
                b.dma("sp", st, wout[dt_][:, pc * 4:(pc + 1) * 4, :])
                b.copy("pool", w[:, pc * 4:(pc + 1) * 4, :], st)
            for tq in range(4):
                ys = ysl[(dt_ * 4 + tq) % 2]
                self.b.s.dma("sp", [(ys.ap[:, i * 8:(i + 1) * 8, :], yv.ap[:, i * 8:(i + 1) * 8, tq * 512:(tq + 1) * 512])
                                    for i in range(4)], reads=[self.yT_d.tok], writes=[ys.tok])
                for j in range(4):
                    tt = tq * 4 + j
                    ts_ = slice(j * 128, (j + 1) * 128)
                    pa = PS[cnt % 3]
                    pbk = PS[3 + cnt % 3]
                    x_r = xr[cnt % 2]
                    x_o = xo[cnt % 2]
                    cnt += 1
                    b.dma("sp", x_r, xsrc[tt * 128:(tt + 1) * 128, dt_ * 512:(dt_ + 1) * 512])
                    if not split:
                        for kc in range(32):
                            b.mm(pa, ys[:, kc, ts_], w[:, kc, :], start=(kc == 0), stop=(kc == 31))
                        b.tt("dve", x_o, pa, x_r, ALU.add)
                    else:
                        for kc in range(16):
                            b.mm(pa, ys[:, kc, ts_], w[:, kc, :], start=(kc == 0), stop=(kc == 15))
                        for kc in range(16, 32):
                            b.mm(pbk, ys[:, kc, ts_], w[:, kc, :], start=(kc == 16), stop=(kc == 31))
                        b.tt("dve", x_o, pa, x_r, ALU.add)
                        b.stt(x_o, pbk, self.rstd[:, tt:tt + 1], x_o, ALU.mult, ALU.add)
                    self.b.s.dma("act", [(xdst.ap[tt * 128:(tt + 1) * 128, dt_ * 512:(dt_ + 1) * 512], x_o.ap)],
                                 reads=[x_o.tok], writes=[xdst.tok], partial=True)

    def final_norm(self, xsrc, out):
        b = self.b
        A = Arena(self.arena, R_BIG)
        fnwb = A([128, 2048], F32, "fnwb")
        xt = [A([128, 2048], F32, "fxt%d" % i) for i in range(2)]
        xo = [A([128, 2048], F32, "fxo%d" % i) for i in range(2)]
        junk = A([128, 2048], BF16, "fjunk")
        sm = [[A([128, 1], F32) for _ in range(4)] for _ in range(2)]
        b.dma("sp", fnwb, self.fnw_d.bc([128, 2048]))
        for tt in range(16):
            x_t, x_o = xt[tt % 2], xo[tt % 2]
            s1, s2, s3, s4 = sm[tt % 2]
            b.dma("sp", x_t, xsrc[tt * 128:(tt + 1) * 128, :])
            b.act(junk, x_t, AF.Square, accum=s1)
            b.ts("dve", s2, s1, 1.0 / D, EPS, ALU.mult, ALU.add)
            b.act(s3, s2, AF.Sqrt)
            b.recip(s4, s3)
            b.stt(x_o, x_t, s4, fnwb, ALU.mult, ALU.mult)
            self.b.s.dma("act", [(out.ap[tt * 128:(tt + 1) * 128, :], x_o.ap)], reads=[x_o.tok],
                         writes=[out.tok], partial=True, is_output=True)

    def sconv_phase(self):
        b = self.b
        A = Arena(self.arena, R_PHASE)
        cin = A([128, 2048], F32, "cin")
        upad = A([128, 2050], F32, "upad")
        ya = A([128, 2048], F32, "ya")
        yb = A([128, 2048], F32, "yb")
        sgc = [A([128, 512], F32, "sgc%d" % i) for i in range(2)]
        yo = A([128, 2048], BF16, "syo")
        b.memset("dve", upad[:, 0:1], 0.0)
        b.memset("dve", upad[:, 2049:2050], 0.0)
        for cb in range(16):
            w0 = self.cols[:, C_SW + cb:C_SW + cb + 1]
            w1 = self.cols[:, C_SW + 16 + cb:C_SW + 16 + cb + 1]
            w2 = self.cols[:, C_SW + 32 + cb:C_SW + 32 + cb + 1]
            wb = self.load_w(self.w_in1[cb])
            self.proj(wb, lambda q, p: b.copy("act", cin[:, q * 512:(q + 1) * 512], p))
            wb = self.load_w(self.w_in1[32 + cb])
            self.proj(wb, lambda q, p: b.tt("dve", upad[:, 1 + q * 512:1 + (q + 1) * 512], p,
                                            cin[:, q * 512:(q + 1) * 512], ALU.mult))
            b.ts("dve", ya, upad[:, 1:2049], w1, None, ALU.mult)
            b.stt(yb, upad[:, 0:2048], w0, ya, ALU.mult, ALU.add)
            b.stt(ya, upad[:, 2:2050], w2, yb, ALU.mult, ALU.add)
            wb = self.load_w(self.w_in1[16 + cb])
            self.proj(wb, lambda q, p: b.tt("dve", yb[:, q * 512:(q + 1) * 512], p, ya[:, q * 512:(q + 1) * 512],
                                            ALU.mult))
            wb = self.load_w(self.w_in1[48 + cb])

            def cons(q, p):
                s = sgc[q % 2]
                b.act(s, p, AF.Silu)
                b.tt("dve", yo[:, q * 512:(q + 1) * 512], yb[:, q * 512:(q + 1) * 512], s, ALU.mult)

            self.proj(wb, cons)
            self.b.s.dma("act", [(self.yT_d.ap[cb * 128:(cb + 1) * 128, :], yo.ap)], reads=[yo.tok],
                         writes=[self.yT_d.tok], partial=True)

    def conv_block(self, wsrc, wcol, out, upad, yq):
        b = self.b
        w0 = self.cols[:, C_CW + wcol:C_CW + wcol + 1]
        w1 = self.cols[:, C_CW + 24 + wcol:C_CW + 24 + wcol + 1]
        w2 = self.cols[:, C_CW + 48 + wcol:C_CW + 48 + wcol + 1]
        cbias = self.cols[:, C_CB + wcol:C_CB + wcol + 1]
        wb = self.load_w(wsrc)
        self.proj(wb, lambda q, p: b.copy("act", upad[:, 1 + q * 512:1 + (q + 1) * 512], p))
        for q in range(4):
            ya, yb = yq[q % 2]
            b.ts("dve", ya, upad[:, 1 + q * 512:1 + (q + 1) * 512], w1, None, ALU.mult)
            b.stt(yb, upad[:, q * 512:(q + 1) * 512], w0, ya, ALU.mult, ALU.add)
            b.stt(ya, upad[:, 2 + q * 512:2 + (q + 1) * 512], w2, yb, ALU.mult, ALU.add)
            b.act(out[:, q * 512:(q + 1) * 512], ya, AF.Silu, bias=cbias)

    def ssd_phase(self):
        b = self.b
        PS = self.PS
        A = Arena(self.arena, R_PHASE)
        dt = A([128, 16, 64], F32, "dt")
        dta = A([128, 16, 64], F32, "dta")
        A0 = Arena(self.arena, (A.p, ARENA_BYTES))
        wdst = A0([128, 16, 64], F32, "wdst")
        wdtb = A0([128, 16, 64], BF16, "wdtb")
        xsT = A([128, 2, 2048], BF16, "xsT")
        xstok = A([128, 16, 256], BF16, "xstok")
        szT = A([128, 2, 2048], BF16, "szT")
        yacc = A([128, 16, 256], F32, "yacc")
        BT = A([128, 2048], BF16, "BT")
        CT = A([128, 2048], BF16, "CT")
        Btok = A([128, 16, 128], BF16, "Btok")
        upad = A([128, 2050], F32, "upad")
        yq = [(A([128, 512], F32, "cya%d" % i), A([128, 512], F32, "cyb%d" % i)) for i in range(2)]
        class CS:
            pass
        sets = []
        for i in range(2):
            c = CS()
            c.X = A([128, 512], F32, "X%d" % i)
            c.Dm = A([128, 512], BF16, "Dm%d" % i)
            c.Wm = A([128, 512], BF16, "Wm%d" % i)
            c.cb = A([128, 128], F32, "cb%d" % i)
            c.xdt = A([128, 256], BF16, "xdt%d" % i)
            c.xdtw = A([128, 256], BF16, "xdtw%d" % i)
            c.tz = A([128, 256], F32, "tz%d" % i)
            c.ac, c.ei, c.wj, c.et, c.dif = [A([128, 4], F32, n + str(i)) for n in ("ac", "ei", "wj", "et", "dif")]
            sets.append(c)
        S32 = A([128, 256], F32, "S32")
        Sbf = A([128, 256], BF16, "Sbf")
        ysum = A([128, 256], F32, "ysum")
        y1 = sets[0].X
        y2 = sets[1].X
        ysq = sets[0].Dm
        yo = [sets[1].Dm, sets[1].Wm]
        pEs, pY = (PS[2], PS[5]), PS[3]
        pS = PS[4][:, 0:256]
        pQ = PS[4][:, 256:260]
        pC = PS[0][:, 0:128]
        pSm = PS[1][:, 0:8]
        pT = PS[5]
        b.memset("dve", upad[:, 0:1], 0.0)
        b.memset("dve", upad[:, 2049:2050], 0.0)
        b.dma("sp", wdst, self.w_dt)
        b.copy("pool", wdtb, wdst)
        for half in range(2):
            p = PS[2 + half]
            for j in range(8):
                tt = half * 8 + j
                for kc in range(KC):
                    b.mm(p[:, j * 64:(j + 1) * 64], self.hT[:, kc, tt * 128:(tt + 1) * 128], wdtb[:, kc, :],
                         start=(kc == 0), stop=(kc == KC - 1))
            b.tt("dve", dt[:, half * 8:(half + 1) * 8, :], p.re("p (j h) -> p j h", h=64),
                 self.dtb_b.v(lambda a: a.unsqueeze(1)).bc([128, 8, 64]), ALU.add)
        b.act(dta, dt, AF.Exp)
        b.act(dt, dta, AF.Ln, bias=1.0)
        b.tt("dve", dta, dt, self.a_b.v(lambda a: a.unsqueeze(1)).bc([128, 16, 64]), ALU.mult)
        b.s.barrier()
        h4 = lambda t, d: t.re("p (h d) -> p h d", d=d)
        for u in range(8):
            g = u // 2
            for blk in range(2):
                self.conv_block(self.w_in1[80 + 2 * u + blk], 2 * u + blk, xsT[:, blk, :], upad, yq)
            for blk in range(2):
                wb = self.load_w(self.w_in1[64 + 2 * u + blk])
                self.proj(wb, lambda q, p, blk=blk: b.act(szT[:, blk, q * 512:(q + 1) * 512], p, AF.Silu))
            if u % 2 == 0:
                self.conv_block(self.w_in1[96 + g], 16 + g, BT, upad, yq)
                self.conv_block(self.w_in1[100 + g], 20 + g, CT, upad, yq)
                for q in range(4):
                    pb = self.PB[q % 2]
                    for j in range(4):
                        tt = 4 * q + j
                        b.tr(pb[:, j * 128:(j + 1) * 128], BT[:, tt * 128:(tt + 1) * 128], self.ident)
                    b.copy("act" if q % 2 else "dve", Btok[:, 4 * q:4 * q + 4, :], pb.re("p (j k) -> p j k", k=128))
            for tt in range(16):
                pb = self.PB[tt % 2]
                for blk in range(2):
                    b.tr(pb[:, blk * 128:(blk + 1) * 128], xsT[:, blk, tt * 128:(tt + 1) * 128], self.ident)
                b.copy("act" if tt % 2 else "dve", xstok[:, tt, :], pb[:, 0:256])
            for dr in range(2):
                order = list(range(16)) if dr == 0 else list(range(15, -1, -1))
                c0 = dr * 32 + 4 * u

                def fa1(idx, dr=dr, order=order, c0=c0):
                    ch = order[idx]
                    c = sets[idx % 2]
                    dta_c = dta[:, ch, c0:c0 + 4]
                    b.mm(pSm[:, 0:4], self.tri[dr], dta_c)
                    b.mm(pSm[:, 4:8], self.ones_f, dta_c)
                    b.copy("dve", c.ac, pSm[:, 0:4])
                    b.act(c.ei, c.ac, AF.Exp)
                    b.tt("dve", c.dif, pSm[:, 4:8], c.ac, ALU.subtract)
                    b.act(c.wj, c.dif, AF.Exp)
                    b.act(c.et, pSm[:, 4:8], AF.Exp)
                    b.tt("dve", h4(c.X, 128), self.tri[dr].v(lambda a: a.unsqueeze(1)).bc([128, 4, 128]),
                         dta_c.v(lambda a: a.unsqueeze(2)).bc([128, 4, 128]), ALU.mult)

                def fa2(idx, dr=dr, order=order):
                    ch = order[idx]
                    c = sets[idx % 2]
                    pE = pEs[idx % 2]
                    chs = slice(ch * 128, (ch + 1) * 128)
                    b.mm(pE, self.sgt[dr], c.X, start=True, stop=False)
                    b.mm(pE, self.ident, self.neg[dr], start=False, stop=True)
                    b.act(c.Dm, pE, AF.Exp)
                    b.mm(pC, BT[:, chs], CT[:, chs])
                    b.copy("act", c.cb, pC)

                def fb(idx, dr=dr, order=order, c0=c0):
                    ch = order[idx]
                    c = sets[idx % 2]
                    dt_c = dt[:, ch, c0:c0 + 4]
                    b.tt("dve", h4(c.Wm, 128), h4(c.Dm, 128), c.cb.v(lambda a: a.unsqueeze(1)).bc([128, 4, 128]),
                         ALU.mult)
                    b.tt("dve", h4(c.xdt, 64), h4(xstok[:, ch, :], 64),
                         dt_c.v(lambda a: a.unsqueeze(2)).bc([128, 4, 64]), ALU.mult)
                    b.tt("dve", h4(c.xdtw, 64), h4(c.xdt, 64), c.wj.v(lambda a: a.unsqueeze(2)).bc([128, 4, 64]),
                         ALU.mult)

                def bpe(idx, dr=dr, order=order):
                    ch = order[idx]
                    c = sets[idx % 2]
                    chs = slice(ch * 128, (ch + 1) * 128)
                    for hl in range(4):
                        b.mm(pY[:, hl * 64:(hl + 1) * 64], c.Wm[:, hl * 128:(hl + 1) * 128],
                             c.xdt[:, hl * 64:(hl + 1) * 64])
                    if idx > 0:
                        b.mm(pY[:, 256:512], CT[:, chs], Sbf)
                    if idx < 15:
                        b.mm(pS, Btok[:, ch, :], c.xdtw)

                def bdve(idx, dr=dr, order=order):
                    ch = order[idx]
                    c = sets[idx % 2]
                    ydst = yacc[:, ch, :]
                    if idx > 0:
                        b.tt("dve", h4(c.tz, 64), h4(pY[:, 256:512], 64),
                             c.ei.v(lambda a: a.unsqueeze(2)).bc([128, 4, 64]), ALU.mult)
                        if dr == 0:
                            b.tt("dve", ydst, pY[:, 0:256], c.tz, ALU.add)
                        else:
                            b.tt("dve", ysum, pY[:, 0:256], c.tz, ALU.add)
                            b.tt("dve", ydst, ydst, ysum, ALU.add)
                    else:
                        if dr == 0:
                            b.copy("dve", ydst, pY[:, 0:256])
                        else:
                            b.tt("dve", ydst, pY[:, 0:256], ydst, ALU.add)
                    if idx < 15:
                        if idx == 0:
                            b.copy("dve", S32, pS)
                        else:
                            b.tt("dve", h4(S32, 64), h4(S32, 64), c.et.v(lambda a: a.unsqueeze(2)).bc([128, 4, 64]),
                                 ALU.mult)
                            b.tt("dve", S32, pS, S32, ALU.add)
                        b.copy("act", Sbf, S32)

                fa1(0)
                fa2(0)
                fb(0)
                for idx in range(16):
                    nxt = idx + 1 < 16
                    if nxt:
                        fa1(idx + 1)
                    bpe(idx)
                    if nxt:
                        fa2(idx + 1)
                    bdve(idx)
                    if nxt:
                        fb(idx + 1)
            for blk in range(2):
                chb = 2 * u + blk
                for q in range(4):
                    qs = slice(q * 512, (q + 1) * 512)
                    for j in range(4):
                        tt = 4 * q + j
                        b.tr(pT[:, j * 128:(j + 1) * 128], yacc[:, tt, blk * 128:(blk + 1) * 128], self.identf)
                    b.stt(y1, xsT[:, blk, qs], self.cols[:, C_DD + chb:C_DD + chb + 1], pT, ALU.mult, ALU.add)
                    b.tt("dve", y2, y1, szT[:, blk, qs], ALU.mult)
                    b.act(ysq, y2, AF.Square)
                    for j in range(4):
                        b.mm(pQ[:, j:j + 1], ysq[:, j * 128:(j + 1) * 128], self.ones_bf[:, 0:1])
                    b.tt("dve", self.ssq[:, 4 * q:4 * q + 4], pQ, self.ssq[:, 4 * q:4 * q + 4], ALU.add)
                    y = yo[q % 2]
                    b.ts("dve", y, y2, self.cols[:, C_DNW + chb:C_DNW + chb + 1], None, ALU.mult)
                    r0 = 2048 + chb * 128
                    self.b.s.dma("act", [(self.yT_d.ap[r0:r0 + 128, qs], y.ap)], reads=[y.tok],
                                 writes=[self.yT_d.tok], partial=True)


def _colize(v):
    v = np.asarray(v, np.float32)
    v = v.reshape(-1, v.shape[-1] // 128, 128)
    return np.ascontiguousarray(v.transpose(2, 0, 1).reshape(128, -1))


_CONST_CACHE = {}


def _constants():
    if _CONST_CACHE:
        return _CONST_CACHE
    j = np.arange(128)[:, None]
    i = np.arange(128)[None, :]
    same = (j // 32) == (i // 32)
    cst = np.zeros((128, NCST), np.float32)
    cst[:, K_ID:K_ID + 128] = np.eye(128)
    cst[:, K_MF:K_MF + 128] = (same & (j <= i))
    cst[:, K_MB:K_MB + 128] = (same & (j >= i))
    cst[:, K_TF:K_TF + 128] = (j <= i)
    cst[:, K_TB:K_TB + 128] = (j >= i)
    cst[:, K_GF:K_GF + 128] = (j > i)
    cst[:, K_GB:K_GB + 128] = (j < i)
    cst[:, K_NF:K_NF + 128] = np.where(j <= i, 0.0, -30000.0)
    cst[:, K_NB:K_NB + 128] = np.where(j >= i, 0.0, -30000.0)
    rst = np.ones((128, 512), np.float32)
    rst[:, ::32] = 0.0
    cst[:, K_RST:K_RST + 512] = rst
    c = np.arange(256)
    m = (c[:, None] * c[None, :]) % 256
    ang = 2.0 * np.pi * m.astype(np.float64) / 256.0
    scale = 1.0 / np.sqrt(2048.0 * 256.0)
    for w, fn in enumerate((np.cos, np.sin)):
        tab = (fn(ang) * scale).astype(np.float32)
        cst[:, K_CC + w * 512:K_CC + (w + 1) * 512] = tab.reshape(2, 128, 256).transpose(1, 0, 2).reshape(128, 512)
    s = np.arange(2048, dtype=np.int64)
    m = (s[:, None] * s[None, :]) % 2048
    ang = 2.0 * np.pi * m.astype(np.float64) / 2048.0
    tabf = np.zeros((2, 4, 2, 128, 8, 512), np.float32)
    for w, fn in enumerate((np.cos, np.sin)):
        tab = fn(ang).astype(np.float32)
        tabf[w] = tab.reshape(2, 8, 128, 4, 512).transpose(3, 0, 2, 1, 4)
    _CONST_CACHE["cst"] = cst
    _CONST_CACHE["tabf"] = np.ascontiguousarray(tabf.reshape(16, 128, 8, 512))
    return _CONST_CACHE


def prepare_inputs(x, norm_w, final_norm_w, ev_w_in, ev_w_out, hgrn_lb_logits, hgrn_norm_w,
                   fnet_w, fnet_b, od_w_in, od_w_out, sconv_w, ssd_conv_w, ssd_conv_b,
                   ssd_dt_bias, ssd_a_log, ssd_d, ssd_norm_w):
    f = lambda a: np.asarray(a, np.float32)
    cs = _constants()
    w0 = f(ev_w_in)[0]
    w_in0 = np.ascontiguousarray(w0.reshape(16, 128, 112, 128).transpose(2, 1, 0, 3))
    w1 = f(od_w_in)[0]
    w_in1 = np.ascontiguousarray(w1[:, :13312].reshape(16, 128, 104, 128).transpose(2, 1, 0, 3))
    w_dt = np.ascontiguousarray(w1[:, 13312:13376].reshape(16, 128, 64).transpose(1, 0, 2))
    wo = lambda w: np.ascontiguousarray(f(w)[0].reshape(32, 128, 4, 512).transpose(2, 1, 0, 3))
    cols = np.concatenate([
        _colize(f(norm_w)[0:1]), _colize(f(norm_w)[1:2]), _colize(f(hgrn_lb_logits)),
        _colize(f(hgrn_norm_w)[0:1]), _colize(f(fnet_b)[0:1]), _colize(f(sconv_w)[0]),
        _colize(f(ssd_conv_w)[0]), _colize(f(ssd_conv_b)[0:1]),
        _colize(np.repeat(f(ssd_d)[0], 64)[None, :]), _colize(f(ssd_norm_w)[0:1])], axis=1)
    assert cols.shape == (128, NCOLS), cols.shape
    shared = {
        "w_in0": w_in0, "w_out0": wo(ev_w_out), "w_in1": w_in1, "w_dt": w_dt, "w_out1": wo(od_w_out),
        "cols": np.ascontiguousarray(cols), "cst": cs["cst"], "fnw": f(final_norm_w).reshape(1, D),
        "dtb": f(ssd_dt_bias).reshape(1, 64), "alog": f(ssd_a_log).reshape(1, 64),
        "fw": np.ascontiguousarray(f(fnet_w)[0].reshape(8, 2, 128, 256).transpose(0, 2, 1, 3)),
        "tabf": cs["tabf"],
    }
    xs = f(x)
    return [dict(shared, x=np.ascontiguousarray(xs[i])) for i in range(xs.shape[0])]


def build_program(debug=False, stop_after=None):
    nc = bass.Bass("TRN2", target_bir_lowering=False)
    Builder(nc, debug=debug, stop_after=stop_after).build()
    return nc


def kernel(**inputs):
    in_maps = prepare_inputs(**inputs)
    nc = build_program()
    res = run_bass_kernel_spmd(nc, in_maps, core_ids=list(range(8)))
    return np.stack([np.asarray(r["out"], np.float32) for r in res.results], axis=0)
```

```python
from contextlib import ExitStack
import numpy as np
import concourse.bass as bass
import concourse.mybir as mybir
from concourse.bass_utils import run_bass_kernel_spmd

F32 = mybir.dt.float32
BF16 = mybir.dt.bfloat16
AF = mybir.ActivationFunctionType
ALU = mybir.AluOpType
AX = mybir.AxisListType

L = 2048
D = 2048
NT = 16
KC = 16
EPS = 1e-6

ENGS = ("pe", "act", "dve", "pool", "sp")


class Tok:
    __slots__ = ("name", "w", "r", "war")

    def __init__(self, name=""):
        self.name = name
        self.w = {}
        self.r = {}
        self.war = {}


def _merge(dst, src):
    for k, v in src.items():
        if dst.get(k, -1) < v:
            dst[k] = v


class T:
    __slots__ = ("ap", "tok")

    def __init__(self, ap, tok=None, name=""):
        self.ap = ap
        self.tok = tok if tok is not None else Tok(name)

    def __getitem__(self, idx):
        return T(self.ap[idx], self.tok)

    def v(self, fn):
        return T(fn(self.ap), self.tok)

    def bc(self, shape):
        return T(self.ap.broadcast_to(list(shape)), self.tok)

    def re(self, s, **kw):
        return T(self.ap.rearrange(s, **kw), self.tok)


def _ap(x):
    return x.ap if isinstance(x, T) else x


def _toks(*xs):
    return [x.tok for x in xs if isinstance(x, T)]


class Op:
    __slots__ = ("eng", "fn", "deps", "seq", "signal", "sigval", "dma", "dsem", "dprev", "dval")


class Sched:
    def __init__(self, nc, n_dma_sems=48):
        self.nc = nc
        self.eng_ops = {e: [] for e in ENGS}
        self.n_dma_sems = n_dma_sems
        self.dma_val = [0] * n_dma_sems
        self.dma_rr = 0
        self.out_deps = {}

    def _deps(self, reads, writes, partial):
        deps = {}
        for t in reads:
            _merge(deps, t.w)
        for t in writes:
            _merge(deps, t.r)
            _merge(deps, t.war)
            if not partial:
                _merge(deps, t.w)
        return deps

    def _update(self, key, val, reads, writes, partial):
        me = {key: val}
        for t in reads:
            _merge(t.r, me)
        for t in writes:
            if t.r:
                t.war = t.r
                t.r = {}
                t.w = dict(me)
            elif partial:
                _merge(t.w, me)
            else:
                t.w = dict(me)

    def op(self, eng, fn, reads=(), writes=(), partial=False):
        o = Op()
        o.eng = eng
        o.fn = fn
        o.dma = None
        o.signal = False
        o.deps = self._deps(reads, writes, partial)
        lst = self.eng_ops[eng]
        lst.append(o)
        o.seq = len(lst)
        self._update(eng, o.seq, reads, writes, partial)
        return o

    def dma(self, queue, parts, reads=(), writes=(), partial=False, is_output=False):
        o = Op()
        o.eng = queue
        o.fn = None
        o.dma = parts
        o.signal = False
        o.deps = self._deps(reads, writes, partial)
        s = self.dma_rr
        self.dma_rr = (self.dma_rr + 1) % self.n_dma_sems
        o.dsem = s
        o.dprev = self.dma_val[s]
        self.dma_val[s] += 16 * len(parts)
        o.dval = self.dma_val[s]
        lst = self.eng_ops[queue]
        lst.append(o)
        o.seq = len(lst)
        self._update(("d", s), o.dval, reads, writes, partial)
        if is_output:
            _merge(self.out_deps, {("d", s): o.dval})
        return o

    def barrier(self):
        deps = {e: len(self.eng_ops[e]) for e in ENGS if len(self.eng_ops[e]) > 0 and e != "sp"}
        for s in range(self.n_dma_sems):
            if self.dma_val[s] > 0:
                deps[("d", s)] = self.dma_val[s]
        for e in ENGS:
            o = Op()
            o.eng = e
            o.fn = None
            o.dma = None
            o.signal = False
            o.deps = {k: v for k, v in deps.items() if k != e}
            lst = self.eng_ops[e]
            lst.append(o)
            o.seq = len(lst)

    def emit(self, es):
        nc = self.nc
        for e in ENGS:
            for o in self.eng_ops[e]:
                nd = {}
                for k, v in o.deps.items():
                    if isinstance(k, str):
                        if k == "pe" and e == "pe":
                            continue
                        while v > 0 and self.eng_ops[k][v - 1].fn is None and self.eng_ops[k][v - 1].dma is None:
                            v -= 1
                        if v == 0:
                            continue
                        p = self.eng_ops[k][v - 1]
                        if p.dma is not None:
                            _merge(nd, {("d", p.dsem): p.dval})
                        else:
                            p.signal = True
                            _merge(nd, {k: v})
                    else:
                        _merge(nd, {k: v})
                o.deps = nd
        for e in ENGS:
            c = 0
            for o in self.eng_ops[e]:
                if o.signal:
                    c += 1
                    o.sigval = c
        esem = {e: es.enter_context(nc.semaphore("s_" + e)) for e in ENGS}
        dsems = [es.enter_context(nc.semaphore("d%d" % i)) for i in range(self.n_dma_sems)]
        block = es.enter_context(nc.Block())
        sched = self

        def run(en, e):
            waited = {}
            for o in sched.eng_ops[en]:
                for k, v in o.deps.items():
                    if isinstance(k, str):
                        kk = k
                        tgt = sched.eng_ops[k][v - 1].sigval
                        sem = esem[k]
                    else:
                        kk = k
                        tgt = v
                        sem = dsems[k[1]]
                    if waited.get(kk, 0) >= tgt:
                        continue
                    waited[kk] = tgt
                    e.wait_ge(sem, tgt)
                if o.dma is not None:
                    kk = ("d", o.dsem)
                    if o.dprev > 0 and waited.get(kk, 0) < o.dprev:
                        e.wait_ge(dsems[o.dsem], o.dprev)
                        waited[kk] = o.dprev
                    for (oa, ia) in o.dma:
                        e.dma_start(out=oa, in_=ia).then_inc(dsems[o.dsem], 16)
                elif o.fn is not None:
                    inst = o.fn(e)
                    if o.signal:
                        inst.then_inc(esem[en], 1)
            if en == "sp":
                for k, v in sched.out_deps.items():
                    if waited.get(k, 0) < v:
                        e.wait_ge(dsems[k[1]], v)

        @block.sync
        def _(e):
            run("sp", e)

        @block.scalar
        def _(e):
            run("act", e)

        @block.vector
        def _(e):
            run("dve", e)

        @block.gpsimd
        def _(e):
            run("pool", e)

        @block.tensor
        def _(e):
            run("pe", e)


class B:
    def __init__(self, nc):
        self.nc = nc
        self.s = Sched(nc)

    def mm(self, out, lhsT, rhs, start=True, stop=True, tile_position=None):
        o, a, b = _ap(out), _ap(lhsT), _ap(rhs)
        kw = {}
        if tile_position is not None:
            kw["tile_position"] = tile_position
        self.s.op("pe", lambda e: e.matmul(o, lhsT=a, rhs=b, start=start, stop=stop, **kw),
                  reads=_toks(lhsT, rhs), writes=_toks(out), partial=True)

    def tr(self, out, in_, ident):
        o, a, i = _ap(out), _ap(in_), _ap(ident)
        self.s.op("pe", lambda e: e.transpose(o, a, i), reads=_toks(in_, ident), writes=_toks(out), partial=True)

    def act(self, out, in_, func, bias=None, scale=None, accum=None, extra_w=()):
        o, a = _ap(out), _ap(in_)
        kw = {}
        if bias is not None:
            kw["bias"] = _ap(bias)
        if scale is not None:
            kw["scale"] = _ap(scale)
        if accum is not None:
            kw["accum_out"] = _ap(accum)
        self.s.op("act", lambda e: e.activation(o, a, func, **kw),
                  reads=_toks(in_, bias, scale), writes=_toks(out, accum) + list(extra_w))

    def tt(self, eng, out, in0, in1, op):
        o, a, b = _ap(out), _ap(in0), _ap(in1)
        self.s.op(eng, lambda e: e.tensor_tensor(o, a, b, op), reads=_toks(in0, in1), writes=_toks(out))

    def ts(self, eng, out, in0, s1, s2=None, op0=ALU.mult, op1=None):
        o, a = _ap(out), _ap(in0)
        x1, x2 = _ap(s1), _ap(s2)
        if op1 is None:
            self.s.op(eng, lambda e: e.tensor_scalar(o, a, x1, None, op0),
                      reads=_toks(in0, s1), writes=_toks(out))
        else:
            self.s.op(eng, lambda e: e.tensor_scalar(o, a, x1, x2, op0, op1),
                      reads=_toks(in0, s1, s2), writes=_toks(out))

    def stt(self, out, in0, scalar, in1, op0, op1):
        o, a, c, b = _ap(out), _ap(in0), _ap(scalar), _ap(in1)
        self.s.op("dve", lambda e: e.scalar_tensor_tensor(o, a, c, b, op0, op1),
                  reads=_toks(in0, scalar, in1), writes=_toks(out))

    def scan(self, out, d0, d1, initial, op0=ALU.mult, op1=ALU.add):
        o, a, b, i = _ap(out), _ap(d0), _ap(d1), _ap(initial)
        self.s.op("dve", lambda e: e.tensor_tensor_scan(o, a, b, i, op0, op1),
                  reads=_toks(d0, d1, initial), writes=_toks(out))

    def copy(self, eng, out, in_):
        o, a = _ap(out), _ap(in_)
        if eng == "act":
            self.s.op("act", lambda e: e.copy(o, a), reads=_toks(in_), writes=_toks(out))
        else:
            self.s.op(eng, lambda e: e.tensor_copy(o, a), reads=_toks(in_), writes=_toks(out))

    def recip(self, out, in_):
        o, a = _ap(out), _ap(in_)
        self.s.op("dve", lambda e: e.reciprocal(o, a), reads=_toks(in_), writes=_toks(out))

    def memset(self, eng, out, val):
        o = _ap(out)
        self.s.op(eng, lambda e: e.memset(o, val), writes=_toks(out))

    def dma(self, queue, out, in_, is_output=False):
        self.s.dma(queue, [(_ap(out), _ap(in_))], reads=_toks(in_), writes=_toks(out), is_output=is_output)

    def dmas(self, queue, parts, is_output=False):
        self.s.dma(queue, [(_ap(o), _ap(i)) for o, i in parts],
                   reads=_toks(*[i for _, i in parts]), writes=_toks(*[o for o, _ in parts]),
                   is_output=is_output)


ARENA_BYTES = 206 * 1024
R_CONST = (0, 13312)
R_WEIGHT = (13312, 50176)
R_HT = (50176, 115712)
R_PHASE = (115712, ARENA_BYTES)
R_BIG = (13312, ARENA_BYTES)

C_NW0, C_NW1, C_LB, C_HNW, C_FB, C_SW, C_CW, C_CB, C_DD, C_DNW = 0, 16, 32, 80, 96, 112, 160, 232, 256, 272
NCOLS = 288
K_ID, K_MF, K_MB, K_TF, K_TB, K_GF, K_GB, K_NF, K_NB, K_RST, K_CC = 0, 128, 256, 384, 512, 640, 768, 896, 1024, 1152, 1664
NCST = 1664 + 1024


def _esize(dt):
    return 4 if dt == F32 else 2


class Arena:
    def __init__(self, arena_ap, region):
        self.a = arena_ap
        self.lo, self.hi = region
        self.p = self.lo

    def __call__(self, shape, dt=F32, name=""):
        n = 1
        for s in shape[1:]:
            n *= s
        nb = n * _esize(dt)
        nb_al = (nb + 31) // 32 * 32
        assert self.p + nb_al <= self.hi, "arena overflow %s: need %d have %d" % (name, nb_al, self.hi - self.p)
        v = self.a[0:shape[0], self.p:self.p + nb].bitcast(dt)
        self.p += nb_al
        if len(shape) == 3:
            v = v.rearrange("p (a b) -> p a b", a=shape[1])
        elif len(shape) == 4:
            v = v.rearrange("p (a b c) -> p a b c", a=shape[1], b=shape[2])
        return T(v, name=name)


class Builder:
    def __init__(self, nc, debug=False, stop_after=None):
        self.nc = nc
        self.b = B(nc)
        self.debug = debug
        self.stop_after = stop_after
        self.wi = 0
        self.pi = 0

    def din(self, name, shape, dt=F32):
        return T(self.nc.dram_tensor(name, list(shape), dt, kind="ExternalInput").ap(), name=name)

    def dscr(self, name, shape, dt=F32, out=False):
        kind = "ExternalOutput" if (out or self.debug) else "Internal"
        return T(self.nc.dram_tensor(name, list(shape), dt, kind=kind).ap(), name=name)

    def build(self):
        nc, b = self.nc, self.b
        self.x_d = self.din("x", [L, D])
        self.w_in0 = self.din("w_in0", [112, 128, 16, 128])
        self.w_out0 = self.din("w_out0", [4, 128, 32, 512])
        self.w_in1 = self.din("w_in1", [104, 128, 16, 128])
        self.w_dt = self.din("w_dt", [128, 16, 64])
        self.w_out1 = self.din("w_out1", [4, 128, 32, 512])
        self.cols_d = self.din("cols", [128, NCOLS])
        self.cst_d = self.din("cst", [128, NCST])
        self.fnw_d = self.din("fnw", [1, D])
        self.dtb_d = self.din("dtb", [1, 64])
        self.alog_d = self.din("alog", [1, 64])
        self.fw_d = self.din("fw", [8, 128, 2, 256])
        self.tabf_d = self.din("tabf", [16, 128, 8, 512])
        self.tabb_d = self.dscr("tabb", [16, 128, 8, 512], BF16)
        self.yT_d = self.dscr("yT", [4096, L], BF16)
        self.x1_d = self.dscr("x1", [L, D])
        self.x2_d = self.dscr("x2", [L, D])
        self.out_d = T(nc.dram_tensor("out", [L, D], F32, kind="ExternalOutput").ap(), name="out")
        self.arena = nc.alloc_sbuf_tensor("arena", [128, ARENA_BYTES], mybir.dt.uint8).ap()
        self.PS = [T(nc.alloc_psum_tensor("ps%d" % i, [128, 512], F32).ap(), name="ps%d" % i) for i in range(6)]
        pbs = [nc.alloc_psum_tensor("pb%d" % i, [128, 1024], BF16).ap() for i in range(2)]
        self.PB = [T(pbs[i][:, 0:512], name="pb%d" % i) for i in range(2)]
        self.PBf = [T(pbs[i].bitcast(F32), tok=self.PB[i].tok) for i in range(2)]
        C = Arena(self.arena, R_CONST)
        self.cols = C([128, NCOLS], F32, "cols")
        self.ident = C([128, 128], BF16, "ident")
        self.identf = C([128, 128], F32, "identf")
        self.ones_bf = C([128, 128], BF16, "ones_bf")
        self.ones_f = C([128, 128], F32, "ones_f")
        self.rst = C([128, 512], F32, "rst")
        self.maskF = C([128, 128], F32, "maskF")
        self.maskB = C([128, 128], F32, "maskB")
        self.tri = [C([128, 128], F32, "triF"), C([128, 128], F32, "triB")]
        self.sgt = [C([128, 128], F32, "sgtF"), C([128, 128], F32, "sgtB")]
        self.neg = [C([128, 512], BF16, "negF"), C([128, 512], BF16, "negB")]
        self.cc = [C([128, 2, 256], BF16, "ccC"), C([128, 2, 256], BF16, "ccS")]
        self.lb3 = C([128, 3, 16], F32, "lb3")
        self.dtb_b = C([128, 64], F32, "dtb_b")
        self.a_b = C([128, 64], F32, "a_b")
        self.epsc = C([128, 1], F32, "epsc")
        self.ssq = C([128, 16], F32, "ssq")
        self.rstd = C([128, 16], F32, "rstd")
        W = Arena(self.arena, R_WEIGHT)
        self.wstage = [W([128, 16, 128], F32, "wst%d" % i) for i in range(2)]
        self.wring = [W([128, 16, 128], BF16, "wr%d" % i) for i in range(5)]
        H = Arena(self.arena, R_HT)
        self.hT = H([128, 16, 2048], BF16, "hT")

        self.setup_phase()
        b.s.barrier()
        if self.stop_after == "setup":
            return self.finish()
        self.norm_phase(self.x_d, C_NW0)
        b.s.barrier()
        if self.stop_after == "norm0":
            return self.finish()
        self.hgrn_phase()
        b.s.barrier()
        if self.stop_after == "hgrn":
            return self.finish()
        self.fnet_phase()
        b.s.barrier()
        if self.stop_after == "fnet":
            return self.finish()
        self.outproj_phase(self.w_out0, self.x_d, self.x1_d, split=False)
        b.s.barrier()
        if self.stop_after == "out0":
            return self.finish()
        self.norm_phase(self.x1_d, C_NW1)
        b.s.barrier()
        self.sconv_phase()
        b.s.barrier()
        if self.stop_after == "sconv":
            return self.finish()
        self.ssd_phase()
        b.s.barrier()
        if self.stop_after == "ssd":
            return self.finish()
        self.outproj_phase(self.w_out1, self.x1_d, self.x2_d, split=True)
        b.s.barrier()
        self.final_norm(self.x2_d, self.out_d)
        return self.finish()

    def finish(self):
        if self.debug and self.stop_after in ("norm0", "hgrn", "fnet", "sconv", "ssd"):
            self.b.s.barrier()
            hd = T(self.nc.dram_tensor("dbg_hT", [128, 16, 2048], BF16, kind="ExternalOutput").ap())
            self.b.dma("sp", hd, self.hT, is_output=True)
        if self.stop_after is not None:
            A = Arena(self.arena, R_PHASE)
            t = A([128, 512], F32, "fin")
            self.b.s.barrier()
            self.b.memset("dve", t, 0.0)
            self.b.dma("sp", self.out_d[0:128, 0:512], t, is_output=True)
        self._es = ExitStack()
        self.b.s.emit(self._es)
        self._es.close()
        return self.nc

    def load_w(self, src):
        b = self.b
        st = self.wstage[self.wi % 2]
        wb = self.wring[self.wi % len(self.wring)]
        self.wi += 1
        b.dma("sp", st, src)
        b.copy("pool", wb, st)
        return wb

    def proj(self, wb, consumer):
        b = self.b
        for q in range(4):
            p = self.PS[self.pi % 2]
            self.pi += 1
            for kc in range(KC):
                b.mm(p, wb[:, kc, :], self.hT[:, kc, q * 512:(q + 1) * 512], start=(kc == 0), stop=(kc == KC - 1))
            consumer(q, p)

    def setup_phase(self):
        b = self.b
        A = Arena(self.arena, R_PHASE)
        cst = A([128, NCST], F32, "cst")
        b.dma("sp", self.cols, self.cols_d)
        b.dma("sp", cst, self.cst_d)
        b.dma("sp", self.dtb_b, self.dtb_d.bc([128, 64]))
        al = A([128, 64], F32, "al")
        b.dma("sp", al, self.alog_d.bc([128, 64]))
        b.copy("dve", self.ident, cst[:, K_ID:K_ID + 128])
        b.copy("dve", self.identf, cst[:, K_ID:K_ID + 128])
        b.memset("dve", self.ones_bf, 1.0)
        b.memset("dve", self.ones_f, 1.0)
        b.copy("dve", self.rst, cst[:, K_RST:K_RST + 512])
        b.copy("dve", self.maskF, cst[:, K_MF:K_MF + 128])
        b.copy("dve", self.maskB, cst[:, K_MB:K_MB + 128])
        b.copy("dve", self.tri[0], cst[:, K_TF:K_TF + 128])
        b.copy("dve", self.tri[1], cst[:, K_TB:K_TB + 128])
        b.copy("dve", self.sgt[0], cst[:, K_GF:K_GF + 128])
        b.copy("dve", self.sgt[1], cst[:, K_GB:K_GB + 128])
        for d, k in ((0, K_NF), (1, K_NB)):
            b.copy("dve", self.neg[d].re("p (h i) -> p h i", h=4),
                   cst[:, k:k + 128].v(lambda a: a.unsqueeze(1)).bc([128, 4, 128]))
        b.copy("dve", self.cc[0], cst[:, K_CC:K_CC + 512].re("p (k d) -> p k d", k=2))
        b.copy("dve", self.cc[1], cst[:, K_CC + 512:K_CC + 1024].re("p (k d) -> p k d", k=2))
        b.act(al, al, AF.Exp)
        b.ts("dve", self.a_b, al, -1.0, None, ALU.mult)
        l0 = self.cols[:, C_LB:C_LB + 16]
        l1 = self.cols[:, C_LB + 16:C_LB + 32]
        l2 = self.cols[:, C_LB + 32:C_LB + 48]
        t1 = A([128, 16], F32, "t1")
        t2 = A([128, 16], F32, "t2")
        b.tt("dve", t1, l1, l0, ALU.subtract)
        b.tt("dve", t2, l2, l0, ALU.subtract)
        b.act(t1, t1, AF.Exp)
        b.act(t2, t2, AF.Exp)
        b.tt("dve", t1, t1, t2, ALU.add)
        b.ts("dve", t1, t1, 1.0, None, ALU.add)
        b.recip(self.lb3[:, 0, :], t1)
        b.ts("dve", self.lb3[:, 1, :], self.lb3[:, 0, :], -1.0, 1.0, ALU.mult, ALU.add)
        b.ts("dve", self.lb3[:, 2, :], self.lb3[:, 0, :], -1.0, None, ALU.add)
        b.memset("dve", self.ssq, 0.0)
        b.memset("dve", self.epsc, EPS)
        st = [A([128, 8, 512], F32, "tst%d" % i) for i in range(2)]
        sb_ = [A([128, 8, 512], BF16, "tsb%d" % i) for i in range(2)]
        for i in range(16):
            b.dma("sp", st[i % 2], self.tabf_d[i])
            b.copy("pool" if i % 2 == 0 else "act", sb_[i % 2], st[i % 2])
            self.b.s.dma("sp", [(self.tabb_d.ap[i], sb_[i % 2].ap)], reads=[sb_[i % 2].tok],
                         writes=[self.tabb_d.tok], partial=True)

    def norm_phase(self, xsrc, ccol):
        b = self.b
        A = Arena(self.arena, R_PHASE)
        xt = [A([128, 2048], F32, "nxt%d" % i) for i in range(2)]
        junk = A([128, 2048], BF16, "njunk")
        xn = A([128, 4, 2048], BF16, "nxn")
        sm = [[A([128, 1], F32) for _ in range(4)] for _ in range(2)]
        for g in range(4):
            for j in range(4):
                tt = g * 4 + j
                x_t = xt[tt % 2]
                s1, s2, s3, s4 = sm[tt % 2]
                b.dma("sp", x_t, xsrc[tt * 128:(tt + 1) * 128, :])
                b.act(junk, x_t, AF.Square, accum=s1)
                b.ts("dve", s2, s1, 1.0 / D, EPS, ALU.mult, ALU.add)
                b.act(s3, s2, AF.Sqrt)
                b.recip(s4, s3)
                b.act(xn[:, j, :], x_t, AF.Copy, scale=s4)
            for c in range(16):
                pb = self.PB[c % 2]
                for j in range(4):
                    b.tr(pb[:, j * 128:(j + 1) * 128], xn[:, j, c * 128:(c + 1) * 128], self.ident)
                dst = self.hT[:, c, g * 512:(g + 1) * 512]
                sc = self.cols[:, ccol + c:ccol + c + 1]
                b.act(dst, pb, AF.Copy, scale=sc)

    def hgrn_phase(self):
        b = self.b
        A = Arena(self.arena, R_PHASE)
        qf = A([128, 2048], F32, "qf")
        vT = A([128, 2048], BF16, "vT")
        V = A([128, 16, 128], BF16, "V")
        sg = A([128, 2048], BF16, "sg")
        qin = A([128, 2048], BF16, "qin")
        kin = A([128, 2048], BF16, "kin")
        kend = A([128, 16, 128], BF16, "kend")
        kendT = [A([128, 512], BF16, "kendT%d" % i) for i in range(2)]
        Sbf = A([128, 64, 128], BF16, "Sbf")
        Of = A([128, 2048], F32, "Of")
        tsets = [[A([128, 512], F32, n + str(i)) for n in ("sn", "lf", "bb", "E", "En", "kinf")] for i in range(2)]
        qcnt = [0]
        Elast = A([128, 64], F32, "Elast")
        S32 = [A([128, 128], F32, "S32_%d" % i) for i in range(2)]
        Am = [A([128, 128], BF16, "Am%d" % i) for i in range(2)]
        ot, rs1, rs2 = tsets[0][0], tsets[0][1], tsets[0][2]
        osq = T(tsets[0][3].ap.bitcast(BF16)[:, 0:512], tok=tsets[0][3].tok)
        yT = A([128, 2048], BF16, "yTt")
        PS = self.PS
        r3 = lambda t: t.re("p (c k) -> p c k", k=32)
        for hd in range(16):
            oml = self.lb3[:, 1, hd:hd + 1]
            noml = self.lb3[:, 2, hd:hd + 1]
            wq = self.load_w(self.w_in0[0 * 16 + hd])
            self.proj(wq, lambda q, p: b.copy("act", qf[:, q * 512:(q + 1) * 512], p))
            wv = self.load_w(self.w_in0[1 * 16 + hd])
            self.proj(wv, lambda q, p: b.copy("dve", vT[:, q * 512:(q + 1) * 512], p))
            for q in range(4):
                pb = self.PB[q % 2]
                for j in range(4):
                    b.tr(pb[:, j * 128:(j + 1) * 128], vT[:, (4 * q + j) * 128:(4 * q + j + 1) * 128], self.ident)
                b.copy("act" if q % 2 else "dve", V[:, 4 * q:4 * q + 4, :], pb.re("p (j k) -> p j k", k=128))
            wg = self.load_w(self.w_in0[4 * 16 + hd])
            self.proj(wg, lambda q, p: b.act(sg[:, q * 512:(q + 1) * 512], p, AF.Silu))
            for dr in (0, 1):
                wz = self.load_w(self.w_in0[(2 + dr) * 16 + hd])

                def partB(q, ts_, dr=dr):
                    qs = slice(q * 512, (q + 1) * 512)
                    sn, lf, bb, E, En, kinf = ts_
                    b.scan(bb, self.rst, lf, 0.0)
                    if dr == 1:
                        b.tt("dve", r3(En), r3(bb)[:, :, 31:32].bc([128, 16, 32]), r3(bb), ALU.subtract)
                        b.tt("dve", bb, En, lf, ALU.add)
                    b.act(E, bb, AF.Exp)
                    b.act(En, bb, AF.Exp, scale=-1.0)
                    b.tt("dve", qin[:, qs], qf[:, qs], E, ALU.mult)
                    b.stt(kinf, sn, oml, En, ALU.mult, ALU.mult)
                    b.copy("act", kin[:, qs], kinf)
                    edge = r3(E)[:, :, 31:32] if dr == 0 else r3(E)[:, :, 0:1]
                    kT = kendT[q % 2]
                    b.tt("dve", r3(kT), r3(kinf), edge.bc([128, 16, 32]), ALU.mult)
                    b.copy("dve", Elast[:, 16 * q:16 * q + 16].v(lambda a: a.unsqueeze(2)), edge)
                    pb = self.PB[q % 2]
                    for j in range(4):
                        b.tr(pb[:, j * 128:(j + 1) * 128], kT[:, j * 128:(j + 1) * 128], self.ident)
                    b.copy("act", kend[:, 4 * q:4 * q + 4, :], pb.re("p (j k) -> p j k", k=128))

                pend = []

                def cons(q, p, dr=dr, pend=pend):
                    ts_ = tsets[qcnt[0] % 2]
                    qcnt[0] += 1
                    sn, lf = ts_[0], ts_[1]
                    b.act(sn, p, AF.Exp)
                    b.act(lf, sn, AF.Ln, bias=1.0)
                    b.act(sn, lf, AF.Exp, scale=-1.0)
                    b.act(lf, sn, AF.Ln, bias=1.0, scale=noml)
                    if pend:
                        partB(*pend.pop())
                    pend.append((q, ts_))

                self.proj(wz, cons)
                partB(*pend.pop())
                order = list(range(64)) if dr == 0 else list(range(63, -1, -1))
                mask = self.maskF if dr == 0 else self.maskB
                zero_c = order[0]
                b.memset("dve", Sbf[:, zero_c, :], 0.0)
                blocks_done = [0]

                def out_block(bi, dr=dr, mask=mask, zero_c=zero_c, blocks_done=blocks_done):
                    q, bl = bi // 4, bi % 4
                    qs = slice(q * 512, (q + 1) * 512)
                    pO = self.PBf[q % 2]
                    bs = slice(bi * 128, (bi + 1) * 128)
                    pA = PS[bi % 2][:, 0:128]
                    am = Am[bi % 2]
                    b.mm(pA, kin[:, bs], qin[:, bs])
                    b.tt("dve", am, pA, mask, ALU.mult)
                    o_sl = pO[:, bl * 128:(bl + 1) * 128]
                    cl = [4 * bi + k for k in range(4) if 4 * bi + k != zero_c]
                    b.mm(o_sl, V[:, bi, :], am, start=True, stop=False)
                    for c in cl:
                        k = c % 4
                        b.mm(o_sl[:, 32 * k:32 * k + 32], Sbf[:, c, :], qin[:, c * 32:(c + 1) * 32],
                             start=False, stop=(c == cl[-1]))
                    blocks_done[0] += 1
                    if blocks_done[0] % 4 != 0:
                        return
                    if dr == 0:
                        b.copy("act", Of[:, qs], pO)
                    else:
                        b.tt("dve", ot, pO, Of[:, qs], ALU.add)
                        b.tt("pool", osq, ot, ot, ALU.mult)
                        pss = PS[q % 2]
                        b.mm(pss, self.ones_bf, osq)
                        b.act(rs2, pss, AF.Ln, bias=self.epsc, scale=1.0 / 128.0)
                        b.act(rs1, rs2, AF.Exp, scale=-0.5)
                        b.tt("dve", ot, ot, rs1, ALU.mult)
                        b.stt(yT[:, qs], ot, self.cols[:, C_HNW + hd:C_HNW + hd + 1], sg[:, qs], ALU.mult, ALU.mult)

                for idx, c in enumerate(order[:-1]):
                    tile, k = c // 4, c % 4
                    pS = PS[2 + k][:, 0:128]
                    b.mm(pS, kend[32 * k:32 * k + 32, tile, :], V[32 * k:32 * k + 32, tile, :],
                         tile_position=(32 * k, 0))
                    new, old = S32[idx % 2], S32[(idx + 1) % 2]
                    if idx == 0:
                        b.copy("dve", new, pS)
                    else:
                        b.stt(new, old, Elast[:, c:c + 1], pS, ALU.mult, ALU.add)
                    cn = order[idx + 1]
                    b.copy("act", Sbf[:, cn, :], new)
                    if dr == 0 and cn % 4 == 3:
                        out_block(cn // 4)
                    elif dr == 1 and cn % 4 == 0:
                        out_block(cn // 4)
            self.b.s.dma("act", [(self.yT_d.ap[hd * 128:(hd + 1) * 128, :], yT.ap)], reads=[yT.tok],
                         writes=[self.yT_d.tok], partial=True)

    def fnet_phase(self):
        b = self.b
        PS = self.PS
        A = Arena(self.arena, R_PHASE)
        uT = A([128, 2, 2048], BF16, "uT")
        sgB = A([128, 2, 2048], BF16, "sgB")
        UW = A([128, 16, 2, 256], BF16, "UW")
        tabs = [[A([128, 8, 512], BF16, "tab%d_%d" % (s, w)) for w in range(2)] for s in range(2)]
        fwst = A([128, 2, 256], F32, "fwst")
        fwb = A([128, 2, 256], BF16, "fwb")
        W12 = A([128, 2, 2, 256], BF16, "W12")
        yo = [A([128, 512], BF16, "fyo%d" % i) for i in range(2)]
        slot_i = 0
        for g in range(8):
            for co in range(2):
                wb = self.load_w(self.w_in0[80 + 2 * g + co])
                self.proj(wb, lambda q, p, co=co: b.copy("act", uT[:, co, q * 512:(q + 1) * 512], p))
            for co in range(2):
                wb = self.load_w(self.w_in0[96 + 2 * g + co])
                self.proj(wb, lambda q, p, co=co: b.act(sgB[:, co, q * 512:(q + 1) * 512], p, AF.Silu))
            b.dma("sp", fwst, self.fw_d[g])
            b.copy("dve", fwb, fwst)
            for co in range(2):
                p = PS[2]
                for w in range(2):
                    for k in range(2):
                        b.mm(p[:, w * 256:(w + 1) * 256], self.cc[w][:, k, co * 128:(co + 1) * 128], fwb[:, k, :],
                             start=(k == 0), stop=(k == 1))
                b.copy("act", W12[:, co, 0, :], p[:, 0:256])
                b.ts("dve", W12[:, co, 1, :], p[:, 256:512], -1.0, None, ALU.mult)
            for st in range(16):
                p = PS[2 + st % 2]
                for co in range(2):
                    b.mm(p, uT[:, co, st * 128:(st + 1) * 128], W12[:, co, :, :].re("p w d -> p (w d)"),
                         start=(co == 0), stop=(co == 1))
                b.copy("act" if st % 2 else "dve", UW[:, st, :, :], p.re("p (w d) -> p w d", w=2))
            for tt in range(4):
                banks = (PS[2], PS[3]) if tt % 2 == 0 else (PS[4], PS[5])
                for half in range(2):
                    slot = tabs[slot_i % 2]
                    slot_i += 1
                    for w in range(2):
                        b.dma("sp", slot[w], self.tabb_d[w * 8 + tt * 2 + half])
                    for dblk in range(2):
                        p = banks[dblk]
                        for j in range(8):
                            st = half * 8 + j
                            for w in range(2):
                                b.mm(p, UW[:, st, w, dblk * 128:(dblk + 1) * 128], slot[w][:, j, :],
                                     start=(half == 0 and j == 0 and w == 0), stop=(half == 1 and j == 7 and w == 1))
                for dblk in range(2):
                    y = yo[dblk]
                    ch = 2 * g + dblk
                    b.stt(y, banks[dblk], self.cols[:, C_FB + ch:C_FB + ch + 1], sgB[:, dblk, tt * 512:(tt + 1) * 512],
                          ALU.add, ALU.mult)
                    r0 = 2048 + ch * 128
                    self.b.s.dma("act", [(self.yT_d.ap[r0:r0 + 128, tt * 512:(tt + 1) * 512], y.ap)], reads=[y.tok],
                                 writes=[self.yT_d.tok], partial=True)

    def outproj_phase(self, wout, xsrc, xdst, split):
        b = self.b
        PS = self.PS
        A = Arena(self.arena, R_BIG)
        wsl = [A([128, 32, 512], BF16, "wsl%d" % i) for i in range(2)]
        ysl = [A([128, 32, 512], BF16, "ysl%d" % i) for i in range(2)]
        wst = [A([128, 4, 512], F32, "owst%d" % i) for i in range(2)]
        xr = [A([128, 512], F32, "xr%d" % i) for i in range(2)]
        xo = [A([128, 512], F32, "xo%d" % i) for i in range(2)]
        if split:
            b.ts("dve", self.rstd, self.ssq, 1.0 / 2048.0, EPS, ALU.mult, ALU.add)
            b.act(self.rstd, self.rstd, AF.Sqrt)
            b.recip(self.rstd, self.rstd)
        yv = self.yT_d.re("(kc p) t -> p kc t", p=128)
        cnt = 0
        for dt_ in range(4):
            w = wsl[dt_ % 2]
            for pc in range(8):
                st = wst[pc % 2]
                b.dma("sp", st, wout[dt_][:, pc * 4:(pc + 1) * 4, :])
                b.copy("pool", w[:, pc * 4:(pc + 1) * 4, :], st)
            for tq in range(4):
                ys = ysl[(dt_ * 4 + tq) % 2]
                self.b.s.dma("sp", [(ys.ap[:, i * 8:(i + 1) * 8, :], yv.ap[:, i * 8:(i + 1) * 8, tq * 512:(tq + 1) * 512])
                                    for i in range(4)], reads=[self.yT_d.tok], writes=[ys.tok])
                for j in range(4):
                    tt = tq * 4 + j
                    ts_ = slice(j * 128, (j + 1) * 128)
                    pa = PS[cnt % 3]
                    pbk = PS[3 + cnt % 3]
                    x_r = xr[cnt % 2]
                    x_o = xo[cnt % 2]
                    cnt += 1
                    b.dma("sp", x_r, xsrc[tt * 128:(tt + 1) * 128, dt_ * 512:(dt_ + 1) * 512])
                    if not split:
                        for kc in range(32):
                            b.mm(pa, ys[:, kc, ts_], w[:, kc, :], start=(kc == 0), stop=(kc == 31))
                        b.tt("dve", x_o, pa, x_r, ALU.add)
                    else:
                        for kc in range(16):
                            b.mm(pa, ys[:, kc, ts_], w[:, kc, :], start=(kc == 0), stop=(kc == 15))
                        for kc in range(16, 32):
                            b.mm(pbk, ys[:, kc, ts_], w[:, kc, :], start=(kc == 16), stop=(kc == 31))
                        b.tt("dve", x_o, pa, x_r, ALU.add)
                        b.stt(x_o, pbk, self.rstd[:, tt:tt + 1], x_o, ALU.mult, ALU.add)
                    self.b.s.dma("act", [(xdst.ap[tt * 128:(tt + 1) * 128, dt_ * 512:(dt_ + 1) * 512], x_o.ap)],
                                 reads=[x_o.tok], writes=[xdst.tok], partial=True)

    def final_norm(self, xsrc, out):
        b = self.b
        A = Arena(self.arena, R_BIG)
        fnwb = A([128, 2048], F32, "fnwb")
        xt = [A([128, 2048], F32, "fxt%d" % i) for i in range(2)]
        xo = [A([128, 2048], F32, "fxo%d" % i) for i in range(2)]
        junk = A([128, 2048], BF16, "fjunk")
        sm = [[A([128, 1], F32) for _ in range(4)] for _ in range(2)]
        b.dma("sp", fnwb, self.fnw_d.bc([128, 2048]))
        for tt in range(16):
            x_t, x_o = xt[tt % 2], xo[tt % 2]
            s1, s2, s3, s4 = sm[tt % 2]
            b.dma("sp", x_t, xsrc[tt * 128:(tt + 1) * 128, :])
            b.act(junk, x_t, AF.Square, accum=s1)
            b.ts("dve", s2, s1, 1.0 / D, EPS, ALU.mult, ALU.add)
            b.act(s3, s2, AF.Sqrt)
            b.recip(s4, s3)
            b.stt(x_o, x_t, s4, fnwb, ALU.mult, ALU.mult)
            self.b.s.dma("act", [(out.ap[tt * 128:(tt + 1) * 128, :], x_o.ap)], reads=[x_o.tok],
                         writes=[out.tok], partial=True, is_output=True)

    def sconv_phase(self):
        b = self.b
        A = Arena(self.arena, R_PHASE)
        cin = A([128, 2048], F32, "cin")
        upad = A([128, 2050], F32, "upad")
        ya = A([128, 2048], F32, "ya")
        yb = A([128, 2048], F32, "yb")
        sgc = [A([128, 512], F32, "sgc%d" % i) for i in range(2)]
        yo = A([128, 2048], BF16, "syo")
        b.memset("dve", upad[:, 0:1], 0.0)
        b.memset("dve", upad[:, 2049:2050], 0.0)
        for cb in range(16):
            w0 = self.cols[:, C_SW + cb:C_SW + cb + 1]
            w1 = self.cols[:, C_SW + 16 + cb:C_SW + 16 + cb + 1]
            w2 = self.cols[:, C_SW + 32 + cb:C_SW + 32 + cb + 1]
            wb = self.load_w(self.w_in1[cb])
            self.proj(wb, lambda q, p: b.copy("act", cin[:, q * 512:(q + 1) * 512], p))
            wb = self.load_w(self.w_in1[32 + cb])
            self.proj(wb, lambda q, p: b.tt("dve", upad[:, 1 + q * 512:1 + (q + 1) * 512], p,
                                            cin[:, q * 512:(q + 1) * 512], ALU.mult))
            b.ts("dve", ya, upad[:, 1:2049], w1, None, ALU.mult)
            b.stt(yb, upad[:, 0:2048], w0, ya, ALU.mult, ALU.add)
            b.stt(ya, upad[:, 2:2050], w2, yb, ALU.mult, ALU.add)
            wb = self.load_w(self.w_in1[16 + cb])
            self.proj(wb, lambda q, p: b.tt("dve", yb[:, q * 512:(q + 1) * 512], p, ya[:, q * 512:(q + 1) * 512],
                                            ALU.mult))
            wb = self.load_w(self.w_in1[48 + cb])

            def cons(q, p):
                s = sgc[q % 2]
                b.act(s, p, AF.Silu)
                b.tt("dve", yo[:, q * 512:(q + 1) * 512], yb[:, q * 512:(q + 1) * 512], s, ALU.mult)

            self.proj(wb, cons)
            self.b.s.dma("act", [(self.yT_d.ap[cb * 128:(cb + 1) * 128, :], yo.ap)], reads=[yo.tok],
                         writes=[self.yT_d.tok], partial=True)

    def conv_block(self, wsrc, wcol, out, upad, yq):
        b = self.b
        w0 = self.cols[:, C_CW + wcol:C_CW + wcol + 1]
        w1 = self.cols[:, C_CW + 24 + wcol:C_CW + 24 + wcol + 1]
        w2 = self.cols[:, C_CW + 48 + wcol:C_CW + 48 + wcol + 1]
        cbias = self.cols[:, C_CB + wcol:C_CB + wcol + 1]
        wb = self.load_w(wsrc)
        self.proj(wb, lambda q, p: b.copy("act", upad[:, 1 + q * 512:1 + (q + 1) * 512], p))
        for q in range(4):
            ya, yb = yq[q % 2]
            b.ts("dve", ya, upad[:, 1 + q * 512:1 + (q + 1) * 512], w1, None, ALU.mult)
            b.stt(yb, upad[:, q * 512:(q + 1) * 512], w0, ya, ALU.mult, ALU.add)
            b.stt(ya, upad[:, 2 + q * 512:2 + (q + 1) * 512], w2, yb, ALU.mult, ALU.add)
            b.act(out[:, q * 512:(q + 1) * 512], ya, AF.Silu, bias=cbias)

    def ssd_phase(self):
        b = self.b
        PS = self.PS
        A = Arena(self.arena, R_PHASE)
        dt = A([128, 16, 64], F32, "dt")
        dta = A([128, 16, 64], F32, "dta")
        A0 = Arena(self.arena, (A.p, ARENA_BYTES))
        wdst = A0([128, 16, 64], F32, "wdst")
        wdtb = A0([128, 16, 64], BF16, "wdtb")
        xsT = A([128, 2, 2048], BF16, "xsT")
        xstok = A([128, 16, 256], BF16, "xstok")
        szT = A([128, 2, 2048], BF16, "szT")
        yacc = A([128, 16, 256], F32, "yacc")
        BT = A([128, 2048], BF16, "BT")
        CT = A([128, 2048], BF16, "CT")
        Btok = A([128, 16, 128], BF16, "Btok")
        upad = A([128, 2050], F32, "upad")
        yq = [(A([128, 512], F32, "cya%d" % i), A([128, 512], F32, "cyb%d" % i)) for i in range(2)]
        class CS:
            pass
        sets = []
        for i in range(2):
            c = CS()
            c.X = A([128, 512], F32, "X%d" % i)
            c.Dm = A([128, 512], BF16, "Dm%d" % i)
            c.Wm = A([128, 512], BF16, "Wm%d" % i)
            c.cb = A([128, 128], F32, "cb%d" % i)
            c.xdt = A([128, 256], BF16, "xdt%d" % i)
            c.xdtw = A([128, 256], BF16, "xdtw%d" % i)
            c.tz = A([128, 256], F32, "tz%d" % i)
            c.ac, c.ei, c.wj, c.et, c.dif = [A([128, 4], F32, n + str(i)) for n in ("ac", "ei", "wj", "et", "dif")]
            sets.append(c)
        S32 = A([128, 256], F32, "S32")
        Sbf = A([128, 256], BF16, "Sbf")
        ysum = A([128, 256], F32, "ysum")
        y1 = sets[0].X
        y2 = sets[1].X
        ysq = sets[0].Dm
        yo = [sets[1].Dm, sets[1].Wm]
        pEs, pY = (PS[2], PS[5]), PS[3]
        pS = PS[4][:, 0:256]
        pQ = PS[4][:, 256:260]
        pC = PS[0][:, 0:128]
        pSm = PS[1][:, 0:8]
        pT = PS[5]
        b.memset("dve", upad[:, 0:1], 0.0)
        b.memset("dve", upad[:, 2049:2050], 0.0)
        b.dma("sp", wdst, self.w_dt)
        b.copy("pool", wdtb, wdst)
        for half in range(2):
            p = PS[2 + half]
            for j in range(8):
                tt = half * 8 + j
                for kc in range(KC):
                    b.mm(p[:, j * 64:(j + 1) * 64], self.hT[:, kc, tt * 128:(tt + 1) * 128], wdtb[:, kc, :],
                         start=(kc == 0), stop=(kc == KC - 1))
            b.tt("dve", dt[:, half * 8:(half + 1) * 8, :], p.re("p (j h) -> p j h", h=64),
                 self.dtb_b.v(lambda a: a.unsqueeze(1)).bc([128, 8, 64]), ALU.add)
        b.act(dta, dt, AF.Exp)
        b.act(dt, dta, AF.Ln, bias=1.0)
        b.tt("dve", dta, dt, self.a_b.v(lambda a: a.unsqueeze(1)).bc([128, 16, 64]), ALU.mult)
        b.s.barrier()
        h4 = lambda t, d: t.re("p (h d) -> p h d", d=d)
        for u in range(8):
            g = u // 2
            for blk in range(2):
                self.conv_block(self.w_in1[80 + 2 * u + blk], 2 * u + blk, xsT[:, blk, :], upad, yq)
            for blk in range(2):
                wb = self.load_w(self.w_in1[64 + 2 * u + blk])
                self.proj(wb, lambda q, p, blk=blk: b.act(szT[:, blk, q * 512:(q + 1) * 512], p, AF.Silu))
            if u % 2 == 0:
                self.conv_block(self.w_in1[96 + g], 16 + g, BT, upad, yq)
                self.conv_block(self.w_in1[100 + g], 20 + g, CT, upad, yq)
                for q in range(4):
                    pb = self.PB[q % 2]
                    for j in range(4):
                        tt = 4 * q + j
                        b.tr(pb[:, j * 128:(j + 1) * 128], BT[:, tt * 128:(tt + 1) * 128], self.ident)
                    b.copy("act" if q % 2 else "dve", Btok[:, 4 * q:4 * q + 4, :], pb.re("p (j k) -> p j k", k=128))
            for tt in range(16):
                pb = self.PB[tt % 2]
                for blk in range(2):
                    b.tr(pb[:, blk * 128:(blk + 1) * 128], xsT[:, blk, tt * 128:(tt + 1) * 128], self.ident)
                b.copy("act" if tt % 2 else "dve", xstok[:, tt, :], pb[:, 0:256])
            for dr in range(2):
                order = list(range(16)) if dr == 0 else list(range(15, -1, -1))
                c0 = dr * 32 + 4 * u

                def fa1(idx, dr=dr, order=order, c0=c0):
                    ch = order[idx]
                    c = sets[idx % 2]
                    dta_c = dta[:, ch, c0:c0 + 4]
                    b.mm(pSm[:, 0:4], self.tri[dr], dta_c)
                    b.mm(pSm[:, 4:8], self.ones_f, dta_c)
                    b.copy("dve", c.ac, pSm[:, 0:4])
                    b.act(c.ei, c.ac, AF.Exp)
                    b.tt("dve", c.dif, pSm[:, 4:8], c.ac, ALU.subtract)
                    b.act(c.wj, c.dif, AF.Exp)
                    b.act(c.et, pSm[:, 4:8], AF.Exp)
                    b.tt("dve", h4(c.X, 128), self.tri[dr].v(lambda a: a.unsqueeze(1)).bc([128, 4, 128]),
                         dta_c.v(lambda a: a.unsqueeze(2)).bc([128, 4, 128]), ALU.mult)

                def fa2(idx, dr=dr, order=order):
                    ch = order[idx]
                    c = sets[idx % 2]
                    pE = pEs[idx % 2]
                    chs = slice(ch * 128, (ch + 1) * 128)
                    b.mm(pE, self.sgt[dr], c.X, start=True, stop=False)
                    b.mm(pE, self.ident, self.neg[dr], start=False, stop=True)
                    b.act(c.Dm, pE, AF.Exp)
                    b.mm(pC, BT[:, chs], CT[:, chs])
                    b.copy("act", c.cb, pC)

                def fb(idx, dr=dr, order=order, c0=c0):
                    ch = order[idx]
                    c = sets[idx % 2]
                    dt_c = dt[:, ch, c0:c0 + 4]
                    b.tt("dve", h4(c.Wm, 128), h4(c.Dm, 128), c.cb.v(lambda a: a.unsqueeze(1)).bc([128, 4, 128]),
                         ALU.mult)
                    b.tt("pool", h4(c.xdt, 64), h4(xstok[:, ch, :], 64),
                         dt_c.v(lambda a: a.unsqueeze(2)).bc([128, 4, 64]), ALU.mult)
                    b.tt("pool", h4(c.xdtw, 64), h4(c.xdt, 64), c.wj.v(lambda a: a.unsqueeze(2)).bc([128, 4, 64]),
                         ALU.mult)

                def bpe(idx, dr=dr, order=order):
                    ch = order[idx]
                    c = sets[idx % 2]
                    chs = slice(ch * 128, (ch + 1) * 128)
                    for hl in range(4):
                        b.mm(pY[:, hl * 64:(hl + 1) * 64], c.Wm[:, hl * 128:(hl + 1) * 128],
                             c.xdt[:, hl * 64:(hl + 1) * 64])
                    if idx > 0:
                        b.mm(pY[:, 256:512], CT[:, chs], Sbf)
                    if idx < 15:
                        b.mm(pS, Btok[:, ch, :], c.xdtw)

                def bdve(idx, dr=dr, order=order):
                    ch = order[idx]
                    c = sets[idx % 2]
                    ydst = yacc[:, ch, :]
                    if idx > 0:
                        b.tt("dve", h4(c.tz, 64), h4(pY[:, 256:512], 64),
                             c.ei.v(lambda a: a.unsqueeze(2)).bc([128, 4, 64]), ALU.mult)
                        if dr == 0:
                            b.tt("dve", ydst, pY[:, 0:256], c.tz, ALU.add)
                        else:
                            b.tt("dve", ysum, pY[:, 0:256], c.tz, ALU.add)
                            b.tt("dve", ydst, ydst, ysum, ALU.add)
                    else:
                        if dr == 0:
                            b.copy("dve", ydst, pY[:, 0:256])
                        else:
                            b.tt("dve", ydst, pY[:, 0:256], ydst, ALU.add)
                    if idx < 15:
                        if idx == 0:
                            b.copy("dve", S32, pS)
                        else:
                            b.tt("dve", h4(S32, 64), h4(S32, 64), c.et.v(lambda a: a.unsqueeze(2)).bc([128, 4, 64]),
                                 ALU.mult)
                            b.tt("dve", S32, pS, S32, ALU.add)
                        b.copy("act", Sbf, S32)

                fa1(0)
                fa2(0)
                fb(0)
                for idx in range(16):
                    nxt = idx + 1 < 16
                    if nxt:
                        fa1(idx + 1)
                    bpe(idx)
                    if nxt:
                        fa2(idx + 1)
                    bdve(idx)
                    if nxt:
                        fb(idx + 1)
            for blk in range(2):
                chb = 2 * u + blk
                for q in range(4):
                    qs = slice(q * 512, (q + 1) * 512)
                    for j in range(4):
                        tt = 4 * q + j
                        b.tr(pT[:, j * 128:(j + 1) * 128], yacc[:, tt, blk * 128:(blk + 1) * 128], self.identf)
                    b.stt(y1, xsT[:, blk, qs], self.cols[:, C_DD + chb:C_DD + chb + 1], pT, ALU.mult, ALU.add)
                    b.tt("dve", y2, y1, szT[:, blk, qs], ALU.mult)
                    b.act(ysq, y2, AF.Square)
                    for j in range(4):
                        b.mm(pQ[:, j:j + 1], ysq[:, j * 128:(j + 1) * 128], self.ones_bf[:, 0:1])
                    b.tt("dve", self.ssq[:, 4 * q:4 * q + 4], pQ, self.ssq[:, 4 * q:4 * q + 4], ALU.add)
                    y = yo[q % 2]
                    b.ts("dve", y, y2, self.cols[:, C_DNW + chb:C_DNW + chb + 1], None, ALU.mult)
                    r0 = 2048 + chb * 128
                    self.b.s.dma("act", [(self.yT_d.ap[r0:r0 + 128, qs], y.ap)], reads=[y.tok],
                                 writes=[self.yT_d.tok], partial=True)


def _colize(v):
    v = np.asarray(v, np.float32)
    v = v.reshape(-1, v.shape[-1] // 128, 128)
    return np.ascontiguousarray(v.transpose(2, 0, 1).reshape(128, -1))


_CONST_CACHE = {}


def _constants():
    if _CONST_CACHE:
        return _CONST_CACHE
    j = np.arange(128)[:, None]
    i = np.arange(128)[None, :]
    same = (j // 32) == (i // 32)
    cst = np.zeros((128, NCST), np.float32)
    cst[:, K_ID:K_ID + 128] = np.eye(128)
    cst[:, K_MF:K_MF + 128] = (same & (j <= i))
    cst[:, K_MB:K_MB + 128] = (same & (j >= i))
    cst[:, K_TF:K_TF + 128] = (j <= i)
    cst[:, K_TB:K_TB + 128] = (j >= i)
    cst[:, K_GF:K_GF + 128] = (j > i)
    cst[:, K_GB:K_GB + 128] = (j < i)
    cst[:, K_NF:K_NF + 128] = np.where(j <= i, 0.0, -30000.0)
    cst[:, K_NB:K_NB + 128] = np.where(j >= i, 0.0, -30000.0)
    rst = np.ones((128, 512), np.float32)
    rst[:, ::32] = 0.0
    cst[:, K_RST:K_RST + 512] = rst
    c = np.arange(256)
    m = (c[:, None] * c[None, :]) % 256
    ang = 2.0 * np.pi * m.astype(np.float64) / 256.0
    scale = 1.0 / np.sqrt(2048.0 * 256.0)
    for w, fn in enumerate((np.cos, np.sin)):
        tab = (fn(ang) * scale).astype(np.float32)
        cst[:, K_CC + w * 512:K_CC + (w + 1) * 512] = tab.reshape(2, 128, 256).transpose(1, 0, 2).reshape(128, 512)
    s = np.arange(2048, dtype=np.int64)
    m = (s[:, None] * s[None, :]) % 2048
    ang = 2.0 * np.pi * m.astype(np.float64) / 2048.0
    tabf = np.zeros((2, 4, 2, 128, 8, 512), np.float32)
    for w, fn in enumerate((np.cos, np.sin)):
        tab = fn(ang).astype(np.float32)
        tabf[w] = tab.reshape(2, 8, 128, 4, 512).transpose(3, 0, 2, 1, 4)
    _CONST_CACHE["cst"] = cst
    _CONST_CACHE["tabf"] = np.ascontiguousarray(tabf.reshape(16, 128, 8, 512))
    return _CONST_CACHE


def prepare_inputs(x, norm_w, final_norm_w, ev_w_in, ev_w_out, hgrn_lb_logits, hgrn_norm_w,
                   fnet_w, fnet_b, od_w_in, od_w_out, sconv_w, ssd_conv_w, ssd_conv_b,
                   ssd_dt_bias, ssd_a_log, ssd_d, ssd_norm_w):
    f = lambda a: np.asarray(a, np.float32)
    cs = _constants()
    w0 = f(ev_w_in)[0]
    w_in0 = np.ascontiguousarray(w0.reshape(16, 128, 112, 128).transpose(2, 1, 0, 3))
    w1 = f(od_w_in)[0]
    w_in1 = np.ascontiguousarray(w1[:, :13312].reshape(16, 128, 104, 128).transpose(2, 1, 0, 3))
    w_dt = np.ascontiguousarray(w1[:, 13312:13376].reshape(16, 128, 64).transpose(1, 0, 2))
    wo = lambda w: np.ascontiguousarray(f(w)[0].reshape(32, 128, 4, 512).transpose(2, 1, 0, 3))
    cols = np.concatenate([
        _colize(f(norm_w)[0:1]), _colize(f(norm_w)[1:2]), _colize(f(hgrn_lb_logits)),
        _colize(f(hgrn_norm_w)[0:1]), _colize(f(fnet_b)[0:1]), _colize(f(sconv_w)[0]),
        _colize(f(ssd_conv_w)[0]), _colize(f(ssd_conv_b)[0:1]),
        _colize(np.repeat(f(ssd_d)[0], 64)[None, :]), _colize(f(ssd_norm_w)[0:1])], axis=1)
    assert cols.shape == (128, NCOLS), cols.shape
    shared = {
        "w_in0": w_in0, "w_out0": wo(ev_w_out), "w_in1": w_in1, "w_dt": w_dt, "w_out1": wo(od_w_out),
        "cols": np.ascontiguousarray(cols), "cst": cs["cst"], "fnw": f(final_norm_w).reshape(1, D),
        "dtb": f(ssd_dt_bias).reshape(1, 64), "alog": f(ssd_a_log).reshape(1, 64),
        "fw": np.ascontiguousarray(f(fnet_w)[0].reshape(8, 2, 128, 256).transpose(0, 2, 1, 3)),
        "tabf": cs["tabf"],
    }
    xs = f(x)
    return [dict(shared, x=np.ascontiguousarray(xs[i])) for i in range(xs.shape[0])]


def build_program(debug=False, stop_after=None):
    nc = bass.Bass("TRN2", target_bir_lowering=False)
    Builder(nc, debug=debug, stop_after=stop_after).build()
    return nc


def kernel(**inputs):
    in_maps = prepare_inputs(**inputs)
    nc = build_program()
    res = run_bass_kernel_spmd(nc, in_maps, core_ids=list(range(8)))
    return np.stack([np.asarray(r["out"], np.float32) for r in res.results], axis=0)
```

```python
from contextlib import ExitStack
import numpy as np
import concourse.bass as bass
import concourse.mybir as mybir
from concourse.bass_utils import run_bass_kernel_spmd

F32 = mybir.dt.float32
BF16 = mybir.dt.bfloat16
AF = mybir.ActivationFunctionType
ALU = mybir.AluOpType
AX = mybir.AxisListType

L = 2048
D = 2048
NT = 16
KC = 16
EPS = 1e-6

ENGS = ("pe", "act", "dve", "pool", "sp")


class Tok:
    __slots__ = ("name", "w", "r", "war")

    def __init__(self, name=""):
        self.name = name
        self.w = {}
        self.r = {}
        self.war = {}


def _merge(dst, src):
    for k, v in src.items():
        if dst.get(k, -1) < v:
            dst[k] = v


class T:
    __slots__ = ("ap", "tok")

    def __init__(self, ap, tok=None, name=""):
        self.ap = ap
        self.tok = tok if tok is not None else Tok(name)

    def __getitem__(self, idx):
        return T(self.ap[idx], self.tok)

    def v(self, fn):
        return T(fn(self.ap), self.tok)

    def bc(self, shape):
        return T(self.ap.broadcast_to(list(shape)), self.tok)

    def re(self, s, **kw):
        return T(self.ap.rearrange(s, **kw), self.tok)


def _ap(x):
    return x.ap if isinstance(x, T) else x


def _toks(*xs):
    return [x.tok for x in xs if isinstance(x, T)]


class Op:
    __slots__ = ("eng", "fn", "deps", "seq", "signal", "sigval", "dma", "dsem", "dprev", "dval")


class Sched:
    def __init__(self, nc, n_dma_sems=48):
        self.nc = nc
        self.eng_ops = {e: [] for e in ENGS}
        self.n_dma_sems = n_dma_sems
        self.dma_val = [0] * n_dma_sems
        self.dma_rr = 0
        self.out_deps = {}

    def _deps(self, reads, writes, partial):
        deps = {}
        for t in reads:
            _merge(deps, t.w)
        for t in writes:
            _merge(deps, t.r)
            _merge(deps, t.war)
            if not partial:
                _merge(deps, t.w)
        return deps

    def _update(self, key, val, reads, writes, partial):
        me = {key: val}
        for t in reads:
            _merge(t.r, me)
        for t in writes:
            if t.r:
                t.war = t.r
                t.r = {}
                t.w = dict(me)
            elif partial:
                _merge(t.w, me)
            else:
                t.w = dict(me)

    def op(self, eng, fn, reads=(), writes=(), partial=False):
        o = Op()
        o.eng = eng
        o.fn = fn
        o.dma = None
        o.signal = False
        o.deps = self._deps(reads, writes, partial)
        lst = self.eng_ops[eng]
        lst.append(o)
        o.seq = len(lst)
        self._update(eng, o.seq, reads, writes, partial)
        return o

    def dma(self, queue, parts, reads=(), writes=(), partial=False, is_output=False):
        o = Op()
        o.eng = queue
        o.fn = None
        o.dma = parts
        o.signal = False
        o.deps = self._deps(reads, writes, partial)
        s = self.dma_rr
        self.dma_rr = (self.dma_rr + 1) % self.n_dma_sems
        o.dsem = s
        o.dprev = self.dma_val[s]
        self.dma_val[s] += 16 * len(parts)
        o.dval = self.dma_val[s]
        lst = self.eng_ops[queue]
        lst.append(o)
        o.seq = len(lst)
        self._update(("d", s), o.dval, reads, writes, partial)
        if is_output:
            _merge(self.out_deps, {("d", s): o.dval})
        return o

    def barrier(self):
        deps = {e: len(self.eng_ops[e]) for e in ENGS if len(self.eng_ops[e]) > 0 and e != "sp"}
        for s in range(self.n_dma_sems):
            if self.dma_val[s] > 0:
                deps[("d", s)] = self.dma_val[s]
        for e in ENGS:
            o = Op()
            o.eng = e
            o.fn = None
            o.dma = None
            o.signal = False
            o.deps = {k: v for k, v in deps.items() if k != e}
            lst = self.eng_ops[e]
            lst.append(o)
            o.seq = len(lst)

    def emit(self, es):
        nc = self.nc
        for e in ENGS:
            for o in self.eng_ops[e]:
                nd = {}
                for k, v in o.deps.items():
                    if isinstance(k, str):
                        if k == "pe" and e == "pe":
                            continue
                        while v > 0 and self.eng_ops[k][v - 1].fn is None and self.eng_ops[k][v - 1].dma is None:
                            v -= 1
                        if v == 0:
                            continue
                        p = self.eng_ops[k][v - 1]
                        if p.dma is not None:
                            _merge(nd, {("d", p.dsem): p.dval})
                        else:
                            p.signal = True
                            _merge(nd, {k: v})
                    else:
                        _merge(nd, {k: v})
                o.deps = nd
        for e in ENGS:
            c = 0
            for o in self.eng_ops[e]:
                if o.signal:
                    c += 1
                    o.sigval = c
        esem = {e: es.enter_context(nc.semaphore("s_" + e)) for e in ENGS}
        dsems = [es.enter_context(nc.semaphore("d%d" % i)) for i in range(self.n_dma_sems)]
        block = es.enter_context(nc.Block())
        sched = self

        def run(en, e):
            waited = {}
            for o in sched.eng_ops[en]:
                for k, v in o.deps.items():
                    if isinstance(k, str):
                        kk = k
                        tgt = sched.eng_ops[k][v - 1].sigval
                        sem = esem[k]
                    else:
                        kk = k
                        tgt = v
                        sem = dsems[k[1]]
                    if waited.get(kk, 0) >= tgt:
                        continue
                    waited[kk] = tgt
                    e.wait_ge(sem, tgt)
                if o.dma is not None:
                    kk = ("d", o.dsem)
                    if o.dprev > 0 and waited.get(kk, 0) < o.dprev:
                        e.wait_ge(dsems[o.dsem], o.dprev)
                        waited[kk] = o.dprev
                    for (oa, ia) in o.dma:
                        e.dma_start(out=oa, in_=ia).then_inc(dsems[o.dsem], 16)
                elif o.fn is not None:
                    inst = o.fn(e)
                    if o.signal:
                        inst.then_inc(esem[en], 1)
            if en == "sp":
                for k, v in sched.out_deps.items():
                    if waited.get(k, 0) < v:
                        e.wait_ge(dsems[k[1]], v)

        @block.sync
        def _(e):
            run("sp", e)

        @block.scalar
        def _(e):
            run("act", e)

        @block.vector
        def _(e):
            run("dve", e)

        @block.gpsimd
        def _(e):
            run("pool", e)

        @block.tensor
        def _(e):
            run("pe", e)


class B:
    def __init__(self, nc):
        self.nc = nc
        self.s = Sched(nc)

    def mm(self, out, lhsT, rhs, start=True, stop=True, tile_position=None):
        o, a, b = _ap(out), _ap(lhsT), _ap(rhs)
        kw = {}
        if tile_position is not None:
            kw["tile_position"] = tile_position
        self.s.op("pe", lambda e: e.matmul(o, lhsT=a, rhs=b, start=start, stop=stop, **kw),
                  reads=_toks(lhsT, rhs), writes=_toks(out), partial=True)

    def tr(self, out, in_, ident):
        o, a, i = _ap(out), _ap(in_), _ap(ident)
        self.s.op("pe", lambda e: e.transpose(o, a, i), reads=_toks(in_, ident), writes=_toks(out), partial=True)

    def act(self, out, in_, func, bias=None, scale=None, accum=None, extra_w=()):
        o, a = _ap(out), _ap(in_)
        kw = {}
        if bias is not None:
            kw["bias"] = _ap(bias)
        if scale is not None:
            kw["scale"] = _ap(scale)
        if accum is not None:
            kw["accum_out"] = _ap(accum)
        self.s.op("act", lambda e: e.activation(o, a, func, **kw),
                  reads=_toks(in_, bias, scale), writes=_toks(out, accum) + list(extra_w))

    def tt(self, eng, out, in0, in1, op):
        o, a, b = _ap(out), _ap(in0), _ap(in1)
        self.s.op(eng, lambda e: e.tensor_tensor(o, a, b, op), reads=_toks(in0, in1), writes=_toks(out))

    def ts(self, eng, out, in0, s1, s2=None, op0=ALU.mult, op1=None):
        o, a = _ap(out), _ap(in0)
        x1, x2 = _ap(s1), _ap(s2)
        if op1 is None:
            self.s.op(eng, lambda e: e.tensor_scalar(o, a, x1, None, op0),
                      reads=_toks(in0, s1), writes=_toks(out))
        else:
            self.s.op(eng, lambda e: e.tensor_scalar(o, a, x1, x2, op0, op1),
                      reads=_toks(in0, s1, s2), writes=_toks(out))

    def stt(self, out, in0, scalar, in1, op0, op1):
        o, a, c, b = _ap(out), _ap(in0), _ap(scalar), _ap(in1)
        self.s.op("dve", lambda e: e.scalar_tensor_tensor(o, a, c, b, op0, op1),
                  reads=_toks(in0, scalar, in1), writes=_toks(out))

    def scan(self, out, d0, d1, initial, op0=ALU.mult, op1=ALU.add):
        o, a, b, i = _ap(out), _ap(d0), _ap(d1), _ap(initial)
        self.s.op("dve", lambda e: e.tensor_tensor_scan(o, a, b, i, op0, op1),
                  reads=_toks(d0, d1, initial), writes=_toks(out))

    def copy(self, eng, out, in_):
        o, a = _ap(out), _ap(in_)
        if eng == "act":
            self.s.op("act", lambda e: e.copy(o, a), reads=_toks(in_), writes=_toks(out))
        else:
            self.s.op(eng, lambda e: e.tensor_copy(o, a), reads=_toks(in_), writes=_toks(out))

    def recip(self, out, in_):
        o, a = _ap(out), _ap(in_)
        self.s.op("dve", lambda e: e.reciprocal(o, a), reads=_toks(in_), writes=_toks(out))

    def memset(self, eng, out, val):
        o = _ap(out)
        self.s.op(eng, lambda e: e.memset(o, val), writes=_toks(out))

    def dma(self, queue, out, in_, is_output=False):
        self.s.dma(queue, [(_ap(out), _ap(in_))], reads=_toks(in_), writes=_toks(out), is_output=is_output)

    def dmas(self, queue, parts, is_output=False):
        self.s.dma(queue, [(_ap(o), _ap(i)) for o, i in parts],
                   reads=_toks(*[i for _, i in parts]), writes=_toks(*[o for o, _ in parts]),
                   is_output=is_output)


ARENA_BYTES = 206 * 1024
R_CONST = (0, 13312)
R_WEIGHT = (13312, 50176)
R_HT = (50176, 115712)
R_PHASE = (115712, ARENA_BYTES)
R_BIG = (13312, ARENA_BYTES)

C_NW0, C_NW1, C_LB, C_HNW, C_FB, C_SW, C_CW, C_CB, C_DD, C_DNW = 0, 16, 32, 80, 96, 112, 160, 232, 256, 272
NCOLS = 288
K_ID, K_MF, K_MB, K_TF, K_TB, K_GF, K_GB, K_NF, K_NB, K_RST, K_CC = 0, 128, 256, 384, 512, 640, 768, 896, 1024, 1152, 1664
NCST = 1664 + 1024


def _esize(dt):
    return 4 if dt == F32 else 2


class Arena:
    def __init__(self, arena_ap, region):
        self.a = arena_ap
        self.lo, self.hi = region
        self.p = self.lo

    def __call__(self, shape, dt=F32, name=""):
        n = 1
        for s in shape[1:]:
            n *= s
        nb = n * _esize(dt)
        nb_al = (nb + 31) // 32 * 32
        assert self.p + nb_al <= self.hi, "arena overflow %s: need %d have %d" % (name, nb_al, self.hi - self.p)
        v = self.a[0:shape[0], self.p:self.p + nb].bitcast(dt)
        self.p += nb_al
        if len(shape) == 3:
            v = v.rearrange("p (a b) -> p a b", a=shape[1])
        elif len(shape) == 4:
            v = v.rearrange("p (a b c) -> p a b c", a=shape[1], b=shape[2])
        return T(v, name=name)


class Builder:
    def __init__(self, nc, debug=False, stop_after=None):
        self.nc = nc
        self.b = B(nc)
        self.debug = debug
        self.stop_after = stop_after
        self.wi = 0
        self.pi = 0

    def din(self, name, shape, dt=F32):
        return T(self.nc.dram_tensor(name, list(shape), dt, kind="ExternalInput").ap(), name=name)

    def dscr(self, name, shape, dt=F32, out=False):
        kind = "ExternalOutput" if (out or self.debug) else "Internal"
        return T(self.nc.dram_tensor(name, list(shape), dt, kind=kind).ap(), name=name)

    def build(self):
        nc, b = self.nc, self.b
        self.x_d = self.din("x", [L, D])
        self.w_in0 = self.din("w_in0", [112, 128, 16, 128])
        self.w_out0 = self.din("w_out0", [4, 128, 32, 512])
        self.w_in1 = self.din("w_in1", [104, 128, 16, 128])
        self.w_dt = self.din("w_dt", [128, 16, 64])
        self.w_out1 = self.din("w_out1", [4, 128, 32, 512])
        self.cols_d = self.din("cols", [128, NCOLS])
        self.cst_d = self.din("cst", [128, NCST])
        self.fnw_d = self.din("fnw", [1, D])
        self.dtb_d = self.din("dtb", [1, 64])
        self.alog_d = self.din("alog", [1, 64])
        self.fw_d = self.din("fw", [8, 128, 2, 256])
        self.tabf_d = self.din("tabf", [16, 128, 8, 512])
        self.tabb_d = self.dscr("tabb", [16, 128, 8, 512], BF16)
        self.yT_d = self.dscr("yT", [4096, L], BF16)
        self.x1_d = self.dscr("x1", [L, D])
        self.x2_d = self.dscr("x2", [L, D])
        self.out_d = T(nc.dram_tensor("out", [L, D], F32, kind="ExternalOutput").ap(), name="out")
        self.arena = nc.alloc_sbuf_tensor("arena", [128, ARENA_BYTES], mybir.dt.uint8).ap()
        self.PS = [T(nc.alloc_psum_tensor("ps%d" % i, [128, 512], F32).ap(), name="ps%d" % i) for i in range(6)]
        pbs = [nc.alloc_psum_tensor("pb%d" % i, [128, 1024], BF16).ap() for i in range(2)]
        self.PB = [T(pbs[i][:, 0:512], name="pb%d" % i) for i in range(2)]
        self.PBf = [T(pbs[i].bitcast(F32), tok=self.PB[i].tok) for i in range(2)]
        C = Arena(self.arena, R_CONST)
        self.cols = C([128, NCOLS], F32, "cols")
        self.ident = C([128, 128], BF16, "ident")
        self.identf = C([128, 128], F32, "identf")
        self.ones_bf = C([128, 128], BF16, "ones_bf")
        self.ones_f = C([128, 128], F32, "ones_f")
        self.rst = C([128, 512], F32, "rst")
        self.maskF = C([128, 128], F32, "maskF")
        self.maskB = C([128, 128], F32, "maskB")
        self.tri = [C([128, 128], F32, "triF"), C([128, 128], F32, "triB")]
        self.sgt = [C([128, 128], F32, "sgtF"), C([128, 128], F32, "sgtB")]
        self.neg = [C([128, 512], BF16, "negF"), C([128, 512], BF16, "negB")]
        self.cc = [C([128, 2, 256], BF16, "ccC"), C([128, 2, 256], BF16, "ccS")]
        self.lb3 = C([128, 3, 16], F32, "lb3")
        self.dtb_b = C([128, 64], F32, "dtb_b")
        self.a_b = C([128, 64], F32, "a_b")
        self.epsc = C([128, 1], F32, "epsc")
        self.ssq = C([128, 16], F32, "ssq")
        self.rstd = C([128, 16], F32, "rstd")
        W = Arena(self.arena, R_WEIGHT)
        self.wstage = [W([128, 16, 128], F32, "wst%d" % i) for i in range(2)]
        self.wring = [W([128, 16, 128], BF16, "wr%d" % i) for i in range(5)]
        H = Arena(self.arena, R_HT)
        self.hT = H([128, 16, 2048], BF16, "hT")

        self.setup_phase()
        b.s.barrier()
        if self.stop_after == "setup":
            return self.finish()
        self.norm_phase(self.x_d, C_NW0)
        b.s.barrier()
        if self.stop_after == "norm0":
            return self.finish()
        self.hgrn_phase()
        b.s.barrier()
        if self.stop_after == "hgrn":
            return self.finish()
        self.fnet_phase()
        b.s.barrier()
        if self.stop_after == "fnet":
            return self.finish()
        self.outproj_phase(self.w_out0, self.x_d, self.x1_d, split=False)
        b.s.barrier()
        if self.stop_after == "out0":
            return self.finish()
        self.norm_phase(self.x1_d, C_NW1)
        b.s.barrier()
        self.sconv_phase()
        b.s.barrier()
        if self.stop_after == "sconv":
            return self.finish()
        self.ssd_phase()
        b.s.barrier()
        if self.stop_after == "ssd":
            return self.finish()
        self.outproj_phase(self.w_out1, self.x1_d, self.x2_d, split=True)
        b.s.barrier()
        self.final_norm(self.x2_d, self.out_d)
        return self.finish()

    def finish(self):
        if self.debug and self.stop_after in ("norm0", "hgrn", "fnet", "sconv", "ssd"):
            self.b.s.barrier()
            hd = T(self.nc.dram_tensor("dbg_hT", [128, 16, 2048], BF16, kind="ExternalOutput").ap())
            self.b.dma("sp", hd, self.hT, is_output=True)
        if self.stop_after is not None:
            A = Arena(self.arena, R_PHASE)
            t = A([128, 512], F32, "fin")
            self.b.s.barrier()
            self.b.memset("dve", t, 0.0)
            self.b.dma("sp", self.out_d[0:128, 0:512], t, is_output=True)
        self._es = ExitStack()
        self.b.s.emit(self._es)
        self._es.close()
        return self.nc

    def load_w(self, src):
        b = self.b
        st = self.wstage[self.wi % 2]
        wb = self.wring[self.wi % len(self.wring)]
        self.wi += 1
        b.dma("sp", st, src)
        b.copy("pool", wb, st)
        return wb

    def proj(self, wb, consumer):
        b = self.b
        for q in range(4):
            p = self.PS[self.pi % 2]
            self.pi += 1
            for kc in range(KC):
                b.mm(p, wb[:, kc, :], self.hT[:, kc, q * 512:(q + 1) * 512], start=(kc == 0), stop=(kc == KC - 1))
            consumer(q, p)

    def setup_phase(self):
        b = self.b
        A = Arena(self.arena, R_PHASE)
        cst = A([128, NCST], F32, "cst")
        b.dma("sp", self.cols, self.cols_d)
        b.dma("sp", cst, self.cst_d)
        b.dma("sp", self.dtb_b, self.dtb_d.bc([128, 64]))
        al = A([128, 64], F32, "al")
        b.dma("sp", al, self.alog_d.bc([128, 64]))
        b.copy("dve", self.ident, cst[:, K_ID:K_ID + 128])
        b.copy("dve", self.identf, cst[:, K_ID:K_ID + 128])
        b.memset("dve", self.ones_bf, 1.0)
        b.memset("dve", self.ones_f, 1.0)
        b.copy("dve", self.rst, cst[:, K_RST:K_RST + 512])
        b.copy("dve", self.maskF, cst[:, K_MF:K_MF + 128])
        b.copy("dve", self.maskB, cst[:, K_MB:K_MB + 128])
        b.copy("dve", self.tri[0], cst[:, K_TF:K_TF + 128])
        b.copy("dve", self.tri[1], cst[:, K_TB:K_TB + 128])
        b.copy("dve", self.sgt[0], cst[:, K_GF:K_GF + 128])
        b.copy("dve", self.sgt[1], cst[:, K_GB:K_GB + 128])
        for d, k in ((0, K_NF), (1, K_NB)):
            b.copy("dve", self.neg[d].re("p (h i) -> p h i", h=4),
                   cst[:, k:k + 128].v(lambda a: a.unsqueeze(1)).bc([128, 4, 128]))
        b.copy("dve", self.cc[0], cst[:, K_CC:K_CC + 512].re("p (k d) -> p k d", k=2))
        b.copy("dve", self.cc[1], cst[:, K_CC + 512:K_CC + 1024].re("p (k d) -> p k d", k=2))
        b.act(al, al, AF.Exp)
        b.ts("dve", self.a_b, al, -1.0, None, ALU.mult)
        l0 = self.cols[:, C_LB:C_LB + 16]
        l1 = self.cols[:, C_LB + 16:C_LB + 32]
        l2 = self.cols[:, C_LB + 32:C_LB + 48]
        t1 = A([128, 16], F32, "t1")
        t2 = A([128, 16], F32, "t2")
        b.tt("dve", t1, l1, l0, ALU.subtract)
        b.tt("dve", t2, l2, l0, ALU.subtract)
        b.act(t1, t1, AF.Exp)
        b.act(t2, t2, AF.Exp)
        b.tt("dve", t1, t1, t2, ALU.add)
        b.ts("dve", t1, t1, 1.0, None, ALU.add)
        b.recip(self.lb3[:, 0, :], t1)
        b.ts("dve", self.lb3[:, 1, :], self.lb3[:, 0, :], -1.0, 1.0, ALU.mult, ALU.add)
        b.ts("dve", self.lb3[:, 2, :], self.lb3[:, 0, :], -1.0, None, ALU.add)
        b.memset("dve", self.ssq, 0.0)
        b.memset("dve", self.epsc, EPS)
        st = [A([128, 8, 512], F32, "tst%d" % i) for i in range(2)]
        sb_ = [A([128, 8, 512], BF16, "tsb%d" % i) for i in range(2)]
        for i in range(16):
            b.dma("sp", st[i % 2], self.tabf_d[i])
            b.copy("pool" if i % 2 == 0 else "act", sb_[i % 2], st[i % 2])
            self.b.s.dma("sp", [(self.tabb_d.ap[i], sb_[i % 2].ap)], reads=[sb_[i % 2].tok],
                         writes=[self.tabb_d.tok], partial=True)

    def norm_phase(self, xsrc, ccol):
        b = self.b
        A = Arena(self.arena, R_PHASE)
        xt = [A([128, 2048], F32, "nxt%d" % i) for i in range(2)]
        junk = A([128, 2048], BF16, "njunk")
        xn = A([128, 4, 2048], BF16, "nxn")
        sm = [[A([128, 1], F32) for _ in range(4)] for _ in range(2)]
        for g in range(4):
            for j in range(4):
                tt = g * 4 + j
                x_t = xt[tt % 2]
                s1, s2, s3, s4 = sm[tt % 2]
                b.dma("sp", x_t, xsrc[tt * 128:(tt + 1) * 128, :])
                b.act(junk, x_t, AF.Square, accum=s1)
                b.ts("dve", s2, s1, 1.0 / D, EPS, ALU.mult, ALU.add)
                b.act(s3, s2, AF.Sqrt)
                b.recip(s4, s3)
                b.act(xn[:, j, :], x_t, AF.Copy, scale=s4)
            for c in range(16):
                pb = self.PB[c % 2]
                for j in range(4):
                    b.tr(pb[:, j * 128:(j + 1) * 128], xn[:, j, c * 128:(c + 1) * 128], self.ident)
                dst = self.hT[:, c, g * 512:(g + 1) * 512]
                sc = self.cols[:, ccol + c:ccol + c + 1]
                b.act(dst, pb, AF.Copy, scale=sc)

    def hgrn_phase(self):
        b = self.b
        A = Arena(self.arena, R_PHASE)
        qf = A([128, 2048], F32, "qf")
        vT = A([128, 2048], BF16, "vT")
        V = A([128, 16, 128], BF16, "V")
        sg = A([128, 2048], BF16, "sg")
        qin = A([128, 2048], BF16, "qin")
        kin = A([128, 2048], BF16, "kin")
        kend = A([128, 16, 128], BF16, "kend")
        kendT = [A([128, 512], BF16, "kendT%d" % i) for i in range(2)]
        Sbf = A([128, 64, 128], BF16, "Sbf")
        Of = A([128, 2048], F32, "Of")
        tsets = [[A([128, 512], F32, n + str(i)) for n in ("sn", "lf", "bb", "E", "En", "kinf")] for i in range(2)]
        qcnt = [0]
        Elast = A([128, 64], F32, "Elast")
        S32 = [A([128, 128], F32, "S32_%d" % i) for i in range(4)]
        SbfC = [T(Sbf.ap[:, c, :], name="Sbf%d" % c) for c in range(64)]
        Am = [A([128, 128], BF16, "Am%d" % i) for i in range(2)]
        ot, rs1, rs2 = tsets[0][0], tsets[0][1], tsets[0][2]
        osq = T(tsets[0][3].ap.bitcast(BF16)[:, 0:512], tok=tsets[0][3].tok)
        yT = A([128, 2048], BF16, "yTt")
        PS = self.PS
        r3 = lambda t: t.re("p (c k) -> p c k", k=32)
        for hd in range(16):
            oml = self.lb3[:, 1, hd:hd + 1]
            noml = self.lb3[:, 2, hd:hd + 1]
            wq = self.load_w(self.w_in0[0 * 16 + hd])
            self.proj(wq, lambda q, p: b.copy("act", qf[:, q * 512:(q + 1) * 512], p))
            wv = self.load_w(self.w_in0[1 * 16 + hd])
            self.proj(wv, lambda q, p: b.copy("dve", vT[:, q * 512:(q + 1) * 512], p))
            for q in range(4):
                pb = self.PB[q % 2]
                for j in range(4):
                    b.tr(pb[:, j * 128:(j + 1) * 128], vT[:, (4 * q + j) * 128:(4 * q + j + 1) * 128], self.ident)
                b.copy("act" if q % 2 else "dve", V[:, 4 * q:4 * q + 4, :], pb.re("p (j k) -> p j k", k=128))
            wg = self.load_w(self.w_in0[4 * 16 + hd])
            self.proj(wg, lambda q, p: b.act(sg[:, q * 512:(q + 1) * 512], p, AF.Silu))
            for dr in (0, 1):
                wz = self.load_w(self.w_in0[(2 + dr) * 16 + hd])

                def partB(q, ts_, dr=dr):
                    qs = slice(q * 512, (q + 1) * 512)
                    sn, lf, bb, E, En, kinf = ts_
                    b.scan(bb, self.rst, lf, 0.0)
                    if dr == 1:
                        b.tt("dve", r3(En), r3(bb)[:, :, 31:32].bc([128, 16, 32]), r3(bb), ALU.subtract)
                        b.tt("dve", bb, En, lf, ALU.add)
                    b.act(E, bb, AF.Exp)
                    b.act(En, bb, AF.Exp, scale=-1.0)
                    b.tt("dve", qin[:, qs], qf[:, qs], E, ALU.mult)
                    b.stt(kinf, sn, oml, En, ALU.mult, ALU.mult)
                    b.copy("act", kin[:, qs], kinf)
                    edge = r3(E)[:, :, 31:32] if dr == 0 else r3(E)[:, :, 0:1]
                    kT = kendT[q % 2]
                    b.tt("dve", r3(kT), r3(kinf), edge.bc([128, 16, 32]), ALU.mult)
                    b.copy("dve", Elast[:, 16 * q:16 * q + 16].v(lambda a: a.unsqueeze(2)), edge)
                    pb = self.PB[q % 2]
                    for j in range(4):
                        b.tr(pb[:, j * 128:(j + 1) * 128], kT[:, j * 128:(j + 1) * 128], self.ident)
                    b.copy("act", kend[:, 4 * q:4 * q + 4, :], pb.re("p (j k) -> p j k", k=128))

                pend = []

                def cons(q, p, dr=dr, pend=pend):
                    ts_ = tsets[qcnt[0] % 2]
                    qcnt[0] += 1
                    sn, lf = ts_[0], ts_[1]
                    b.act(sn, p, AF.Exp)
                    b.act(lf, sn, AF.Ln, bias=1.0)
                    b.act(sn, lf, AF.Exp, scale=-1.0)
                    b.act(lf, sn, AF.Ln, bias=1.0, scale=noml)
                    if pend:
                        partB(*pend.pop())
                    pend.append((q, ts_))

                self.proj(wz, cons)
                partB(*pend.pop())
                order = list(range(64)) if dr == 0 else list(range(63, -1, -1))
                mask = self.maskF if dr == 0 else self.maskB
                zero_c = order[0]
                b.memset("dve", SbfC[zero_c], 0.0)
                blocks_done = [0]
                pend_e2 = []

                def out_block(bi, dr=dr, mask=mask, zero_c=zero_c, blocks_done=blocks_done, pend_e2=pend_e2):
                    while pend_e2:
                        pend_e2.pop()()
                    q, bl = bi // 4, bi % 4
                    qs = slice(q * 512, (q + 1) * 512)
                    pO = self.PBf[q % 2]
                    bs = slice(bi * 128, (bi + 1) * 128)
                    pA = PS[bi % 2][:, 0:128]
                    am = Am[bi % 2]
                    b.mm(pA, kin[:, bs], qin[:, bs])
                    b.tt("dve", am, pA, mask, ALU.mult)
                    o_sl = pO[:, bl * 128:(bl + 1) * 128]
                    cl = [4 * bi + k for k in range(4) if 4 * bi + k != zero_c]
                    b.mm(o_sl, V[:, bi, :], am, start=True, stop=False)
                    for c in cl:
                        k = c % 4
                        b.mm(o_sl[:, 32 * k:32 * k + 32], SbfC[c], qin[:, c * 32:(c + 1) * 32],
                             start=False, stop=(c == cl[-1]))
                    blocks_done[0] += 1
                    if blocks_done[0] % 4 != 0:
                        return
                    if dr == 0:
                        b.copy("act", Of[:, qs], pO)
                    else:
                        b.tt("dve", ot, pO, Of[:, qs], ALU.add)
                        b.tt("pool", osq, ot, ot, ALU.mult)
                        pss = PS[q % 2]
                        b.mm(pss, self.ones_bf, osq)
                        b.act(rs2, pss, AF.Ln, bias=self.epsc, scale=1.0 / 128.0)
                        b.act(rs1, rs2, AF.Exp, scale=-0.5)

                        def e2(qs=qs):
                            b.tt("dve", ot, ot, rs1, ALU.mult)
                            b.stt(yT[:, qs], ot, self.cols[:, C_HNW + hd:C_HNW + hd + 1], sg[:, qs], ALU.mult,
                                  ALU.mult)

                        pend_e2.append(e2)

                for idx, c in enumerate(order[:-1]):
                    tile, k = c // 4, c % 4
                    pS = PS[2 + k][:, 0:128]
                    b.mm(pS, kend[32 * k:32 * k + 32, tile, :], V[32 * k:32 * k + 32, tile, :],
                         tile_position=(32 * k, 0))
                    new, old = S32[idx % 4], S32[(idx + 3) % 4]
                    if idx == 0:
                        b.copy("dve", new, pS)
                    else:
                        b.stt(new, old, Elast[:, c:c + 1], pS, ALU.mult, ALU.add)
                    cn = order[idx + 1]
                    b.copy("act", SbfC[cn], new)
                    if dr == 0 and cn % 4 == 3:
                        out_block(cn // 4)
                    elif dr == 1 and cn % 4 == 0:
                        out_block(cn // 4)
                while pend_e2:
                    pend_e2.pop()()
            self.b.s.dma("act", [(self.yT_d.ap[hd * 128:(hd + 1) * 128, :], yT.ap)], reads=[yT.tok],
                         writes=[self.yT_d.tok], partial=True)

    def fnet_phase(self):
        b = self.b
        PS = self.PS
        A = Arena(self.arena, R_PHASE)
        uT = A([128, 2, 2048], BF16, "uT")
        sgB = A([128, 2, 2048], BF16, "sgB")
        UW = A([128, 16, 2, 256], BF16, "UW")
        tabs = [[A([128, 8, 512], BF16, "tab%d_%d" % (s, w)) for w in range(2)] for s in range(2)]
        fwst = A([128, 2, 256], F32, "fwst")
        fwb = A([128, 2, 256], BF16, "fwb")
        W12 = A([128, 2, 2, 256], BF16, "W12")
        yo = [A([128, 512], BF16, "fyo%d" % i) for i in range(2)]
        slot_i = 0
        for g in range(8):
            for co in range(2):
                wb = self.load_w(self.w_in0[80 + 2 * g + co])
                self.proj(wb, lambda q, p, co=co: b.copy("act", uT[:, co, q * 512:(q + 1) * 512], p))
            for co in range(2):
                wb = self.load_w(self.w_in0[96 + 2 * g + co])
                self.proj(wb, lambda q, p, co=co: b.act(sgB[:, co, q * 512:(q + 1) * 512], p, AF.Silu))
            b.dma("sp", fwst, self.fw_d[g])
            b.copy("dve", fwb, fwst)
            for co in range(2):
                p = PS[2]
                for w in range(2):
                    for k in range(2):
                        b.mm(p[:, w * 256:(w + 1) * 256], self.cc[w][:, k, co * 128:(co + 1) * 128], fwb[:, k, :],
                             start=(k == 0), stop=(k == 1))
                b.copy("act", W12[:, co, 0, :], p[:, 0:256])
                b.ts("dve", W12[:, co, 1, :], p[:, 256:512], -1.0, None, ALU.mult)
            for st in range(16):
                p = PS[2 + st % 2]
                for co in range(2):
                    b.mm(p, uT[:, co, st * 128:(st + 1) * 128], W12[:, co, :, :].re("p w d -> p (w d)"),
                         start=(co == 0), stop=(co == 1))
                b.copy("act" if st % 2 else "dve", UW[:, st, :, :], p.re("p (w d) -> p w d", w=2))
            for tt in range(4):
                banks = (PS[2], PS[3]) if tt % 2 == 0 else (PS[4], PS[5])
                for half in range(2):
                    slot = tabs[slot_i % 2]
                    slot_i += 1
                    for w in range(2):
                        b.dma("sp", slot[w], self.tabb_d[w * 8 + tt * 2 + half])
                    for dblk in range(2):
                        p = banks[dblk]
                        for j in range(8):
                            st = half * 8 + j
                            for w in range(2):
                                b.mm(p, UW[:, st, w, dblk * 128:(dblk + 1) * 128], slot[w][:, j, :],
                                     start=(half == 0 and j == 0 and w == 0), stop=(half == 1 and j == 7 and w == 1))
                for dblk in range(2):
                    y = yo[dblk]
                    ch = 2 * g + dblk
                    b.stt(y, banks[dblk], self.cols[:, C_FB + ch:C_FB + ch + 1], sgB[:, dblk, tt * 512:(tt + 1) * 512],
                          ALU.add, ALU.mult)
                    r0 = 2048 + ch * 128
                    self.b.s.dma("act", [(self.yT_d.ap[r0:r0 + 128, tt * 512:(tt + 1) * 512], y.ap)], reads=[y.tok],
                                 writes=[self.yT_d.tok], partial=True)

    def outproj_phase(self, wout, xsrc, xdst, split):
        b = self.b
        PS = self.PS
        A = Arena(self.arena, R_BIG)
        wsl = [A([128, 32, 512], BF16, "wsl%d" % i) for i in range(2)]
        ysl = [A([128, 32, 512], BF16, "ysl%d" % i) for i in range(2)]
        wst = [A([128, 4, 512], F32, "owst%d" % i) for i in range(2)]
        xr = [A([128, 512], F32, "xr%d" % i) for i in range(2)]
        xo = [A([128, 512], F32, "xo%d" % i) for i in range(2)]
        if split:
            b.ts("dve", self.rstd, self.ssq, 1.0 / 2048.0, EPS, ALU.mult, ALU.add)
            b.act(self.rstd, self.rstd, AF.Sqrt)
            b.recip(self.rstd, self.rstd)
        yv = self.yT_d.re("(kc p) t -> p kc t", p=128)
        cnt = 0
        for dt_ in range(4):
            w = wsl[dt_ % 2]
            for pc in range(8):
                st = wst[pc % 2]
                b.dma("sp", st, wout[dt_][:, pc * 4:(pc + 1) * 4, :])
                b.copy("pool", w[:, pc * 4:(pc + 1) * 4, :], st)
            for tq in range(4):
                ys = ysl[(dt_ * 4 + tq) % 2]
                self.b.s.dma("sp", [(ys.ap[:, i * 8:(i + 1) * 8, :], yv.ap[:, i * 8:(i + 1) * 8, tq * 512:(tq + 1) * 512])
                                    for i in range(4)], reads=[self.yT_d.tok], writes=[ys.tok])
                for j in range(4):
                    tt = tq * 4 + j
                    ts_ = slice(j * 128, (j + 1) * 128)
                    pa = PS[cnt % 3]
                    pbk = PS[3 + cnt % 3]
                    x_r = xr[cnt % 2]
                    x_o = xo[cnt % 2]
                    cnt += 1
                    b.dma("sp", x_r, xsrc[tt * 128:(tt + 1) * 128, dt_ * 512:(dt_ + 1) * 512])
                    if not split:
                        for kc in range(32):
                            b.mm(pa, ys[:, kc, ts_], w[:, kc, :], start=(kc == 0), stop=(kc == 31))
                        b.tt("dve", x_o, pa, x_r, ALU.add)
                    else:
                        for kc in range(16):
                            b.mm(pa, ys[:, kc, ts_], w[:, kc, :], start=(kc == 0), stop=(kc == 15))
                        for kc in range(16, 32):
                            b.mm(pbk, ys[:, kc, ts_], w[:, kc, :], start=(kc == 16), stop=(kc == 31))
                        b.tt("dve", x_o, pa, x_r, ALU.add)
                        b.stt(x_o, pbk, self.rstd[:, tt:tt + 1], x_o, ALU.mult, ALU.add)
                    self.b.s.dma("act", [(xdst.ap[tt * 128:(tt + 1) * 128, dt_ * 512:(dt_ + 1) * 512], x_o.ap)],
                                 reads=[x_o.tok], writes=[xdst.tok], partial=True)

    def final_norm(self, xsrc, out):
        b = self.b
        A = Arena(self.arena, R_BIG)
        fnwb = A([128, 2048], F32, "fnwb")
        xt = [A([128, 2048], F32, "fxt%d" % i) for i in range(2)]
        xo = [A([128, 2048], F32, "fxo%d" % i) for i in range(2)]
        junk = A([128, 2048], BF16, "fjunk")
        sm = [[A([128, 1], F32) for _ in range(4)] for _ in range(2)]
        b.dma("sp", fnwb, self.fnw_d.bc([128, 2048]))
        for tt in range(16):
            x_t, x_o = xt[tt % 2], xo[tt % 2]
            s1, s2, s3, s4 = sm[tt % 2]
            b.dma("sp", x_t, xsrc[tt * 128:(tt + 1) * 128, :])
            b.act(junk, x_t, AF.Square, accum=s1)
            b.ts("dve", s2, s1, 1.0 / D, EPS, ALU.mult, ALU.add)
            b.act(s3, s2, AF.Sqrt)
            b.recip(s4, s3)
            b.stt(x_o, x_t, s4, fnwb, ALU.mult, ALU.mult)
            self.b.s.dma("act", [(out.ap[tt * 128:(tt + 1) * 128, :], x_o.ap)], reads=[x_o.tok],
                         writes=[out.tok], partial=True, is_output=True)

    def sconv_phase(self):
        b = self.b
        A = Arena(self.arena, R_PHASE)
        cin = A([128, 2048], F32, "cin")
        upad = A([128, 2050], F32, "upad")
        ya = A([128, 2048], F32, "ya")
        yb = A([128, 2048], F32, "yb")
        sgc = [A([128, 512], F32, "sgc%d" % i) for i in range(2)]
        yo = A([128, 2048], BF16, "syo")
        b.memset("dve", upad[:, 0:1], 0.0)
        b.memset("dve", upad[:, 2049:2050], 0.0)
        for cb in range(16):
            w0 = self.cols[:, C_SW + cb:C_SW + cb + 1]
            w1 = self.cols[:, C_SW + 16 + cb:C_SW + 16 + cb + 1]
            w2 = self.cols[:, C_SW + 32 + cb:C_SW + 32 + cb + 1]
            wb = self.load_w(self.w_in1[cb])
            self.proj(wb, lambda q, p: b.copy("act", cin[:, q * 512:(q + 1) * 512], p))
            wb = self.load_w(self.w_in1[32 + cb])
            self.proj(wb, lambda q, p: b.tt("dve", upad[:, 1 + q * 512:1 + (q + 1) * 512], p,
                                            cin[:, q * 512:(q + 1) * 512], ALU.mult))
            b.ts("dve", ya, upad[:, 1:2049], w1, None, ALU.mult)
            b.stt(yb, upad[:, 0:2048], w0, ya, ALU.mult, ALU.add)
            b.stt(ya, upad[:, 2:2050], w2, yb, ALU.mult, ALU.add)
            wb = self.load_w(self.w_in1[16 + cb])
            self.proj(wb, lambda q, p: b.tt("dve", yb[:, q * 512:(q + 1) * 512], p, ya[:, q * 512:(q + 1) * 512],
                                            ALU.mult))
            wb = self.load_w(self.w_in1[48 + cb])

            def cons(q, p):
                s = sgc[q % 2]
                b.act(s, p, AF.Silu)
                b.tt("dve", yo[:, q * 512:(q + 1) * 512], yb[:, q * 512:(q + 1) * 512], s, ALU.mult)

            self.proj(wb, cons)
            self.b.s.dma("act", [(self.yT_d.ap[cb * 128:(cb + 1) * 128, :], yo.ap)], reads=[yo.tok],
                         writes=[self.yT_d.tok], partial=True)

    def conv_block(self, wsrc, wcol, out, upad, yq):
        b = self.b
        w0 = self.cols[:, C_CW + wcol:C_CW + wcol + 1]
        w1 = self.cols[:, C_CW + 24 + wcol:C_CW + 24 + wcol + 1]
        w2 = self.cols[:, C_CW + 48 + wcol:C_CW + 48 + wcol + 1]
        cbias = self.cols[:, C_CB + wcol:C_CB + wcol + 1]
        wb = self.load_w(wsrc)
        self.proj(wb, lambda q, p: b.copy("act", upad[:, 1 + q * 512:1 + (q + 1) * 512], p))
        for q in range(4):
            ya, yb = yq[q % 2]
            b.ts("dve", ya, upad[:, 1 + q * 512:1 + (q + 1) * 512], w1, None, ALU.mult)
            b.stt(yb, upad[:, q * 512:(q + 1) * 512], w0, ya, ALU.mult, ALU.add)
            b.stt(ya, upad[:, 2 + q * 512:2 + (q + 1) * 512], w2, yb, ALU.mult, ALU.add)
            b.act(out[:, q * 512:(q + 1) * 512], ya, AF.Silu, bias=cbias)

    def ssd_phase(self):
        b = self.b
        PS = self.PS
        A = Arena(self.arena, R_PHASE)
        dt = A([128, 16, 64], F32, "dt")
        dta = A([128, 16, 64], F32, "dta")
        A0 = Arena(self.arena, (A.p, ARENA_BYTES))
        wdst = A0([128, 16, 64], F32, "wdst")
        wdtb = A0([128, 16, 64], BF16, "wdtb")
        xsT = A([128, 2, 2048], BF16, "xsT")
        xstok = A([128, 16, 256], BF16, "xstok")
        szT = A([128, 2, 2048], BF16, "szT")
        yacc = A([128, 16, 256], F32, "yacc")
        BT = A([128, 2048], BF16, "BT")
        CT = A([128, 2048], BF16, "CT")
        Btok = A([128, 16, 128], BF16, "Btok")
        upad = A([128, 2050], F32, "upad")
        yq = [(A([128, 512], F32, "cya%d" % i), A([128, 512], F32, "cyb%d" % i)) for i in range(2)]
        class CS:
            pass
        sets = []
        for i in range(2):
            c = CS()
            c.X = A([128, 512], F32, "X%d" % i)
            c.Dm = A([128, 512], BF16, "Dm%d" % i)
            c.Wm = A([128, 512], BF16, "Wm%d" % i)
            c.cb = A([128, 128], F32, "cb%d" % i)
            c.xdt = A([128, 256], BF16, "xdt%d" % i)
            c.xdtw = A([128, 256], BF16, "xdtw%d" % i)
            c.tz = A([128, 256], F32, "tz%d" % i)
            c.ac, c.ei, c.wj, c.et, c.dif = [A([128, 4], F32, n + str(i)) for n in ("ac", "ei", "wj", "et", "dif")]
            sets.append(c)
        S32 = A([128, 256], F32, "S32")
        Sbf = A([128, 256], BF16, "Sbf")
        ysum = A([128, 256], F32, "ysum")
        y1 = sets[0].X
        y2 = sets[1].X
        ysq = sets[0].Dm
        yo = [sets[1].Dm, sets[1].Wm]
        pEs, pY = (PS[2], PS[5]), PS[3]
        pS = PS[4][:, 0:256]
        pQ = PS[4][:, 256:260]
        pC = PS[0][:, 0:128]
        pSm = PS[1][:, 0:8]
        pT = PS[5]
        b.memset("dve", upad[:, 0:1], 0.0)
        b.memset("dve", upad[:, 2049:2050], 0.0)
        b.dma("sp", wdst, self.w_dt)
        b.copy("pool", wdtb, wdst)
        for half in range(2):
            p = PS[2 + half]
            for j in range(8):
                tt = half * 8 + j
                for kc in range(KC):
                    b.mm(p[:, j * 64:(j + 1) * 64], self.hT[:, kc, tt * 128:(tt + 1) * 128], wdtb[:, kc, :],
                         start=(kc == 0), stop=(kc == KC - 1))
            b.tt("dve", dt[:, half * 8:(half + 1) * 8, :], p.re("p (j h) -> p j h", h=64),
                 self.dtb_b.v(lambda a: a.unsqueeze(1)).bc([128, 8, 64]), ALU.add)
        b.act(dta, dt, AF.Exp)
        b.act(dt, dta, AF.Ln, bias=1.0)
        b.tt("dve", dta, dt, self.a_b.v(lambda a: a.unsqueeze(1)).bc([128, 16, 64]), ALU.mult)
        b.s.barrier()
        h4 = lambda t, d: t.re("p (h d) -> p h d", d=d)
        for u in range(8):
            g = u // 2
            for blk in range(2):
                self.conv_block(self.w_in1[80 + 2 * u + blk], 2 * u + blk, xsT[:, blk, :], upad, yq)
            for blk in range(2):
                wb = self.load_w(self.w_in1[64 + 2 * u + blk])
                self.proj(wb, lambda q, p, blk=blk: b.act(szT[:, blk, q * 512:(q + 1) * 512], p, AF.Silu))
            if u % 2 == 0:
                self.conv_block(self.w_in1[96 + g], 16 + g, BT, upad, yq)
                self.conv_block(self.w_in1[100 + g], 20 + g, CT, upad, yq)
                for q in range(4):
                    pb = self.PB[q % 2]
                    for j in range(4):
                        tt = 4 * q + j
                        b.tr(pb[:, j * 128:(j + 1) * 128], BT[:, tt * 128:(tt + 1) * 128], self.ident)
                    b.copy("act" if q % 2 else "dve", Btok[:, 4 * q:4 * q + 4, :], pb.re("p (j k) -> p j k", k=128))
            for tt in range(16):
                pb = self.PB[tt % 2]
                for blk in range(2):
                    b.tr(pb[:, blk * 128:(blk + 1) * 128], xsT[:, blk, tt * 128:(tt + 1) * 128], self.ident)
                b.copy("act" if tt % 2 else "dve", xstok[:, tt, :], pb[:, 0:256])
            for dr in range(2):
                order = list(range(16)) if dr == 0 else list(range(15, -1, -1))
                c0 = dr * 32 + 4 * u

                def fa1(idx, dr=dr, order=order, c0=c0):
                    ch = order[idx]
                    c = sets[idx % 2]
                    dta_c = dta[:, ch, c0:c0 + 4]
                    b.mm(pSm[:, 0:4], self.tri[dr], dta_c)
                    b.mm(pSm[:, 4:8], self.ones_f, dta_c)
                    b.copy("dve", c.ac, pSm[:, 0:4])
                    b.act(c.ei, c.ac, AF.Exp)
                    b.tt("dve", c.dif, pSm[:, 4:8], c.ac, ALU.subtract)
                    b.act(c.wj, c.dif, AF.Exp)
                    b.act(c.et, pSm[:, 4:8], AF.Exp)
                    b.tt("dve", h4(c.X, 128), self.tri[dr].v(lambda a: a.unsqueeze(1)).bc([128, 4, 128]),
                         dta_c.v(lambda a: a.unsqueeze(2)).bc([128, 4, 128]), ALU.mult)

                def fa2(idx, dr=dr, order=order):
                    ch = order[idx]
                    c = sets[idx % 2]
                    pE = pEs[idx % 2]
                    chs = slice(ch * 128, (ch + 1) * 128)
                    b.mm(pE, self.sgt[dr], c.X, start=True, stop=False)
                    b.mm(pE, self.ident, self.neg[dr], start=False, stop=True)
                    b.act(c.Dm, pE, AF.Exp)
                    b.mm(pC, BT[:, chs], CT[:, chs])
                    b.copy("act", c.cb, pC)

                def fb(idx, dr=dr, order=order, c0=c0):
                    ch = order[idx]
                    c = sets[idx % 2]
                    dt_c = dt[:, ch, c0:c0 + 4]
                    b.tt("dve", h4(c.Wm, 128), h4(c.Dm, 128), c.cb.v(lambda a: a.unsqueeze(1)).bc([128, 4, 128]),
                         ALU.mult)
                    b.tt("pool", h4(c.xdt, 64), h4(xstok[:, ch, :], 64),
                         dt_c.v(lambda a: a.unsqueeze(2)).bc([128, 4, 64]), ALU.mult)
                    b.tt("pool", h4(c.xdtw, 64), h4(c.xdt, 64), c.wj.v(lambda a: a.unsqueeze(2)).bc([128, 4, 64]),
                         ALU.mult)

                def bpe(idx, dr=dr, order=order):
                    ch = order[idx]
                    c = sets[idx % 2]
                    chs = slice(ch * 128, (ch + 1) * 128)
                    for hl in range(4):
                        b.mm(pY[:, hl * 64:(hl + 1) * 64], c.Wm[:, hl * 128:(hl + 1) * 128],
                             c.xdt[:, hl * 64:(hl + 1) * 64])
                    if idx > 0:
                        b.mm(pY[:, 256:512], CT[:, chs], Sbf)
                    if idx < 15:
                        b.mm(pS, Btok[:, ch, :], c.xdtw)

                def bdve(idx, dr=dr, order=order):
                    ch = order[idx]
                    c = sets[idx % 2]
                    ydst = yacc[:, ch, :]
                    if idx > 0:
                        b.tt("dve", h4(c.tz, 64), h4(pY[:, 256:512], 64),
                             c.ei.v(lambda a: a.unsqueeze(2)).bc([128, 4, 64]), ALU.mult)
                        if dr == 0:
                            b.tt("dve", ydst, pY[:, 0:256], c.tz, ALU.add)
                        else:
                            b.tt("dve", ysum, pY[:, 0:256], c.tz, ALU.add)
                            b.tt("dve", ydst, ydst, ysum, ALU.add)
                    else:
                        if dr == 0:
                            b.copy("dve", ydst, pY[:, 0:256])
                        else:
                            b.tt("dve", ydst, pY[:, 0:256], ydst, ALU.add)
                    if idx < 15:
                        if idx == 0:
                            b.copy("dve", S32, pS)
                        else:
                            b.tt("dve", h4(S32, 64), h4(S32, 64), c.et.v(lambda a: a.unsqueeze(2)).bc([128, 4, 64]),
                                 ALU.mult)
                            b.tt("dve", S32, pS, S32, ALU.add)
                        b.copy("act", Sbf, S32)

                fa1(0)
                fa2(0)
                fb(0)
                for idx in range(16):
                    nxt = idx + 1 < 16
                    if nxt:
                        fa1(idx + 1)
                    bpe(idx)
                    if nxt:
                        fa2(idx + 1)
                    bdve(idx)
                    if nxt:
                        fb(idx + 1)
            for blk in range(2):
                chb = 2 * u + blk
                for q in range(4):
                    qs = slice(q * 512, (q + 1) * 512)
                    for j in range(4):
                        tt = 4 * q + j
                        b.tr(pT[:, j * 128:(j + 1) * 128], yacc[:, tt, blk * 128:(blk + 1) * 128], self.identf)
                    b.stt(y1, xsT[:, blk, qs], self.cols[:, C_DD + chb:C_DD + chb + 1], pT, ALU.mult, ALU.add)
                    b.tt("dve", y2, y1, szT[:, blk, qs], ALU.mult)
                    b.act(ysq, y2, AF.Square)
                    for j in range(4):
                        b.mm(pQ[:, j:j + 1], ysq[:, j * 128:(j + 1) * 128], self.ones_bf[:, 0:1])
                    b.tt("dve", self.ssq[:, 4 * q:4 * q + 4], pQ, self.ssq[:, 4 * q:4 * q + 4], ALU.add)
                    y = yo[q % 2]
                    b.ts("dve", y, y2, self.cols[:, C_DNW + chb:C_DNW + chb + 1], None, ALU.mult)
                    r0 = 2048 + chb * 128
                    self.b.s.dma("act", [(self.yT_d.ap[r0:r0 + 128, qs], y.ap)], reads=[y.tok],
                                 writes=[self.yT_d.tok], partial=True)


def _colize(v):
    v = np.asarray(v, np.float32)
    v = v.reshape(-1, v.shape[-1] // 128, 128)
    return np.ascontiguousarray(v.transpose(2, 0, 1).reshape(128, -1))


_CONST_CACHE = {}


def _constants():
    if _CONST_CACHE:
        return _CONST_CACHE
    j = np.arange(128)[:, None]
    i = np.arange(128)[None, :]
    same = (j // 32) == (i // 32)
    cst = np.zeros((128, NCST), np.float32)
    cst[:, K_ID:K_ID + 128] = np.eye(128)
    cst[:, K_MF:K_MF + 128] = (same & (j <= i))
    cst[:, K_MB:K_MB + 128] = (same & (j >= i))
    cst[:, K_TF:K_TF + 128] = (j <= i)
    cst[:, K_TB:K_TB + 128] = (j >= i)
    cst[:, K_GF:K_GF + 128] = (j > i)
    cst[:, K_GB:K_GB + 128] = (j < i)
    cst[:, K_NF:K_NF + 128] = np.where(j <= i, 0.0, -30000.0)
    cst[:, K_NB:K_NB + 128] = np.where(j >= i, 0.0, -30000.0)
    rst = np.ones((128, 512), np.float32)
    rst[:, ::32] = 0.0
    cst[:, K_RST:K_RST + 512] = rst
    c = np.arange(256)
    m = (c[:, None] * c[None, :]) % 256
    ang = 2.0 * np.pi * m.astype(np.float64) / 256.0
    scale = 1.0 / np.sqrt(2048.0 * 256.0)
    for w, fn in enumerate((np.cos, np.sin)):
        tab = (fn(ang) * scale).astype(np.float32)
        cst[:, K_CC + w * 512:K_CC + (w + 1) * 512] = tab.reshape(2, 128, 256).transpose(1, 0, 2).reshape(128, 512)
    s = np.arange(2048, dtype=np.int64)
    m = (s[:, None] * s[None, :]) % 2048
    ang = 2.0 * np.pi * m.astype(np.float64) / 2048.0
    tabf = np.zeros((2, 4, 2, 128, 8, 512), np.float32)
    for w, fn in enumerate((np.cos, np.sin)):
        tab = fn(ang).astype(np.float32)
        tabf[w] = tab.reshape(2, 8, 128, 4, 512).transpose(3, 0, 2, 1, 4)
    _CONST_CACHE["cst"] = cst
    _CONST_CACHE["tabf"] = np.ascontiguousarray(tabf.reshape(16, 128, 8, 512))
    return _CONST_CACHE


def prepare_inputs(x, norm_w, final_norm_w, ev_w_in, ev_w_out, hgrn_lb_logits, hgrn_norm_w,
                   fnet_w, fnet_b, od_w_in, od_w_out, sconv_w, ssd_conv_w, ssd_conv_b,
                   ssd_dt_bias, ssd_a_log, ssd_d, ssd_norm_w):
    f = lambda a: np.asarray(a, np.float32)
    cs = _constants()
    w0 = f(ev_w_in)[0]
    w_in0 = np.ascontiguousarray(w0.reshape(16, 128, 112, 128).transpose(2, 1, 0, 3))
    w1 = f(od_w_in)[0]
    w_in1 = np.ascontiguousarray(w1[:, :13312].reshape(16, 128, 104, 128).transpose(2, 1, 0, 3))
    w_dt = np.ascontiguousarray(w1[:, 13312:13376].reshape(16, 128, 64).transpose(1, 0, 2))
    wo = lambda w: np.ascontiguousarray(f(w)[0].reshape(32, 128, 4, 512).transpose(2, 1, 0, 3))
    cols = np.concatenate([
        _colize(f(norm_w)[0:1]), _colize(f(norm_w)[1:2]), _colize(f(hgrn_lb_logits)),
        _colize(f(hgrn_norm_w)[0:1]), _colize(f(fnet_b)[0:1]), _colize(f(sconv_w)[0]),
        _colize(f(ssd_conv_w)[0]), _colize(f(ssd_conv_b)[0:1]),
        _colize(np.repeat(f(ssd_d)[0], 64)[None, :]), _colize(f(ssd_norm_w)[0:1])], axis=1)
    assert cols.shape == (128, NCOLS), cols.shape
    shared = {
        "w_in0": w_in0, "w_out0": wo(ev_w_out), "w_in1": w_in1, "w_dt": w_dt, "w_out1": wo(od_w_out),
        "cols": np.ascontiguousarray(cols), "cst": cs["cst"], "fnw": f(final_norm_w).reshape(1, D),
        "dtb": f(ssd_dt_bias).reshape(1, 64), "alog": f(ssd_a_log).reshape(1, 64),
        "fw": np.ascontiguousarray(f(fnet_w)[0].reshape(8, 2, 128, 256).transpose(0, 2, 1, 3)),
        "tabf": cs["tabf"],
    }
    xs = f(x)
    return [dict(shared, x=np.ascontiguousarray(xs[i])) for i in range(xs.shape[0])]


def build_program(debug=False, stop_after=None):
    nc = bass.Bass("TRN2", target_bir_lowering=False)
    Builder(nc, debug=debug, stop_after=stop_after).build()
    return nc


def kernel(**inputs):
    in_maps = prepare_inputs(**inputs)
    nc = build_program()
    res = run_bass_kernel_spmd(nc, in_maps, core_ids=list(range(8)))
    return np.stack([np.asarray(r["out"], np.float32) for r in res.results], axis=0)
```

```python
from contextlib import ExitStack
import numpy as np
import concourse.bass as bass
import concourse.mybir as mybir
from concourse.bass_utils import run_bass_kernel_spmd

F32 = mybir.dt.float32
BF16 = mybir.dt.bfloat16
AF = mybir.ActivationFunctionType
ALU = mybir.AluOpType
AX = mybir.AxisListType

L = 2048
D = 2048
NT = 16
KC = 16
EPS = 1e-6

ENGS = ("pe", "act", "dve", "pool", "sp")


class Tok:
    __slots__ = ("name", "w", "r", "war")

    def __init__(self, name=""):
        self.name = name
        self.w = {}
        self.r = {}
        self.war = {}


def _merge(dst, src):
    for k, v in src.items():
        if dst.get(k, -1) < v:
            dst[k] = v


class T:
    __slots__ = ("ap", "tok")

    def __init__(self, ap, tok=None, name=""):
        self.ap = ap
        self.tok = tok if tok is not None else Tok(name)

    def __getitem__(self, idx):
        return T(self.ap[idx], self.tok)

    def v(self, fn):
        return T(fn(self.ap), self.tok)

    def bc(self, shape):
        return T(self.ap.broadcast_to(list(shape)), self.tok)

    def re(self, s, **kw):
        return T(self.ap.rearrange(s, **kw), self.tok)


def _ap(x):
    return x.ap if isinstance(x, T) else x


def _toks(*xs):
    return [x.tok for x in xs if isinstance(x, T)]


class Op:
    __slots__ = ("eng", "fn", "deps", "seq", "signal", "sigval", "dma", "dsem", "dprev", "dval")


class Sched:
    def __init__(self, nc, n_dma_sems=48):
        self.nc = nc
        self.eng_ops = {e: [] for e in ENGS}
        self.n_dma_sems = n_dma_sems
        self.dma_val = [0] * n_dma_sems
        self.dma_rr = 0
        self.out_deps = {}

    def _deps(self, reads, writes, partial):
        deps = {}
        for t in reads:
            _merge(deps, t.w)
        for t in writes:
            _merge(deps, t.r)
            _merge(deps, t.war)
            if not partial:
                _merge(deps, t.w)
        return deps

    def _update(self, key, val, reads, writes, partial):
        me = {key: val}
        for t in reads:
            _merge(t.r, me)
        for t in writes:
            if t.r:
                t.war = t.r
                t.r = {}
                t.w = dict(me)
            elif partial:
                _merge(t.w, me)
            else:
                t.w = dict(me)

    def op(self, eng, fn, reads=(), writes=(), partial=False):
        o = Op()
        o.eng = eng
        o.fn = fn
        o.dma = None
        o.signal = False
        o.deps = self._deps(reads, writes, partial)
        lst = self.eng_ops[eng]
        lst.append(o)
        o.seq = len(lst)
        self._update(eng, o.seq, reads, writes, partial)
        return o

    def dma(self, queue, parts, reads=(), writes=(), partial=False, is_output=False):
        o = Op()
        o.eng = queue
        o.fn = None
        o.dma = parts
        o.signal = False
        o.deps = self._deps(reads, writes, partial)
        s = self.dma_rr
        self.dma_rr = (self.dma_rr + 1) % self.n_dma_sems
        o.dsem = s
        o.dprev = self.dma_val[s]
        self.dma_val[s] += 16 * len(parts)
        o.dval = self.dma_val[s]
        lst = self.eng_ops[queue]
        lst.append(o)
        o.seq = len(lst)
        self._update(("d", s), o.dval, reads, writes, partial)
        if is_output:
            _merge(self.out_deps, {("d", s): o.dval})
        return o

    def barrier(self):
        deps = {e: len(self.eng_ops[e]) for e in ENGS if len(self.eng_ops[e]) > 0 and e != "sp"}
        for s in range(self.n_dma_sems):
            if self.dma_val[s] > 0:
                deps[("d", s)] = self.dma_val[s]
        for e in ENGS:
            o = Op()
            o.eng = e
            o.fn = None
            o.dma = None
            o.signal = False
            o.deps = {k: v for k, v in deps.items() if k != e}
            lst = self.eng_ops[e]
            lst.append(o)
            o.seq = len(lst)

    def emit(self, es):
        nc = self.nc
        for e in ENGS:
            for o in self.eng_ops[e]:
                nd = {}
                for k, v in o.deps.items():
                    if isinstance(k, str):
                        if k == "pe" and e == "pe":
                            continue
                        while v > 0 and self.eng_ops[k][v - 1].fn is None and self.eng_ops[k][v - 1].dma is None:
                            v -= 1
                        if v == 0:
                            continue
                        p = self.eng_ops[k][v - 1]
                        if p.dma is not None:
                            _merge(nd, {("d", p.dsem): p.dval})
                        else:
                            p.signal = True
                            _merge(nd, {k: v})
                    else:
                        _merge(nd, {k: v})
                o.deps = nd
        for e in ENGS:
            c = 0
            for o in self.eng_ops[e]:
                if o.signal:
                    c += 1
                    o.sigval = c
        esem = {e: es.enter_context(nc.semaphore("s_" + e)) for e in ENGS}
        dsems = [es.enter_context(nc.semaphore("d%d" % i)) for i in range(self.n_dma_sems)]
        block = es.enter_context(nc.Block())
        sched = self

        def run(en, e):
            waited = {}
            for o in sched.eng_ops[en]:
                for k, v in o.deps.items():
                    if isinstance(k, str):
                        kk = k
                        tgt = sched.eng_ops[k][v - 1].sigval
                        sem = esem[k]
                    else:
                        kk = k
                        tgt = v
                        sem = dsems[k[1]]
                    if waited.get(kk, 0) >= tgt:
                        continue
                    waited[kk] = tgt
                    e.wait_ge(sem, tgt)
                if o.dma is not None:
                    kk = ("d", o.dsem)
                    if o.dprev > 0 and waited.get(kk, 0) < o.dprev:
                        e.wait_ge(dsems[o.dsem], o.dprev)
                        waited[kk] = o.dprev
                    for (oa, ia) in o.dma:
                        e.dma_start(out=oa, in_=ia).then_inc(dsems[o.dsem], 16)
                elif o.fn is not None:
                    inst = o.fn(e)
                    if o.signal:
                        inst.then_inc(esem[en], 1)
            if en == "sp":
                for k, v in sched.out_deps.items():
                    if waited.get(k, 0) < v:
                        e.wait_ge(dsems[k[1]], v)

        @block.sync
        def _(e):
            run("sp", e)

        @block.scalar
        def _(e):
            run("act", e)

        @block.vector
        def _(e):
            run("dve", e)

        @block.gpsimd
        def _(e):
            run("pool", e)

        @block.tensor
        def _(e):
            run("pe", e)


class B:
    def __init__(self, nc):
        self.nc = nc
        self.s = Sched(nc)

    def mm(self, out, lhsT, rhs, start=True, stop=True, tile_position=None):
        o, a, b = _ap(out), _ap(lhsT), _ap(rhs)
        kw = {}
        if tile_position is not None:
            kw["tile_position"] = tile_position
        self.s.op("pe", lambda e: e.matmul(o, lhsT=a, rhs=b, start=start, stop=stop, **kw),
                  reads=_toks(lhsT, rhs), writes=_toks(out), partial=True)

    def tr(self, out, in_, ident):
        o, a, i = _ap(out), _ap(in_), _ap(ident)
        self.s.op("pe", lambda e: e.transpose(o, a, i), reads=_toks(in_, ident), writes=_toks(out), partial=True)

    def act(self, out, in_, func, bias=None, scale=None, accum=None, extra_w=()):
        o, a = _ap(out), _ap(in_)
        kw = {}
        if bias is not None:
            kw["bias"] = _ap(bias)
        if scale is not None:
            kw["scale"] = _ap(scale)
        if accum is not None:
            kw["accum_out"] = _ap(accum)
        self.s.op("act", lambda e: e.activation(o, a, func, **kw),
                  reads=_toks(in_, bias, scale), writes=_toks(out, accum) + list(extra_w))

    def tt(self, eng, out, in0, in1, op):
        o, a, b = _ap(out), _ap(in0), _ap(in1)
        self.s.op(eng, lambda e: e.tensor_tensor(o, a, b, op), reads=_toks(in0, in1), writes=_toks(out))

    def ts(self, eng, out, in0, s1, s2=None, op0=ALU.mult, op1=None):
        o, a = _ap(out), _ap(in0)
        x1, x2 = _ap(s1), _ap(s2)
        if op1 is None:
            self.s.op(eng, lambda e: e.tensor_scalar(o, a, x1, None, op0),
                      reads=_toks(in0, s1), writes=_toks(out))
        else:
            self.s.op(eng, lambda e: e.tensor_scalar(o, a, x1, x2, op0, op1),
                      reads=_toks(in0, s1, s2), writes=_toks(out))

    def stt(self, out, in0, scalar, in1, op0, op1):
        o, a, c, b = _ap(out), _ap(in0), _ap(scalar), _ap(in1)
        self.s.op("dve", lambda e: e.scalar_tensor_tensor(o, a, c, b, op0, op1),
                  reads=_toks(in0, scalar, in1), writes=_toks(out))

    def scan(self, out, d0, d1, initial, op0=ALU.mult, op1=ALU.add):
        o, a, b, i = _ap(out), _ap(d0), _ap(d1), _ap(initial)
        self.s.op("dve", lambda e: e.tensor_tensor_scan(o, a, b, i, op0, op1),
                  reads=_toks(d0, d1, initial), writes=_toks(out))

    def copy(self, eng, out, in_):
        o, a = _ap(out), _ap(in_)
        if eng == "act":
            self.s.op("act", lambda e: e.copy(o, a), reads=_toks(in_), writes=_toks(out))
        else:
            self.s.op(eng, lambda e: e.tensor_copy(o, a), reads=_toks(in_), writes=_toks(out))

    def recip(self, out, in_):
        o, a = _ap(out), _ap(in_)
        self.s.op("dve", lambda e: e.reciprocal(o, a), reads=_toks(in_), writes=_toks(out))

    def memset(self, eng, out, val):
        o = _ap(out)
        self.s.op(eng, lambda e: e.memset(o, val), writes=_toks(out))

    def dma(self, queue, out, in_, is_output=False):
        self.s.dma(queue, [(_ap(out), _ap(in_))], reads=_toks(in_), writes=_toks(out), is_output=is_output)

    def dmas(self, queue, parts, is_output=False):
        self.s.dma(queue, [(_ap(o), _ap(i)) for o, i in parts],
                   reads=_toks(*[i for _, i in parts]), writes=_toks(*[o for o, _ in parts]),
                   is_output=is_output)


ARENA_BYTES = 206 * 1024
R_CONST = (0, 13312)
R_WEIGHT = (13312, 50176)
R_HT = (50176, 115712)
R_PHASE = (115712, ARENA_BYTES)
R_BIG = (13312, ARENA_BYTES)

C_NW0, C_NW1, C_LB, C_HNW, C_FB, C_SW, C_CW, C_CB, C_DD, C_DNW = 0, 16, 32, 80, 96, 112, 160, 232, 256, 272
NCOLS = 288
K_ID, K_MF, K_MB, K_TF, K_TB, K_GF, K_GB, K_NF, K_NB, K_RST, K_CC = 0, 128, 256, 384, 512, 640, 768, 896, 1024, 1152, 1664
NCST = 1664 + 1024


def _esize(dt):
    return 4 if dt == F32 else 2


class Arena:
    def __init__(self, arena_ap, region):
        self.a = arena_ap
        self.lo, self.hi = region
        self.p = self.lo

    def __call__(self, shape, dt=F32, name=""):
        n = 1
        for s in shape[1:]:
            n *= s
        nb = n * _esize(dt)
        nb_al = (nb + 31) // 32 * 32
        assert self.p + nb_al <= self.hi, "arena overflow %s: need %d have %d" % (name, nb_al, self.hi - self.p)
        v = self.a[0:shape[0], self.p:self.p + nb].bitcast(dt)
        self.p += nb_al
        if len(shape) == 3:
            v = v.rearrange("p (a b) -> p a b", a=shape[1])
        elif len(shape) == 4:
            v = v.rearrange("p (a b c) -> p a b c", a=shape[1], b=shape[2])
        return T(v, name=name)


class Builder:
    def __init__(self, nc, debug=False, stop_after=None):
        self.nc = nc
        self.b = B(nc)
        self.debug = debug
        self.stop_after = stop_after
        self.wi = 0
        self.pi = 0

    def din(self, name, shape, dt=F32):
        return T(self.nc.dram_tensor(name, list(shape), dt, kind="ExternalInput").ap(), name=name)

    def dscr(self, name, shape, dt=F32, out=False):
        kind = "ExternalOutput" if (out or self.debug) else "Internal"
        return T(self.nc.dram_tensor(name, list(shape), dt, kind=kind).ap(), name=name)

    def build(self):
        nc, b = self.nc, self.b
        self.x_d = self.din("x", [L, D])
        self.w_in0 = self.din("w_in0", [112, 128, 16, 128])
        self.w_out0 = self.din("w_out0", [4, 128, 32, 512])
        self.w_in1 = self.din("w_in1", [104, 128, 16, 128])
        self.w_dt = self.din("w_dt", [128, 16, 64])
        self.w_out1 = self.din("w_out1", [4, 128, 32, 512])
        self.cols_d = self.din("cols", [128, NCOLS])
        self.cst_d = self.din("cst", [128, NCST])
        self.fnw_d = self.din("fnw", [1, D])
        self.dtb_d = self.din("dtb", [1, 64])
        self.alog_d = self.din("alog", [1, 64])
        self.fw_d = self.din("fw", [8, 128, 2, 256])
        self.tabf_d = self.din("tabf", [16, 128, 8, 512])
        self.tabb_d = self.dscr("tabb", [16, 128, 8, 512], BF16)
        self.yT_d = self.dscr("yT", [4096, L], BF16)
        self.x1_d = self.dscr("x1", [L, D])
        self.x2_d = self.dscr("x2", [L, D])
        self.out_d = T(nc.dram_tensor("out", [L, D], F32, kind="ExternalOutput").ap(), name="out")
        self.arena = nc.alloc_sbuf_tensor("arena", [128, ARENA_BYTES], mybir.dt.uint8).ap()
        self.PS = [T(nc.alloc_psum_tensor("ps%d" % i, [128, 512], F32).ap(), name="ps%d" % i) for i in range(6)]
        pbs = [nc.alloc_psum_tensor("pb%d" % i, [128, 1024], BF16).ap() for i in range(2)]
        self.PB = [T(pbs[i][:, 0:512], name="pb%d" % i) for i in range(2)]
        self.PBf = [T(pbs[i].bitcast(F32), tok=self.PB[i].tok) for i in range(2)]
        C = Arena(self.arena, R_CONST)
        self.cols = C([128, NCOLS], F32, "cols")
        self.ident = C([128, 128], BF16, "ident")
        self.identf = C([128, 128], F32, "identf")
        self.ones_bf = C([128, 128], BF16, "ones_bf")
        self.ones_f = C([128, 128], F32, "ones_f")
        self.rst = C([128, 512], F32, "rst")
        self.maskF = C([128, 128], F32, "maskF")
        self.maskB = C([128, 128], F32, "maskB")
        self.tri = [C([128, 128], F32, "triF"), C([128, 128], F32, "triB")]
        self.sgt = [C([128, 128], F32, "sgtF"), C([128, 128], F32, "sgtB")]
        self.neg = [C([128, 512], BF16, "negF"), C([128, 512], BF16, "negB")]
        self.cc = [C([128, 2, 256], BF16, "ccC"), C([128, 2, 256], BF16, "ccS")]
        self.lb3 = C([128, 3, 16], F32, "lb3")
        self.dtb_b = C([128, 64], F32, "dtb_b")
        self.a_b = C([128, 64], F32, "a_b")
        self.epsc = C([128, 1], F32, "epsc")
        self.ssq = C([128, 16], F32, "ssq")
        self.rstd = C([128, 16], F32, "rstd")
        W = Arena(self.arena, R_WEIGHT)
        self.wstage = [W([128, 16, 128], F32, "wst%d" % i) for i in range(2)]
        self.wring = [W([128, 16, 128], BF16, "wr%d" % i) for i in range(5)]
        H = Arena(self.arena, R_HT)
        self.hT = H([128, 16, 2048], BF16, "hT")

        self.setup_phase()
        b.s.barrier()
        if self.stop_after == "setup":
            return self.finish()
        self.norm_phase(self.x_d, C_NW0)
        b.s.barrier()
        if self.stop_after == "norm0":
            return self.finish()
        self.hgrn_phase()
        b.s.barrier()
        if self.stop_after == "hgrn":
            return self.finish()
        self.fnet_phase()
        b.s.barrier()
        if self.stop_after == "fnet":
            return self.finish()
        self.outproj_phase(self.w_out0, self.x_d, self.x1_d, split=False)
        b.s.barrier()
        if self.stop_after == "out0":
            return self.finish()
        self.norm_phase(self.x1_d, C_NW1)
        b.s.barrier()
        self.sconv_phase()
        b.s.barrier()
        if self.stop_after == "sconv":
            return self.finish()
        self.ssd_phase()
        b.s.barrier()
        if self.stop_after == "ssd":
            return self.finish()
        self.outproj_phase(self.w_out1, self.x1_d, self.x2_d, split=True)
        b.s.barrier()
        self.final_norm(self.x2_d, self.out_d)
        return self.finish()

    def finish(self):
        if self.debug and self.stop_after in ("norm0", "hgrn", "fnet", "sconv", "ssd"):
            self.b.s.barrier()
            hd = T(self.nc.dram_tensor("dbg_hT", [128, 16, 2048], BF16, kind="ExternalOutput").ap())
            self.b.dma("sp", hd, self.hT, is_output=True)
        if self.stop_after is not None:
            A = Arena(self.arena, R_PHASE)
            t = A([128, 512], F32, "fin")
            self.b.s.barrier()
            self.b.memset("dve", t, 0.0)
            self.b.dma("sp", self.out_d[0:128, 0:512], t, is_output=True)
        self._es = ExitStack()
        self.b.s.emit(self._es)
        self._es.close()
        return self.nc

    def load_w(self, src):
        b = self.b
        st = self.wstage[self.wi % 2]
        wb = self.wring[self.wi % len(self.wring)]
        self.wi += 1
        b.dma("sp", st, src)
        b.copy("pool", wb, st)
        return wb

    def proj(self, wb, consumer):
        b = self.b
        for q in range(4):
            p = self.PS[self.pi % 2]
            self.pi += 1
            for kc in range(KC):
                b.mm(p, wb[:, kc, :], self.hT[:, kc, q * 512:(q + 1) * 512], start=(kc == 0), stop=(kc == KC - 1))
            consumer(q, p)

    def setup_phase(self):
        b = self.b
        A = Arena(self.arena, R_PHASE)
        cst = A([128, NCST], F32, "cst")
        b.dma("sp", self.cols, self.cols_d)
        b.dma("sp", cst, self.cst_d)
        b.dma("sp", self.dtb_b, self.dtb_d.bc([128, 64]))
        al = A([128, 64], F32, "al")
        b.dma("sp", al, self.alog_d.bc([128, 64]))
        b.copy("dve", self.ident, cst[:, K_ID:K_ID + 128])
        b.copy("dve", self.identf, cst[:, K_ID:K_ID + 128])
        b.memset("dve", self.ones_bf, 1.0)
        b.memset("dve", self.ones_f, 1.0)
        b.copy("dve", self.rst, cst[:, K_RST:K_RST + 512])
        b.copy("dve", self.maskF, cst[:, K_MF:K_MF + 128])
        b.copy("dve", self.maskB, cst[:, K_MB:K_MB + 128])
        b.copy("dve", self.tri[0], cst[:, K_TF:K_TF + 128])
        b.copy("dve", self.tri[1], cst[:, K_TB:K_TB + 128])
        b.copy("dve", self.sgt[0], cst[:, K_GF:K_GF + 128])
        b.copy("dve", self.sgt[1], cst[:, K_GB:K_GB + 128])
        for d, k in ((0, K_NF), (1, K_NB)):
            b.copy("dve", self.neg[d].re("p (h i) -> p h i", h=4),
                   cst[:, k:k + 128].v(lambda a: a.unsqueeze(1)).bc([128, 4, 128]))
        b.copy("dve", self.cc[0], cst[:, K_CC:K_CC + 512].re("p (k d) -> p k d", k=2))
        b.copy("dve", self.cc[1], cst[:, K_CC + 512:K_CC + 1024].re("p (k d) -> p k d", k=2))
        b.act(al, al, AF.Exp)
        b.ts("dve", self.a_b, al, -1.0, None, ALU.mult)
        l0 = self.cols[:, C_LB:C_LB + 16]
        l1 = self.cols[:, C_LB + 16:C_LB + 32]
        l2 = self.cols[:, C_LB + 32:C_LB + 48]
        t1 = A([128, 16], F32, "t1")
        t2 = A([128, 16], F32, "t2")
        b.tt("dve", t1, l1, l0, ALU.subtract)
        b.tt("dve", t2, l2, l0, ALU.subtract)
        b.act(t1, t1, AF.Exp)
        b.act(t2, t2, AF.Exp)
        b.tt("dve", t1, t1, t2, ALU.add)
        b.ts("dve", t1, t1, 1.0, None, ALU.add)
        b.recip(self.lb3[:, 0, :], t1)
        b.ts("dve", self.lb3[:, 1, :], self.lb3[:, 0, :], -1.0, 1.0, ALU.mult, ALU.add)
        b.ts("dve", self.lb3[:, 2, :], self.lb3[:, 0, :], -1.0, None, ALU.add)
        b.memset("dve", self.ssq, 0.0)
        b.memset("dve", self.epsc, EPS)
        st = [A([128, 8, 512], F32, "tst%d" % i) for i in range(2)]
        sb_ = [A([128, 8, 512], BF16, "tsb%d" % i) for i in range(2)]
        for i in range(16):
            b.dma("sp", st[i % 2], self.tabf_d[i])
            b.copy("pool" if i % 2 == 0 else "act", sb_[i % 2], st[i % 2])
            self.b.s.dma("sp", [(self.tabb_d.ap[i], sb_[i % 2].ap)], reads=[sb_[i % 2].tok],
                         writes=[self.tabb_d.tok], partial=True)

    def norm_phase(self, xsrc, ccol):
        b = self.b
        A = Arena(self.arena, R_PHASE)
        xt = [A([128, 2048], F32, "nxt%d" % i) for i in range(2)]
        junk = A([128, 2048], BF16, "njunk")
        xn = A([128, 4, 2048], BF16, "nxn")
        sm = [[A([128, 1], F32) for _ in range(4)] for _ in range(2)]
        for g in range(4):
            for j in range(4):
                tt = g * 4 + j
                x_t = xt[tt % 2]
                s1, s2, s3, s4 = sm[tt % 2]
                b.dma("sp", x_t, xsrc[tt * 128:(tt + 1) * 128, :])
                b.act(junk, x_t, AF.Square, accum=s1)
                b.ts("dve", s2, s1, 1.0 / D, EPS, ALU.mult, ALU.add)
                b.act(s3, s2, AF.Sqrt)
                b.recip(s4, s3)
                b.act(xn[:, j, :], x_t, AF.Copy, scale=s4)
            for c in range(16):
                pb = self.PB[c % 2]
                for j in range(4):
                    b.tr(pb[:, j * 128:(j + 1) * 128], xn[:, j, c * 128:(c + 1) * 128], self.ident)
                dst = self.hT[:, c, g * 512:(g + 1) * 512]
                sc = self.cols[:, ccol + c:ccol + c + 1]
                b.act(dst, pb, AF.Copy, scale=sc)

    def hgrn_phase(self):
        b = self.b
        A = Arena(self.arena, R_PHASE)
        qf = A([128, 2048], F32, "qf")
        vT = A([128, 2048], BF16, "vT")
        V = A([128, 16, 128], BF16, "V")
        sg = A([128, 2048], BF16, "sg")
        qin = A([128, 2048], BF16, "qin")
        kin = A([128, 2048], BF16, "kin")
        kend = A([128, 16, 128], BF16, "kend")
        kendT = [A([128, 512], BF16, "kendT%d" % i) for i in range(2)]
        Sbf = A([128, 64, 128], BF16, "Sbf")
        Of = A([128, 2048], F32, "Of")
        tsets = [[A([128, 512], F32, n + str(i)) for n in ("sn", "lf", "bb", "E", "En", "kinf")] for i in range(2)]
        qcnt = [0]
        Elast = A([128, 64], F32, "Elast")
        S32 = [A([128, 128], F32, "S32_%d" % i) for i in range(4)]
        SbfC = [T(Sbf.ap[:, c, :], name="Sbf%d" % c) for c in range(64)]
        Am = [A([128, 128], BF16, "Am%d" % i) for i in range(2)]
        ot, rs1, rs2 = tsets[0][0], tsets[0][1], tsets[0][2]
        osq = T(tsets[0][3].ap.bitcast(BF16)[:, 0:512], tok=tsets[0][3].tok)
        yT = A([128, 2048], BF16, "yTt")
        PS = self.PS
        r3 = lambda t: t.re("p (c k) -> p c k", k=32)
        for hd in range(16):
            oml = self.lb3[:, 1, hd:hd + 1]
            noml = self.lb3[:, 2, hd:hd + 1]
            wq = self.load_w(self.w_in0[0 * 16 + hd])
            self.proj(wq, lambda q, p: b.copy("act", qf[:, q * 512:(q + 1) * 512], p))
            wv = self.load_w(self.w_in0[1 * 16 + hd])
            self.proj(wv, lambda q, p: b.copy("dve", vT[:, q * 512:(q + 1) * 512], p))
            for q in range(4):
                pb = self.PB[q % 2]
                for j in range(4):
                    b.tr(pb[:, j * 128:(j + 1) * 128], vT[:, (4 * q + j) * 128:(4 * q + j + 1) * 128], self.ident)
                b.copy("act" if q % 2 else "dve", V[:, 4 * q:4 * q + 4, :], pb.re("p (j k) -> p j k", k=128))
            wg = self.load_w(self.w_in0[4 * 16 + hd])
            self.proj(wg, lambda q, p: b.act(sg[:, q * 512:(q + 1) * 512], p, AF.Silu))
            for dr in (0, 1):
                wz = self.load_w(self.w_in0[(2 + dr) * 16 + hd])

                def partB(q, ts_, dr=dr):
                    qs = slice(q * 512, (q + 1) * 512)
                    sn, lf, bb, E, En, kinf = ts_
                    b.scan(bb, self.rst, lf, 0.0)
                    if dr == 1:
                        b.tt("dve", r3(En), r3(bb)[:, :, 31:32].bc([128, 16, 32]), r3(bb), ALU.subtract)
                        b.tt("dve", bb, En, lf, ALU.add)
                    b.act(E, bb, AF.Exp)
                    b.act(En, bb, AF.Exp, scale=-1.0)
                    b.tt("dve", qin[:, qs], qf[:, qs], E, ALU.mult)
                    b.stt(kinf, sn, oml, En, ALU.mult, ALU.mult)
                    b.copy("act", kin[:, qs], kinf)
                    edge = r3(E)[:, :, 31:32] if dr == 0 else r3(E)[:, :, 0:1]
                    kT = kendT[q % 2]
                    b.tt("dve", r3(kT), r3(kinf), edge.bc([128, 16, 32]), ALU.mult)
                    b.copy("dve", Elast[:, 16 * q:16 * q + 16].v(lambda a: a.unsqueeze(2)), edge)
                    pb = self.PB[q % 2]
                    for j in range(4):
                        b.tr(pb[:, j * 128:(j + 1) * 128], kT[:, j * 128:(j + 1) * 128], self.ident)
                    b.copy("act", kend[:, 4 * q:4 * q + 4, :], pb.re("p (j k) -> p j k", k=128))

                pend = []

                def cons(q, p, dr=dr, pend=pend):
                    ts_ = tsets[qcnt[0] % 2]
                    qcnt[0] += 1
                    sn, lf = ts_[0], ts_[1]
                    b.act(sn, p, AF.Exp)
                    b.act(lf, sn, AF.Ln, bias=1.0)
                    b.act(sn, lf, AF.Exp, scale=-1.0)
                    b.act(lf, sn, AF.Ln, bias=1.0, scale=noml)
                    if pend:
                        partB(*pend.pop())
                    pend.append((q, ts_))

                self.proj(wz, cons)
                partB(*pend.pop())
                order = list(range(64)) if dr == 0 else list(range(63, -1, -1))
                mask = self.maskF if dr == 0 else self.maskB
                zero_c = order[0]
                b.memset("dve", SbfC[zero_c], 0.0)
                blocks_done = [0]
                pend_e2 = []

                def out_block(bi, dr=dr, mask=mask, zero_c=zero_c, blocks_done=blocks_done, pend_e2=pend_e2):
                    todo = list(pend_e2)
                    del pend_e2[:]
                    for f in todo:
                        f()
                    q, bl = bi // 4, bi % 4
                    qs = slice(q * 512, (q + 1) * 512)
                    pO = self.PBf[q % 2]
                    bs = slice(bi * 128, (bi + 1) * 128)
                    pA = PS[bi % 2][:, 0:128]
                    am = Am[bi % 2]
                    b.mm(pA, kin[:, bs], qin[:, bs])
                    b.tt("dve", am, pA, mask, ALU.mult)
                    o_sl = pO[:, bl * 128:(bl + 1) * 128]
                    cl = [4 * bi + k for k in range(4) if 4 * bi + k != zero_c]
                    b.mm(o_sl, V[:, bi, :], am, start=True, stop=False)
                    for c in cl:
                        k = c % 4
                        b.mm(o_sl[:, 32 * k:32 * k + 32], SbfC[c], qin[:, c * 32:(c + 1) * 32],
                             start=False, stop=(c == cl[-1]))
                    blocks_done[0] += 1
                    if blocks_done[0] % 4 != 0:
                        return
                    if dr == 0:
                        b.copy("act", Of[:, qs], pO)
                    else:
                        b.tt("dve", ot, pO, Of[:, qs], ALU.add)
                        b.tt("pool", osq, ot, ot, ALU.mult)

                        def e3(qs=qs):
                            b.tt("dve", ot, ot, rs1, ALU.mult)
                            b.stt(yT[:, qs], ot, self.cols[:, C_HNW + hd:C_HNW + hd + 1], sg[:, qs], ALU.mult,
                                  ALU.mult)

                        def e2(q=q, e3=e3):
                            pss = PS[q % 2]
                            b.mm(pss, self.ones_bf, osq)
                            b.act(rs2, pss, AF.Ln, bias=self.epsc, scale=1.0 / 128.0)
                            b.act(rs1, rs2, AF.Exp, scale=-0.5)
                            pend_e2.append(e3)

                        pend_e2.append(e2)

                for idx, c in enumerate(order[:-1]):
                    tile, k = c // 4, c % 4
                    pS = PS[2 + k][:, 0:128]
                    b.mm(pS, kend[32 * k:32 * k + 32, tile, :], V[32 * k:32 * k + 32, tile, :],
                         tile_position=(32 * k, 0))
                    new, old = S32[idx % 4], S32[(idx + 3) % 4]
                    if idx == 0:
                        b.copy("dve", new, pS)
                    else:
                        b.stt(new, old, Elast[:, c:c + 1], pS, ALU.mult, ALU.add)
                    cn = order[idx + 1]
                    b.copy("act", SbfC[cn], new)
                    if dr == 0 and cn % 4 == 3:
                        out_block(cn // 4)
                    elif dr == 1 and cn % 4 == 0:
                        out_block(cn // 4)
                while pend_e2:
                    pend_e2.pop(0)()
            self.b.s.dma("act", [(self.yT_d.ap[hd * 128:(hd + 1) * 128, :], yT.ap)], reads=[yT.tok],
                         writes=[self.yT_d.tok], partial=True)

    def fnet_phase(self):
        b = self.b
        PS = self.PS
        A = Arena(self.arena, R_PHASE)
        uT = A([128, 2, 2048], BF16, "uT")
        sgB = A([128, 2, 2048], BF16, "sgB")
        UW = A([128, 16, 2, 256], BF16, "UW")
        tabs = [[A([128, 8, 512], BF16, "tab%d_%d" % (s, w)) for w in range(2)] for s in range(2)]
        fwst = A([128, 2, 256], F32, "fwst")
        fwb = A([128, 2, 256], BF16, "fwb")
        W12 = A([128, 2, 2, 256], BF16, "W12")
        yo = [A([128, 512], BF16, "fyo%d" % i) for i in range(2)]
        slot_i = 0
        for g in range(8):
            for co in range(2):
                wb = self.load_w(self.w_in0[80 + 2 * g + co])
                self.proj(wb, lambda q, p, co=co: b.copy("act", uT[:, co, q * 512:(q + 1) * 512], p))
            for co in range(2):
                wb = self.load_w(self.w_in0[96 + 2 * g + co])
                self.proj(wb, lambda q, p, co=co: b.act(sgB[:, co, q * 512:(q + 1) * 512], p, AF.Silu))
            b.dma("sp", fwst, self.fw_d[g])
            b.copy("dve", fwb, fwst)
            for co in range(2):
                p = PS[2]
                for w in range(2):
                    for k in range(2):
                        b.mm(p[:, w * 256:(w + 1) * 256], self.cc[w][:, k, co * 128:(co + 1) * 128], fwb[:, k, :],
                             start=(k == 0), stop=(k == 1))
                b.copy("act", W12[:, co, 0, :], p[:, 0:256])
                b.ts("dve", W12[:, co, 1, :], p[:, 256:512], -1.0, None, ALU.mult)
            for st in range(16):
                p = PS[2 + st % 2]
                for co in range(2):
                    b.mm(p, uT[:, co, st * 128:(st + 1) * 128], W12[:, co, :, :].re("p w d -> p (w d)"),
                         start=(co == 0), stop=(co == 1))
                b.copy("act" if st % 2 else "dve", UW[:, st, :, :], p.re("p (w d) -> p w d", w=2))
            for tt in range(4):
                banks = (PS[2], PS[3]) if tt % 2 == 0 else (PS[4], PS[5])
                for half in range(2):
                    slot = tabs[slot_i % 2]
                    slot_i += 1
                    for w in range(2):
                        b.dma("sp", slot[w], self.tabb_d[w * 8 + tt * 2 + half])
                    for dblk in range(2):
                        p = banks[dblk]
                        for j in range(8):
                            st = half * 8 + j
                            for w in range(2):
                                b.mm(p, UW[:, st, w, dblk * 128:(dblk + 1) * 128], slot[w][:, j, :],
                                     start=(half == 0 and j == 0 and w == 0), stop=(half == 1 and j == 7 and w == 1))
                for dblk in range(2):
                    y = yo[dblk]
                    ch = 2 * g + dblk
                    b.stt(y, banks[dblk], self.cols[:, C_FB + ch:C_FB + ch + 1], sgB[:, dblk, tt * 512:(tt + 1) * 512],
                          ALU.add, ALU.mult)
                    r0 = 2048 + ch * 128
                    self.b.s.dma("act", [(self.yT_d.ap[r0:r0 + 128, tt * 512:(tt + 1) * 512], y.ap)], reads=[y.tok],
                                 writes=[self.yT_d.tok], partial=True)

    def outproj_phase(self, wout, xsrc, xdst, split):
        b = self.b
        PS = self.PS
        A = Arena(self.arena, R_BIG)
        wsl = [A([128, 32, 512], BF16, "wsl%d" % i) for i in range(2)]
        ysl = [A([128, 32, 512], BF16, "ysl%d" % i) for i in range(2)]
        wst = [A([128, 4, 512], F32, "owst%d" % i) for i in range(2)]
        xr = [A([128, 512], F32, "xr%d" % i) for i in range(2)]
        xo = [A([128, 512], F32, "xo%d" % i) for i in range(2)]
        if split:
            b.ts("dve", self.rstd, self.ssq, 1.0 / 2048.0, EPS, ALU.mult, ALU.add)
            b.act(self.rstd, self.rstd, AF.Sqrt)
            b.recip(self.rstd, self.rstd)
        yv = self.yT_d.re("(kc p) t -> p kc t", p=128)
        cnt = 0
        for dt_ in range(4):
            w = wsl[dt_ % 2]
            for pc in range(8):
                st = wst[pc % 2]
                b.dma("sp", st, wout[dt_][:, pc * 4:(pc + 1) * 4, :])
                b.copy("pool", w[:, pc * 4:(pc + 1) * 4, :], st)
            for tq in range(4):
                ys = ysl[(dt_ * 4 + tq) % 2]
                self.b.s.dma("sp", [(ys.ap[:, i * 8:(i + 1) * 8, :], yv.ap[:, i * 8:(i + 1) * 8, tq * 512:(tq + 1) * 512])
                                    for i in range(4)], reads=[self.yT_d.tok], writes=[ys.tok])
                for j in range(4):
                    tt = tq * 4 + j
                    ts_ = slice(j * 128, (j + 1) * 128)
                    pa = PS[cnt % 3]
                    pbk = PS[3 + cnt % 3]
                    x_r = xr[cnt % 2]
                    x_o = xo[cnt % 2]
                    cnt += 1
                    b.dma("sp", x_r, xsrc[tt * 128:(tt + 1) * 128, dt_ * 512:(dt_ + 1) * 512])
                    if not split:
                        for kc in range(32):
                            b.mm(pa, ys[:, kc, ts_], w[:, kc, :], start=(kc == 0), stop=(kc == 31))
                        b.tt("dve", x_o, pa, x_r, ALU.add)
                    else:
                        for kc in range(16):
                            b.mm(pa, ys[:, kc, ts_], w[:, kc, :], start=(kc == 0), stop=(kc == 15))
                        for kc in range(16, 32):
                            b.mm(pbk, ys[:, kc, ts_], w[:, kc, :], start=(kc == 16), stop=(kc == 31))
                        b.tt("dve", x_o, pa, x_r, ALU.add)
                        b.stt(x_o, pbk, self.rstd[:, tt:tt + 1], x_o, ALU.mult, ALU.add)
                    self.b.s.dma("act", [(xdst.ap[tt * 128:(tt + 1) * 128, dt_ * 512:(dt_ + 1) * 512], x_o.ap)],
                                 reads=[x_o.tok], writes=[xdst.tok], partial=True)

    def final_norm(self, xsrc, out):
        b = self.b
        A = Arena(self.arena, R_BIG)
        fnwb = A([128, 2048], F32, "fnwb")
        xt = [A([128, 2048], F32, "fxt%d" % i) for i in range(2)]
        xo = [A([128, 2048], F32, "fxo%d" % i) for i in range(2)]
        junk = A([128, 2048], BF16, "fjunk")
        sm = [[A([128, 1], F32) for _ in range(4)] for _ in range(2)]
        b.dma("sp", fnwb, self.fnw_d.bc([128, 2048]))
        for tt in range(16):
            x_t, x_o = xt[tt % 2], xo[tt % 2]
            s1, s2, s3, s4 = sm[tt % 2]
            b.dma("sp", x_t, xsrc[tt * 128:(tt + 1) * 128, :])
            b.act(junk, x_t, AF.Square, accum=s1)
            b.ts("dve", s2, s1, 1.0 / D, EPS, ALU.mult, ALU.add)
            b.act(s3, s2, AF.Sqrt)
            b.recip(s4, s3)
            b.stt(x_o, x_t, s4, fnwb, ALU.mult, ALU.mult)
            self.b.s.dma("act", [(out.ap[tt * 128:(tt + 1) * 128, :], x_o.ap)], reads=[x_o.tok],
                         writes=[out.tok], partial=True, is_output=True)

    def sconv_phase(self):
        b = self.b
        A = Arena(self.arena, R_PHASE)
        cin = A([128, 2048], F32, "cin")
        upad = A([128, 2050], F32, "upad")
        ya = A([128, 2048], F32, "ya")
        yb = A([128, 2048], F32, "yb")
        sgc = [A([128, 512], F32, "sgc%d" % i) for i in range(2)]
        yo = A([128, 2048], BF16, "syo")
        b.memset("dve", upad[:, 0:1], 0.0)
        b.memset("dve", upad[:, 2049:2050], 0.0)
        for cb in range(16):
            w0 = self.cols[:, C_SW + cb:C_SW + cb + 1]
            w1 = self.cols[:, C_SW + 16 + cb:C_SW + 16 + cb + 1]
            w2 = self.cols[:, C_SW + 32 + cb:C_SW + 32 + cb + 1]
            wb = self.load_w(self.w_in1[cb])
            self.proj(wb, lambda q, p: b.copy("act", cin[:, q * 512:(q + 1) * 512], p))
            wb = self.load_w(self.w_in1[32 + cb])
            self.proj(wb, lambda q, p: b.tt("dve", upad[:, 1 + q * 512:1 + (q + 1) * 512], p,
                                            cin[:, q * 512:(q + 1) * 512], ALU.mult))
            b.ts("dve", ya, upad[:, 1:2049], w1, None, ALU.mult)
            b.stt(yb, upad[:, 0:2048], w0, ya, ALU.mult, ALU.add)
            b.stt(ya, upad[:, 2:2050], w2, yb, ALU.mult, ALU.add)
            wb = self.load_w(self.w_in1[16 + cb])
            self.proj(wb, lambda q, p: b.tt("dve", yb[:, q * 512:(q + 1) * 512], p, ya[:, q * 512:(q + 1) * 512],
                                            ALU.mult))
            wb = self.load_w(self.w_in1[48 + cb])

            def cons(q, p):
                s = sgc[q % 2]
                b.act(s, p, AF.Silu)
                b.tt("dve", yo[:, q * 512:(q + 1) * 512], yb[:, q * 512:(q + 1) * 512], s, ALU.mult)

            self.proj(wb, cons)
            self.b.s.dma("act", [(self.yT_d.ap[cb * 128:(cb + 1) * 128, :], yo.ap)], reads=[yo.tok],
                         writes=[self.yT_d.tok], partial=True)

    def conv_block(self, wsrc, wcol, out, upad, yq):
        b = self.b
        w0 = self.cols[:, C_CW + wcol:C_CW + wcol + 1]
        w1 = self.cols[:, C_CW + 24 + wcol:C_CW + 24 + wcol + 1]
        w2 = self.cols[:, C_CW + 48 + wcol:C_CW + 48 + wcol + 1]
        cbias = self.cols[:, C_CB + wcol:C_CB + wcol + 1]
        wb = self.load_w(wsrc)
        self.proj(wb, lambda q, p: b.copy("act", upad[:, 1 + q * 512:1 + (q + 1) * 512], p))
        for q in range(4):
            ya, yb = yq[q % 2]
            b.ts("dve", ya, upad[:, 1 + q * 512:1 + (q + 1) * 512], w1, None, ALU.mult)
            b.stt(yb, upad[:, q * 512:(q + 1) * 512], w0, ya, ALU.mult, ALU.add)
            b.stt(ya, upad[:, 2 + q * 512:2 + (q + 1) * 512], w2, yb, ALU.mult, ALU.add)
            b.act(out[:, q * 512:(q + 1) * 512], ya, AF.Silu, bias=cbias)

    def ssd_phase(self):
        b = self.b
        PS = self.PS
        A = Arena(self.arena, R_PHASE)
        dt = A([128, 16, 64], F32, "dt")
        dta = A([128, 16, 64], F32, "dta")
        A0 = Arena(self.arena, (A.p, ARENA_BYTES))
        wdst = A0([128, 16, 64], F32, "wdst")
        wdtb = A0([128, 16, 64], BF16, "wdtb")
        xsT = A([128, 2, 2048], BF16, "xsT")
        xstok = A([128, 16, 256], BF16, "xstok")
        szT = A([128, 2, 2048], BF16, "szT")
        yacc = A([128, 16, 256], F32, "yacc")
        BT = A([128, 2048], BF16, "BT")
        CT = A([128, 2048], BF16, "CT")
        Btok = A([128, 16, 128], BF16, "Btok")
        upad = A([128, 2050], F32, "upad")
        yq = [(A([128, 512], F32, "cya%d" % i), A([128, 512], F32, "cyb%d" % i)) for i in range(2)]
        class CS:
            pass
        sets = []
        for i in range(2):
            c = CS()
            c.X = A([128, 512], F32, "X%d" % i)
            c.Dm = A([128, 512], BF16, "Dm%d" % i)
            c.Wm = A([128, 512], BF16, "Wm%d" % i)
            c.cb = A([128, 128], F32, "cb%d" % i)
            c.xdt = A([128, 256], BF16, "xdt%d" % i)
            c.xdtw = A([128, 256], BF16, "xdtw%d" % i)
            c.tz = A([128, 256], F32, "tz%d" % i)
            c.ac, c.ei, c.wj, c.et, c.dif = [A([128, 4], F32, n + str(i)) for n in ("ac", "ei", "wj", "et", "dif")]
            sets.append(c)
        S32 = A([128, 256], F32, "S32")
        Sbf = A([128, 256], BF16, "Sbf")
        ysum = A([128, 256], F32, "ysum")
        y1 = sets[0].X
        y2 = sets[1].X
        ysq = sets[0].Dm
        yo = [sets[1].Dm, sets[1].Wm]
        pEs, pY = (PS[2], PS[5]), PS[3]
        pS = PS[4][:, 0:256]
        pQ = PS[4][:, 256:260]
        pC = PS[0][:, 0:128]
        pSm = PS[1][:, 0:8]
        pT = PS[5]
        b.memset("dve", upad[:, 0:1], 0.0)
        b.memset("dve", upad[:, 2049:2050], 0.0)
        b.dma("sp", wdst, self.w_dt)
        b.copy("pool", wdtb, wdst)
        for half in range(2):
            p = PS[2 + half]
            for j in range(8):
                tt = half * 8 + j
                for kc in range(KC):
                    b.mm(p[:, j * 64:(j + 1) * 64], self.hT[:, kc, tt * 128:(tt + 1) * 128], wdtb[:, kc, :],
                         start=(kc == 0), stop=(kc == KC - 1))
            b.tt("dve", dt[:, half * 8:(half + 1) * 8, :], p.re("p (j h) -> p j h", h=64),
                 self.dtb_b.v(lambda a: a.unsqueeze(1)).bc([128, 8, 64]), ALU.add)
        b.act(dta, dt, AF.Exp)
        b.act(dt, dta, AF.Ln, bias=1.0)
        b.tt("dve", dta, dt, self.a_b.v(lambda a: a.unsqueeze(1)).bc([128, 16, 64]), ALU.mult)
        b.s.barrier()
        h4 = lambda t, d: t.re("p (h d) -> p h d", d=d)
        for u in range(8):
            g = u // 2
            for blk in range(2):
                self.conv_block(self.w_in1[80 + 2 * u + blk], 2 * u + blk, xsT[:, blk, :], upad, yq)
            for blk in range(2):
                wb = self.load_w(self.w_in1[64 + 2 * u + blk])
                self.proj(wb, lambda q, p, blk=blk: b.act(szT[:, blk, q * 512:(q + 1) * 512], p, AF.Silu))
            if u % 2 == 0:
                self.conv_block(self.w_in1[96 + g], 16 + g, BT, upad, yq)
                self.conv_block(self.w_in1[100 + g], 20 + g, CT, upad, yq)
                for q in range(4):
                    pb = self.PB[q % 2]
                    for j in range(4):
                        tt = 4 * q + j
                        b.tr(pb[:, j * 128:(j + 1) * 128], BT[:, tt * 128:(tt + 1) * 128], self.ident)
                    b.copy("act" if q % 2 else "dve", Btok[:, 4 * q:4 * q + 4, :], pb.re("p (j k) -> p j k", k=128))
            for tt in range(16):
                pb = self.PB[tt % 2]
                for blk in range(2):
                    b.tr(pb[:, blk * 128:(blk + 1) * 128], xsT[:, blk, tt * 128:(tt + 1) * 128], self.ident)
                b.copy("act" if tt % 2 else "dve", xstok[:, tt, :], pb[:, 0:256])
            for dr in range(2):
                order = list(range(16)) if dr == 0 else list(range(15, -1, -1))
                c0 = dr * 32 + 4 * u

                def fa1(idx, dr=dr, order=order, c0=c0):
                    ch = order[idx]
                    c = sets[idx % 2]
                    dta_c = dta[:, ch, c0:c0 + 4]
                    b.mm(pSm[:, 0:4], self.tri[dr], dta_c)
                    b.mm(pSm[:, 4:8], self.ones_f, dta_c)
                    b.copy("dve", c.ac, pSm[:, 0:4])
                    b.act(c.ei, c.ac, AF.Exp)
                    b.tt("dve", c.dif, pSm[:, 4:8], c.ac, ALU.subtract)
                    b.act(c.wj, c.dif, AF.Exp)
                    b.act(c.et, pSm[:, 4:8], AF.Exp)
                    b.tt("dve", h4(c.X, 128), self.tri[dr].v(lambda a: a.unsqueeze(1)).bc([128, 4, 128]),
                         dta_c.v(lambda a: a.unsqueeze(2)).bc([128, 4, 128]), ALU.mult)

                def fa2(idx, dr=dr, order=order):
                    ch = order[idx]
                    c = sets[idx % 2]
                    pE = pEs[idx % 2]
                    chs = slice(ch * 128, (ch + 1) * 128)
                    b.mm(pE, self.sgt[dr], c.X, start=True, stop=False)
                    b.mm(pE, self.ident, self.neg[dr], start=False, stop=True)
                    b.act(c.Dm, pE, AF.Exp)
                    b.mm(pC, BT[:, chs], CT[:, chs])
                    b.copy("act", c.cb, pC)

                def fb(idx, dr=dr, order=order, c0=c0):
                    ch = order[idx]
                    c = sets[idx % 2]
                    dt_c = dt[:, ch, c0:c0 + 4]
                    b.tt("dve", h4(c.Wm, 128), h4(c.Dm, 128), c.cb.v(lambda a: a.unsqueeze(1)).bc([128, 4, 128]),
                         ALU.mult)
                    b.tt("pool", h4(c.xdt, 64), h4(xstok[:, ch, :], 64),
                         dt_c.v(lambda a: a.unsqueeze(2)).bc([128, 4, 64]), ALU.mult)
                    b.tt("pool", h4(c.xdtw, 64), h4(c.xdt, 64), c.wj.v(lambda a: a.unsqueeze(2)).bc([128, 4, 64]),
                         ALU.mult)

                def bpe(idx, dr=dr, order=order):
                    ch = order[idx]
                    c = sets[idx % 2]
                    chs = slice(ch * 128, (ch + 1) * 128)
                    for hl in range(4):
                        b.mm(pY[:, hl * 64:(hl + 1) * 64], c.Wm[:, hl * 128:(hl + 1) * 128],
                             c.xdt[:, hl * 64:(hl + 1) * 64])
                    if idx > 0:
                        b.mm(pY[:, 256:512], CT[:, chs], Sbf)
                    if idx < 15:
                        b.mm(pS, Btok[:, ch, :], c.xdtw)

                def bdve(idx, dr=dr, order=order):
                    ch = order[idx]
                    c = sets[idx % 2]
                    ydst = yacc[:, ch, :]
                    if idx > 0:
                        b.tt("dve", h4(c.tz, 64), h4(pY[:, 256:512], 64),
                             c.ei.v(lambda a: a.unsqueeze(2)).bc([128, 4, 64]), ALU.mult)
                        if dr == 0:
                            b.tt("dve", ydst, pY[:, 0:256], c.tz, ALU.add)
                        else:
                            b.tt("dve", ysum, pY[:, 0:256], c.tz, ALU.add)
                            b.tt("dve", ydst, ydst, ysum, ALU.add)
                    else:
                        if dr == 0:
                            b.copy("dve", ydst, pY[:, 0:256])
                        else:
                            b.tt("dve", ydst, pY[:, 0:256], ydst, ALU.add)
                    if idx < 15:
                        if idx == 0:
                            b.copy("dve", S32, pS)
                        else:
                            b.tt("dve", h4(S32, 64), h4(S32, 64), c.et.v(lambda a: a.unsqueeze(2)).bc([128, 4, 64]),
                                 ALU.mult)
                            b.tt("dve", S32, pS, S32, ALU.add)
                        b.copy("act", Sbf, S32)

                fa1(0)
                fa2(0)
                fb(0)
                for idx in range(16):
                    nxt = idx + 1 < 16
                    if nxt:
                        fa1(idx + 1)
                    bpe(idx)
                    if nxt:
                        fa2(idx + 1)
                    bdve(idx)
                    if nxt:
                        fb(idx + 1)
            for blk in range(2):
                chb = 2 * u + blk
                for q in range(4):
                    qs = slice(q * 512, (q + 1) * 512)
                    for j in range(4):
                        tt = 4 * q + j
                        b.tr(pT[:, j * 128:(j + 1) * 128], yacc[:, tt, blk * 128:(blk + 1) * 128], self.identf)
                    b.stt(y1, xsT[:, blk, qs], self.cols[:, C_DD + chb:C_DD + chb + 1], pT, ALU.mult, ALU.add)
                    b.tt("dve", y2, y1, szT[:, blk, qs], ALU.mult)
                    b.act(ysq, y2, AF.Square)
                    for j in range(4):
                        b.mm(pQ[:, j:j + 1], ysq[:, j * 128:(j + 1) * 128], self.ones_bf[:, 0:1])
                    b.tt("dve", self.ssq[:, 4 * q:4 * q + 4], pQ, self.ssq[:, 4 * q:4 * q + 4], ALU.add)
                    y = yo[q % 2]
                    b.ts("dve", y, y2, self.cols[:, C_DNW + chb:C_DNW + chb + 1], None, ALU.mult)
                    r0 = 2048 + chb * 128
                    self.b.s.dma("act", [(self.yT_d.ap[r0:r0 + 128, qs], y.ap)], reads=[y.tok],
                                 writes=[self.yT_d.tok], partial=True)


def _colize(v):
    v = np.asarray(v, np.float32)
    v = v.reshape(-1, v.shape[-1] // 128, 128)
    return np.ascontiguousarray(v.transpose(2, 0, 1).reshape(128, -1))


_CONST_CACHE = {}


def _constants():
    if _CONST_CACHE:
        return _CONST_CACHE
    j = np.arange(128)[:, None]
    i = np.arange(128)[None, :]
    same = (j // 32) == (i // 32)
    cst = np.zeros((128, NCST), np.float32)
    cst[:, K_ID:K_ID + 128] = np.eye(128)
    cst[:, K_MF:K_MF + 128] = (same & (j <= i))
    cst[:, K_MB:K_MB + 128] = (same & (j >= i))
    cst[:, K_TF:K_TF + 128] = (j <= i)
    cst[:, K_TB:K_TB + 128] = (j >= i)
    cst[:, K_GF:K_GF + 128] = (j > i)
    cst[:, K_GB:K_GB + 128] = (j < i)
    cst[:, K_NF:K_NF + 128] = np.where(j <= i, 0.0, -30000.0)
    cst[:, K_NB:K_NB + 128] = np.where(j >= i, 0.0, -30000.0)
    rst = np.ones((128, 512), np.float32)
    rst[:, ::32] = 0.0
    cst[:, K_RST:K_RST + 512] = rst
    c = np.arange(256)
    m = (c[:, None] * c[None, :]) % 256
    ang = 2.0 * np.pi * m.astype(np.float64) / 256.0
    scale = 1.0 / np.sqrt(2048.0 * 256.0)
    for w, fn in enumerate((np.cos, np.sin)):
        tab = (fn(ang) * scale).astype(np.float32)
        cst[:, K_CC + w * 512:K_CC + (w + 1) * 512] = tab.reshape(2, 128, 256).transpose(1, 0, 2).reshape(128, 512)
    s = np.arange(2048, dtype=np.int64)
    m = (s[:, None] * s[None, :]) % 2048
    ang = 2.0 * np.pi * m.astype(np.float64) / 2048.0
    tabf = np.zeros((2, 4, 2, 128, 8, 512), np.float32)
    for w, fn in enumerate((np.cos, np.sin)):
        tab = fn(ang).astype(np.float32)
        tabf[w] = tab.reshape(2, 8, 128, 4, 512).transpose(3, 0, 2, 1, 4)
    _CONST_CACHE["cst"] = cst
    _CONST_CACHE["tabf"] = np.ascontiguousarray(tabf.reshape(16, 128, 8, 512))
    return _CONST_CACHE


def prepare_inputs(x, norm_w, final_norm_w, ev_w_in, ev_w_out, hgrn_lb_logits, hgrn_norm_w,
                   fnet_w, fnet_b, od_w_in, od_w_out, sconv_w, ssd_conv_w, ssd_conv_b,
                   ssd_dt_bias, ssd_a_log, ssd_d, ssd_norm_w):
    f = lambda a: np.asarray(a, np.float32)
    cs = _constants()
    w0 = f(ev_w_in)[0]
    w_in0 = np.ascontiguousarray(w0.reshape(16, 128, 112, 128).transpose(2, 1, 0, 3))
    w1 = f(od_w_in)[0]
    w_in1 = np.ascontiguousarray(w1[:, :13312].reshape(16, 128, 104, 128).transpose(2, 1, 0, 3))
    w_dt = np.ascontiguousarray(w1[:, 13312:13376].reshape(16, 128, 64).transpose(1, 0, 2))
    wo = lambda w: np.ascontiguousarray(f(w)[0].reshape(32, 128, 4, 512).transpose(2, 1, 0, 3))
    cols = np.concatenate([
        _colize(f(norm_w)[0:1]), _colize(f(norm_w)[1:2]), _colize(f(hgrn_lb_logits)),
        _colize(f(hgrn_norm_w)[0:1]), _colize(f(fnet_b)[0:1]), _colize(f(sconv_w)[0]),
        _colize(f(ssd_conv_w)[0]), _colize(f(ssd_conv_b)[0:1]),
        _colize(np.repeat(f(ssd_d)[0], 64)[None, :]), _colize(f(ssd_norm_w)[0:1])], axis=1)
    assert cols.shape == (128, NCOLS), cols.shape
    shared = {
        "w_in0": w_in0, "w_out0": wo(ev_w_out), "w_in1": w_in1, "w_dt": w_dt, "w_out1": wo(od_w_out),
        "cols": np.ascontiguousarray(cols), "cst": cs["cst"], "fnw": f(final_norm_w).reshape(1, D),
        "dtb": f(ssd_dt_bias).reshape(1, 64), "alog": f(ssd_a_log).reshape(1, 64),
        "fw": np.ascontiguousarray(f(fnet_w)[0].reshape(8, 2, 128, 256).transpose(0, 2, 1, 3)),
        "tabf": cs["tabf"],
    }
    xs = f(x)
    return [dict(shared, x=np.ascontiguousarray(xs[i])) for i in range(xs.shape[0])]


def build_program(debug=False, stop_after=None):
    nc = bass.Bass("TRN2", target_bir_lowering=False)
    Builder(nc, debug=debug, stop_after=stop_after).build()
    return nc


def kernel(**inputs):
    in_maps = prepare_inputs(**inputs)
    nc = build_program()
    res = run_bass_kernel_spmd(nc, in_maps, core_ids=list(range(8)))
    return np.stack([np.asarray(r["out"], np.float32) for r in res.results], axis=0)
```
